# Optimizing a Trainium2 kernel written in Bass

```python
import numpy as np
import jax
import jax.numpy as jnp
from jax import lax

D_MODEL = 1024
BATCH = 32
SEQ = 256
DEPTH = 2
DEC_BATCH = 8
DEC_SEQ = 1024
PAST_LEN = 256

GRID_W = 64
N_HEADS = 4
HEAD_DIM = 64
MIX_W = N_HEADS * HEAD_DIM
N_BRANCH = 4
CHUNK = 64
GLA_RANK = 16
GLA_TAU = 16.0
RWKV_W_RANK = 32
RWKV_A_RANK = 32
RWKV_G_RANK = 64
SHIFT_TAPS = 3
NA_WIN_R = 8
NA_WIN_C = 16
NA_QCB = 16
NA_KCB = 32
N_EXPERTS = 16
EC_CAPACITY = 2
EXPERT_FF = 2048
ROPE_BASE = 10000.0
LN_EPS = 1e-5
DEEPNORM_ALPHA = (2 * DEPTH) ** 0.25
DEEPNORM_BETA = (8 * DEPTH) ** -0.25
F32 = jnp.float32

IN_COLUMNS = (
    ('m_q', MIX_W), ('m_k', MIX_W), ('m_v', MIX_W), ('m_o', MIX_W),
    ('m_i', 2 * N_HEADS), ('m_f', 2 * N_HEADS),
    ('g_q', MIX_W), ('g_k', MIX_W), ('g_v', MIX_W), ('g_g', MIX_W), ('g_a', 2 * GLA_RANK),
    ('r_rkv', 3 * MIX_W), ('r_w', 2 * RWKV_W_RANK), ('r_a', 2 * RWKV_A_RANK), ('r_g', RWKV_G_RANK),
    ('n_qkv', 3 * MIX_W),
    ('merge', N_BRANCH * D_MODEL),
)

kernel_name = 'hybrid_diffusion_prefix_trunk_step'


def _split_columns(proj):
    cols, off = {}, 0
    for name, width in IN_COLUMNS:
        cols[name] = proj[..., off:off + width]
        off += width
    return cols


def _heads(a):
    return a.reshape(a.shape[0], a.shape[1], N_HEADS, HEAD_DIM)


def _orient(a2, t_axis):
    return jnp.stack([a2[0], jnp.flip(a2[1], t_axis)])


def _both_dirs(a, t_axis):
    return jnp.stack([a, jnp.flip(a, t_axis)])


def _merge_dirs(a2, t_axis):
    return a2[0] + jnp.flip(a2[1], t_axis)


def _head_norm(x, centre):
    if centre:
        x = x - x.mean(-1, keepdims=True)
    return x * lax.rsqrt(jnp.mean(x * x, -1, keepdims=True) + LN_EPS)


def _layer_norm(x, g, b):
    x32 = x.astype(F32)
    mu = x32.mean(-1, keepdims=True)
    xc = x32 - mu
    var = jnp.mean(xc * xc, -1, keepdims=True)
    return (xc * lax.rsqrt(var + LN_EPS) * g + b).astype(x.dtype)


def _axial_rope(x):
    t = jnp.arange(x.shape[1])
    half = HEAD_DIM // 2
    nf = half // 2
    inv = ROPE_BASE ** (-jnp.arange(nf, dtype=F32) / nf)

    def rot(xs, pos):
        ang = pos.astype(F32)[:, None] * inv[None, :]
        cos = jnp.cos(ang)[None, :, None, :]
        sin = jnp.sin(ang)[None, :, None, :]
        x1, x2 = xs[..., :nf], xs[..., nf:]
        return jnp.concatenate([x1 * cos - x2 * sin, x1 * sin + x2 * cos], -1)

    return jnp.concatenate([rot(x[..., :half], t // GRID_W), rot(x[..., half:], t % GRID_W)], -1)


def _token_shift(x, taps):
    T = x.shape[1]
    xp = jnp.pad(x, ((0, 0), (1, 1), (0, 0)))
    return taps[0] * xp[:, :T] + taps[1] * xp[:, 1:T + 1] + taps[2] * xp[:, 2:T + 2]


def _mlstm_chunkwise(q, k, v, ig, lf, C0, n0, m0):
    B, H, T, d = q.shape
    nc = T // CHUNK
    causal = jnp.tril(jnp.ones((CHUNK, CHUNK), dtype=bool))

    def to_chunks(a):
        return jnp.moveaxis(a.reshape(B, H, nc, CHUNK, *a.shape[3:]), 2, 0)

    def step(carry, inp):
        C, n, m = carry
        qc, kc, vc, ic, fc = inp
        b = jnp.cumsum(fc, axis=-1)
        log_d = jnp.where(causal, b[..., :, None] - b[..., None, :] + ic[..., None, :], -jnp.inf)
        m_inter = b + m[..., None]
        m_t = jnp.maximum(m_inter, log_d.max(-1))
        s = jnp.einsum('bhtd,bhsd->bhts', qc, kc) * jnp.exp(log_d - m_t[..., None])
        w_inter = jnp.exp(m_inter - m_t)
        num = jnp.einsum('bhts,bhsd->bhtd', s, vc) + w_inter[..., None] * jnp.einsum('bhtk,bhkv->bhtv', qc, C)
        den = s.sum(-1) + w_inter * jnp.einsum('bhtk,bhk->bht', qc, n)
        h = num / jnp.maximum(jnp.abs(den), jnp.exp(-m_t))[..., None]
        b_end = b[..., -1]
        log_w = b_end[..., None] - b + ic
        m_new = jnp.maximum(b_end + m, log_w.max(-1))
        carry_w = jnp.exp(b_end + m - m_new)
        wk = jnp.exp(log_w - m_new[..., None])[..., None] * kc
        C_new = carry_w[..., None, None] * C + jnp.einsum('bhtk,bhtv->bhkv', wk, vc)
        n_new = carry_w[..., None] * n + wk.sum(2)
        return (C_new, n_new, m_new), h

    (C, n, m), hs = lax.scan(step, (C0, n0, m0), tuple(to_chunks(a) for a in (q, k, v, ig, lf)))
    return jnp.moveaxis(hs, 0, 2).reshape(B, H, T, d), C, n, m


def _gla_chunkwise(q, k, v, g, S0):
    B, H, T, dk = q.shape
    nc = T // CHUNK
    causal = jnp.tril(jnp.ones((CHUNK, CHUNK), dtype=bool))[:, :, None]

    def to_chunks(a):
        return jnp.moveaxis(a.reshape(B, H, nc, CHUNK, a.shape[-1]), 2, 0)

    def step(S, inp):
        qc, kc, vc, gc = inp
        G = jnp.cumsum(gc, axis=2)
        rel = jnp.exp(jnp.where(causal, G[:, :, :, None, :] - G[:, :, None, :, :], -jnp.inf))
        att = jnp.einsum('bhtk,bhsk,bhtsk->bhts', qc, kc, rel)
        o = jnp.einsum('bhts,bhsv->bhtv', att, vc) + jnp.einsum('bhtk,bhkv->bhtv', qc * jnp.exp(G), S)
        G_end = G[:, :, -1]
        S_new = jnp.exp(G_end)[..., None] * S + jnp.einsum('bhtk,bhtv->bhkv', kc * jnp.exp(G_end[:, :, None] - G), vc)
        return S_new, o

    S, outs = lax.scan(step, S0, tuple(to_chunks(a) for a in (q, k, v, g)))
    return jnp.moveaxis(outs, 0, 2).reshape(B, H, T, v.shape[-1]), S


def _rwkv7_scan(r, w, kap, a, khat, v, S0):
    def step(S, inp):
        rt, wt, kt, at, kht, vt = inp
        sk = jnp.einsum('bhvk,bhk->bhv', S, kt)
        S = S * wt[:, :, None, :] - sk[..., None] * (at * kt)[:, :, None, :] + vt[..., None] * kht[:, :, None, :]
        return S, jnp.einsum('bhvk,bhk->bhv', S, rt)

    S, ys = lax.scan(step, S0, tuple(jnp.moveaxis(t, 1, 0) for t in (r, w, kap, a, khat, v)))
    return jnp.moveaxis(ys, 0, 1), S


def _mlstm_branch(cols, b_ig, b_fg, C0, n0, m0, latent):
    B, T, _ = cols['m_q'].shape
    q, k, v = _heads(cols['m_q']), _heads(cols['m_k']), _heads(cols['m_v'])
    if latent:
        q, k = _axial_rope(q), _axial_rope(k)
    k = k * HEAD_DIM ** -0.5
    ig = jnp.transpose(cols['m_i'].reshape(B, T, 2, N_HEADS) + b_ig, (2, 0, 3, 1))
    lf = jnp.transpose(jax.nn.log_sigmoid(cols['m_f'].reshape(B, T, 2, N_HEADS) + b_fg), (2, 0, 3, 1))
    bhtd = lambda a: jnp.transpose(a, (0, 2, 1, 3))
    h2, C, n, m = jax.vmap(_mlstm_chunkwise, in_axes=(0, 0, 0, 0, 0, 1, 1, 1), out_axes=(0, 1, 1, 1))(
        _both_dirs(bhtd(q), 2), _both_dirs(bhtd(k), 2), _both_dirs(bhtd(v), 2),
        _orient(ig, 2), _orient(lf, 2), C0.astype(F32), n0.astype(F32), m0.astype(F32))
    h = _head_norm(bhtd(_merge_dirs(h2, 2)), True).reshape(B, T, MIX_W)
    return h * jax.nn.sigmoid(cols['m_o']), C, n, m


def _gla_branch(cols, w_gla_a2, b_gla_a, S0, latent):
    B, T, _ = cols['g_q'].shape
    q, k, v = _heads(cols['g_q']), _heads(cols['g_k']), _heads(cols['g_v'])
    if latent:
        q, k = _axial_rope(q), _axial_rope(k)
    q = q * HEAD_DIM ** -0.5
    g = jax.nn.log_sigmoid(jnp.einsum('btzr,zrc->btzc', cols['g_a'].reshape(B, T, 2, GLA_RANK), w_gla_a2) + b_gla_a) / GLA_TAU
    g = jnp.transpose(g.reshape(B, T, 2, N_HEADS, HEAD_DIM), (2, 0, 3, 1, 4))
    bhtd = lambda a: jnp.transpose(a, (0, 2, 1, 3))
    o2, S = jax.vmap(_gla_chunkwise, in_axes=(0, 0, 0, 0, 1), out_axes=(0, 1))(
        _both_dirs(bhtd(q), 2), _both_dirs(bhtd(k), 2), _both_dirs(bhtd(v), 2), _orient(g, 2), S0.astype(F32))
    o = _head_norm(bhtd(_merge_dirs(o2, 2)), False).reshape(B, T, MIX_W)
    return o * jax.nn.silu(cols['g_g']), S


def _rwkv7_branch(cols, shift, w0, w_w2, a0, w_a2, w_g2, k_k, k_a, r_k, S0):
    B, T, _ = cols['r_rkv'].shape
    rkv = _token_shift(cols['r_rkv'], shift)
    r, k, v = (_heads(t) for t in jnp.split(rkv, 3, axis=-1))
    lora_w = jnp.einsum('btzr,zrc->btzc', jnp.tanh(cols['r_w'].reshape(B, T, 2, RWKV_W_RANK)), w_w2)
    decay = jnp.exp(-jnp.exp(-jax.nn.softplus(-(w0 + lora_w)) - 0.5))
    a = jax.nn.sigmoid(a0 + jnp.einsum('btzr,zrc->btzc', cols['r_a'].reshape(B, T, 2, RWKV_A_RANK), w_a2))
    decay = decay.reshape(B, T, 2, N_HEADS, HEAD_DIM)
    a = a.reshape(B, T, 2, N_HEADS, HEAD_DIM)
    kap = k * k_k.reshape(N_HEADS, HEAD_DIM)
    kap = kap * lax.rsqrt(jnp.sum(kap * kap, -1, keepdims=True) + LN_EPS)
    khat = k[:, :, None] * (1.0 + (a - 1.0) * k_a.reshape(N_HEADS, HEAD_DIM))
    per_dir = lambda t: _orient(jnp.moveaxis(t, 2, 0), 1)
    y2, S = jax.vmap(_rwkv7_scan, in_axes=(0, 0, 0, 0, 0, 0, 1), out_axes=(0, 1))(
        _both_dirs(r, 1), per_dir(decay), _both_dirs(kap, 1), per_dir(a), per_dir(khat), _both_dirs(v, 1), S0.astype(F32))
    y = _head_norm(_merge_dirs(y2, 1), True)
    bonus = jnp.sum(r * k * r_k.reshape(N_HEADS, HEAD_DIM), -1, keepdims=True) * v
    gate = jnp.einsum('btr,rc->btc', jax.nn.sigmoid(cols['r_g']), w_g2)
    return (y + bonus).reshape(B, T, MIX_W) * gate, S


def _ctx_attention(q, k, v):
    s = jnp.einsum('bthd,bhsd->bhts', q, k) * HEAD_DIM ** -0.5
    p = jax.nn.softmax(s.astype(F32), axis=-1)
    return jnp.einsum('bhts,bhsd->bthd', p, v)


def _na_latent(q, k, v, k_ctx, v_ctx, rpb):
    B, T, H, d = q.shape
    rows = T // GRID_W
    kr = min(NA_WIN_R, rows)
    ncb = GRID_W // NA_QCB
    r_q = np.arange(rows)
    row_gather = np.clip(r_q - kr // 2, 0, rows - kr)[:, None] + np.arange(kr)
    c_start = np.clip(np.arange(GRID_W) - NA_WIN_C // 2, 0, GRID_W - NA_WIN_C)
    blk = np.arange(ncb)
    col_gather = np.clip(blk * NA_QCB - (NA_KCB - NA_QCB) // 2, 0, GRID_W - NA_KCB)[:, None] + np.arange(NA_KCB)
    q_cols = blk[:, None] * NA_QCB + np.arange(NA_QCB)
    cs = c_start[q_cols][:, :, None]
    col_ok = (col_gather[:, None, :] >= cs) & (col_gather[:, None, :] < cs + NA_WIN_C)
    d_row = row_gather - r_q[:, None] + NA_WIN_R - 1
    d_col = col_gather[:, None, :] - q_cols[:, :, None] + NA_WIN_C - 1
    bias = rpb[:, d_row[:, None, None, :, None], d_col[None, :, :, None, :]]
    kg = k.reshape(B, rows, GRID_W, H, d)[:, row_gather][:, :, :, col_gather]
    vg = v.reshape(B, rows, GRID_W, H, d)[:, row_gather][:, :, :, col_gather]
    qg = q.reshape(B, rows, ncb, NA_QCB, H, d)
    scale = d ** -0.5
    s_loc = jnp.einsum('brjuhd,brijwhd->bhrjuiw', qg, kg) * scale + bias[None]
    s_loc = jnp.where(col_ok[:, :, None, :], s_loc, -jnp.inf)
    s_ctx = jnp.einsum('brjuhd,bhcd->bhrjuc', qg, k_ctx) * scale
    n_loc = kr * NA_KCB
    s_all = jnp.concatenate([s_loc.reshape(*s_loc.shape[:5], n_loc), s_ctx], axis=-1).astype(F32)
    p = jax.nn.softmax(s_all, axis=-1)
    p_loc = p[..., :n_loc].reshape(s_loc.shape)
    p_ctx = p[..., n_loc:]
    o = jnp.einsum('bhrjuiw,brijwhd->brjuhd', p_loc, vg) + jnp.einsum('bhrjuc,bhcd->brjuhd', p_ctx, v_ctx)
    return o.reshape(B, T, H, d)


def _na_branch(cols, rpb, ctx_kv, latent):
    B, T, _ = cols['n_qkv'].shape
    q, k, v = (_heads(t) for t in jnp.split(cols['n_qkv'], 3, axis=-1))
    if latent:
        return _na_latent(q, k, v, ctx_kv[0], ctx_kv[1], rpb).reshape(B, T, MIX_W), None
    kt, vt = jnp.transpose(k, (0, 2, 1, 3)), jnp.transpose(v, (0, 2, 1, 3))
    return _ctx_attention(q, kt, vt).reshape(B, T, MIX_W), (kt, vt)


def _mixing_sublayer(h, lp, init, ctx_kv, latent):
    B, T, _ = h.shape
    cols = _split_columns(jnp.einsum('btd,dn->btn', h, lp['w_in']).astype(F32))
    C0, n0, m0, Sg0, Sr0 = init
    m_out, C, n, m = _mlstm_branch(cols, lp['b_ig'], lp['b_fg'], C0, n0, m0, latent)
    g_out, Sg = _gla_branch(cols, lp['w_gla_a2'], lp['b_gla_a'], Sg0, latent)
    r_out, Sr = _rwkv7_branch(cols, lp['shift_rwkv'], lp['w0_rwkv'], lp['w_w2'], lp['a0_rwkv'], lp['w_a2'],
                              lp['w_g2'], lp['k_k'], lp['k_a'], lp['r_k'], Sr0)
    n_out, kv = _na_branch(cols, lp['rpb'], ctx_kv, latent)
    branches = jnp.stack([m_out, g_out, r_out, n_out], axis=2).astype(h.dtype)
    widened = jnp.einsum('btzc,zcd->btzd', branches, lp['w_br'])
    gates = jax.nn.sigmoid(cols['merge'].reshape(B, T, N_BRANCH, D_MODEL)).astype(h.dtype)
    out = jnp.einsum('btd,de->bte', jnp.sum(gates * widened, axis=2), lp['w_out'])
    return out, (C, n, m, Sg, Sr), kv


def _expert_choice_moe(h, w_router, w_up, w_down):
    B, T, _ = h.shape
    cap = EC_CAPACITY * T // N_EXPERTS
    aff = jax.nn.softmax(jnp.einsum('btd,de->bte', h, w_router).astype(F32), axis=-1)
    gate, idx = lax.top_k(jnp.swapaxes(aff, 1, 2), cap)
    bidx = jnp.arange(B)[:, None, None]
    xe = h[bidx, idx]
    up = jnp.einsum('becd,edf->becf', xe, w_up)
    a, b = jnp.split(up, 2, axis=-1)
    ye = jnp.einsum('becf,efd->becd', jax.nn.silu(a) * b, w_down)
    return jnp.zeros_like(h).at[bidx, idx].add(gate[..., None].astype(h.dtype) * ye)


def _trunk_layer(x, mod, lp, init, ctx_kv, latent):
    sh1, sc1, g1, sh2, sc2, g2 = jnp.split(mod[:, None, :].astype(x.dtype), 6, axis=-1)
    h = x * (1 + sc1) + sh1
    mix, states, kv = _mixing_sublayer(h, lp, init, ctx_kv, latent)
    x = _layer_norm(DEEPNORM_ALPHA * x + g1 * mix, lp['ln_g'][0], lp['ln_b'][0])
    h = x * (1 + sc2) + sh2
    ff = _expert_choice_moe(h, lp['w_router'], lp['w_up'], lp['w_down'])
    x = _layer_norm(DEEPNORM_ALPHA * x + g2 * ff, lp['ln_g'][1], lp['ln_b'][1])
    return x, states, kv


def setup_inputs(seed: int = 0) -> dict:
    key = jax.random.key(seed)
    ks = iter(jax.random.split(key, 48))

    def nrm(shape, scale=1.0):
        return scale * jax.random.normal(next(ks), shape, F32)

    H, d = N_HEADS, HEAD_DIM
    n_in = sum(width for _, width in IN_COLUMNS)
    taps = jnp.array([0.25, 0.5, 0.25], F32)[None, :, None]
    return {
        'x_prompt': nrm((BATCH, SEQ, D_MODEL)),
        'x_sample': nrm((DEC_BATCH, DEC_SEQ, D_MODEL)),
        'state_mlstm_C': nrm((DEC_BATCH, DEPTH, 2, H, d, d), 0.1),
        'state_mlstm_n': nrm((DEC_BATCH, DEPTH, 2, H, d), 0.1),
        'state_mlstm_m': nrm((DEC_BATCH, DEPTH, 2, H)),
        'state_gla': nrm((DEC_BATCH, DEPTH, 2, H, d, d), 0.1),
        'state_rwkv': nrm((DEC_BATCH, DEPTH, 2, H, d, d), 0.1),
        'cache_na_k': nrm((DEC_BATCH, DEPTH, H, PAST_LEN, d)),
        'cache_na_v': nrm((DEC_BATCH, DEPTH, H, PAST_LEN, d)),
        'c': nrm((DEC_BATCH, D_MODEL)),
        'c_ctx': nrm((D_MODEL,)),
        'w_ada': nrm((DEPTH, D_MODEL, 6 * D_MODEL), 0.5 * D_MODEL ** -0.5),
        'b_ada': nrm((DEPTH, 6 * D_MODEL), 0.02),
        'w_in': nrm((DEPTH, D_MODEL, n_in), D_MODEL ** -0.5),
        'b_ig': nrm((DEPTH, 2, H), 0.1),
        'b_fg': 3.0 + 3.0 * jax.random.uniform(next(ks), (DEPTH, 2, H), F32),
        'w_gla_a2': nrm((DEPTH, 2, GLA_RANK, MIX_W), GLA_RANK ** -0.5),
        'b_gla_a': nrm((DEPTH, 2, MIX_W), 0.1),
        'shift_rwkv': taps + nrm((DEPTH, SHIFT_TAPS, 3 * MIX_W), 0.05),
        'w0_rwkv': nrm((DEPTH, 2, MIX_W), 0.5),
        'w_w2': nrm((DEPTH, 2, RWKV_W_RANK, MIX_W), 0.1),
        'a0_rwkv': nrm((DEPTH, 2, MIX_W), 0.5),
        'w_a2': nrm((DEPTH, 2, RWKV_A_RANK, MIX_W), RWKV_A_RANK ** -0.5),
        'w_g2': nrm((DEPTH, RWKV_G_RANK, MIX_W), RWKV_G_RANK ** -0.5),
        'k_k': 0.85 + nrm((DEPTH, MIX_W), 0.05),
        'k_a': 1.0 + nrm((DEPTH, MIX_W), 0.05),
        'r_k': nrm((DEPTH, MIX_W), 0.1),
        'rpb': nrm((DEPTH, H, 2 * NA_WIN_R - 1, 2 * NA_WIN_C - 1), 0.1),
        'w_br': nrm((DEPTH, N_BRANCH, MIX_W, D_MODEL), MIX_W ** -0.5),
        'w_out': nrm((DEPTH, D_MODEL, D_MODEL), DEEPNORM_BETA * D_MODEL ** -0.5),
        'ln_g': 1.0 + nrm((DEPTH, 2, D_MODEL), 0.05),
        'ln_b': nrm((DEPTH, 2, D_MODEL), 0.02),
        'w_router': nrm((DEPTH, D_MODEL, N_EXPERTS), D_MODEL ** -0.5),
        'w_up': nrm((DEPTH, N_EXPERTS, D_MODEL, 2 * EXPERT_FF), D_MODEL ** -0.5),
        'w_down': nrm((DEPTH, N_EXPERTS, EXPERT_FF, D_MODEL), DEEPNORM_BETA * EXPERT_FF ** -0.5),
    }


def reference(x_prompt, x_sample, state_mlstm_C, state_mlstm_n, state_mlstm_m, state_gla, state_rwkv,
              cache_na_k, cache_na_v, c, c_ctx, w_ada, b_ada, w_in, b_ig, b_fg, w_gla_a2, b_gla_a,
              shift_rwkv, w0_rwkv, w_w2, a0_rwkv, w_a2, w_g2, k_k, k_a, r_k, rpb, w_br, w_out,
              ln_g, ln_b, w_router, w_up, w_down):
    def layer_params(l):
        return {'w_in': w_in[l], 'b_ig': b_ig[l], 'b_fg': b_fg[l], 'w_gla_a2': w_gla_a2[l],
                'b_gla_a': b_gla_a[l], 'shift_rwkv': shift_rwkv[l], 'w0_rwkv': w0_rwkv[l], 'w_w2': w_w2[l],
                'a0_rwkv': a0_rwkv[l], 'w_a2': w_a2[l], 'w_g2': w_g2[l], 'k_k': k_k[l], 'k_a': k_a[l],
                'r_k': r_k[l], 'rpb': rpb[l], 'w_br': w_br[l], 'w_out': w_out[l], 'ln_g': ln_g[l],
                'ln_b': ln_b[l], 'w_router': w_router[l], 'w_up': w_up[l], 'w_down': w_down[l]}

    bp = x_prompt.shape[0]
    zero_state = (jnp.zeros((bp, 2, N_HEADS, HEAD_DIM, HEAD_DIM), F32),
                  jnp.zeros((bp, 2, N_HEADS, HEAD_DIM), F32),
                  jnp.zeros((bp, 2, N_HEADS), F32),
                  jnp.zeros((bp, 2, N_HEADS, HEAD_DIM, HEAD_DIM), F32),
                  jnp.zeros((bp, 2, N_HEADS, HEAD_DIM, HEAD_DIM), F32))
    cond_ctx = jax.nn.silu(c_ctx)[None, :]
    y_prompt = x_prompt
    l_C, l_n, l_m, l_g, l_r, l_k, l_v = [], [], [], [], [], [], []
    for l in range(DEPTH):
        mod = cond_ctx @ w_ada[l] + b_ada[l]
        y_prompt, (C, n, m, Sg, Sr), (kc, vc) = _trunk_layer(y_prompt, mod, layer_params(l), zero_state, None, False)
        l_C.append(C)
        l_n.append(n)
        l_m.append(m)
        l_g.append(Sg)
        l_r.append(Sr)
        l_k.append(kc)
        l_v.append(vc)
    new_mlstm_C = jnp.stack(l_C, axis=1)
    new_mlstm_n = jnp.stack(l_n, axis=1)
    new_mlstm_m = jnp.stack(l_m, axis=1)
    new_gla = jnp.stack(l_g, axis=1)
    new_rwkv = jnp.stack(l_r, axis=1)
    new_na_k = jnp.stack(l_k, axis=1)
    new_na_v = jnp.stack(l_v, axis=1)

    cond = jax.nn.silu(c)
    y_sample = x_sample
    for l in range(DEPTH):
        mod = cond @ w_ada[l] + b_ada[l]
        init = (state_mlstm_C[:, l], state_mlstm_n[:, l], state_mlstm_m[:, l], state_gla[:, l], state_rwkv[:, l])
        y_sample, _, _ = _trunk_layer(y_sample, mod, layer_params(l), init,
                                      (cache_na_k[:, l], cache_na_v[:, l]), True)

    return (y_prompt, y_sample, new_mlstm_C, new_mlstm_n, new_mlstm_m, new_gla, new_rwkv, new_na_k, new_na_v)
```

```python
import math
from contextlib import ExitStack, contextmanager
import numpy as np
import concourse.bass as bass
import concourse.mybir as mybir
from concourse.bass_utils import run_bass_kernel_spmd

F32 = mybir.dt.float32
BF16 = mybir.dt.bfloat16
AF = mybir.ActivationFunctionType
ALU = mybir.AluOpType
AX = mybir.AxisListType

ENGS = ("pe", "act", "dve", "pool", "sp")
N_DMA_SLOTS = 12

NCORES = 8
DEPTH = 2
D = 1024
NT = 16
NTOK = 2048
NIN = 7920
H = 4
HD = 64
MIXW = 256
NEXP = 16
FF = 2048
LN_EPS = 1e-5
ALPHA = (2 * DEPTH) ** 0.25
NEG = -30000.0

COLS = {}
_off = 0
for _n, _w in (('m_q', 256), ('m_k', 256), ('m_v', 256), ('m_o', 256), ('m_i', 8), ('m_f', 8),
               ('g_q', 256), ('g_k', 256), ('g_v', 256), ('g_g', 256), ('g_a', 32),
               ('r_rkv', 768), ('r_w', 64), ('r_a', 64), ('r_g', 64), ('n_qkv', 768), ('merge', 4096)):
    COLS[_n] = _off
    _off += _w
assert _off == NIN


class Buf:
    __slots__ = ("name", "last_w", "readers")

    def __init__(self, name=""):
        self.name = name
        self.last_w = None
        self.readers = []


class Sched:
    def __init__(self, nc):
        self.nc = nc
        self.q = {e: [] for e in ENGS}
        self.cnt = {e: 0 for e in ENGS}
        self.sems = {}
        self.seen = {e: {} for e in ENGS}
        self.pending = {e: [] for e in ENGS}
        self.dma_slot = {e: 0 for e in ENGS}
        self.dma_cnt = {}
        for e in ENGS:
            self.sems[("c", e)] = nc.alloc_semaphore(name=f"c_{e}")
        for e in ("sp", "act", "pool"):
            for s in range(N_DMA_SLOTS):
                self.sems[("d", e, s)] = nc.alloc_semaphore(name=f"d_{e}_{s}")
                self.dma_cnt[(e, s)] = 0
        self.n_ops = 0

    def _collect(self, eng, reads, writes, extra):
        need = {}

        def add(tok):
            if tok is None:
                return
            sid, val = tok
            if need.get(sid, 0) < val:
                need[sid] = val
        for b in reads:
            add(b.last_w)
        for b in writes:
            add(b.last_w)
            for t in b.readers:
                add(t)
        for t in extra:
            add(t)
        for t in self.pending[eng]:
            add(t)
        self.pending[eng] = []
        waits = []
        seen = self.seen[eng]
        for sid, val in need.items():
            if sid == ("c", "pe") and eng == "pe":
                continue
            if seen.get(sid, 0) >= val:
                continue
            seen[sid] = val
            waits.append((sid, val))
        return waits

    def _commit(self, tok, reads, writes):
        for b in reads:
            b.readers.append(tok)
            if len(b.readers) > 64:
                mx = {}
                for sid, val in b.readers:
                    if mx.get(sid, 0) < val:
                        mx[sid] = val
                b.readers = list(mx.items())
        for b in writes:
            b.last_w = tok
            b.readers = []

    def op(self, eng, fn, reads=(), writes=()):
        waits = self._collect(eng, reads, writes, ())
        self.cnt[eng] += 1
        tok = (("c", eng), self.cnt[eng])
        self.q[eng].append((waits, fn, ("c", eng), 1))
        self._commit(tok, reads, writes)
        self.n_ops += 1
        return tok

    def dma(self, eng, out_ap, in_ap, reads=(), writes=(), **kw):
        slot = self.dma_slot[eng]
        self.dma_slot[eng] = (slot + 1) % N_DMA_SLOTS
        sid = ("d", eng, slot)
        prev = self.dma_cnt[(eng, slot)]
        extra = [(sid, 16 * prev)] if prev > 0 else []
        waits = self._collect(eng, reads, writes, extra)
        self.dma_cnt[(eng, slot)] = prev + 1
        tok = (sid, 16 * (prev + 1))

        def fn(e, out_ap=out_ap, in_ap=in_ap, kw=kw):
            return e.dma_start(out=out_ap, in_=in_ap, **kw)
        self.q[eng].append((waits, fn, sid, 16))
        self._commit(tok, reads, writes)
        self.n_ops += 1
        return tok

    def all_tokens(self):
        toks = []
        for e in ENGS:
            if self.cnt[e] > 0:
                toks.append((("c", e), self.cnt[e]))
        for (e, s), c in self.dma_cnt.items():
            if c > 0:
                toks.append((("d", e, s), 16 * c))
        return toks

    def barrier(self):
        toks = self.all_tokens()
        for e in ENGS:
            self.pending[e] = list(toks)

    def emit(self):
        nc = self.nc
        final_waits = self.all_tokens()
        engmap = {"pe": "tensor", "act": "scalar", "dve": "vector", "pool": "gpsimd", "sp": "sync"}
        with nc.Block() as block:
            for e in ENGS:
                items = self.q[e]
                fw = final_waits if e == "sp" else []
                sems = self.sems

                def body(engobj, items=items, fw=fw, sems=sems):
                    for waits, fn, sid, inc in items:
                        for (wsid, val) in waits:
                            engobj.wait_ge(sems[wsid], val)
                        ins = fn(engobj)
                        ins.then_inc(sems[sid], inc)
                    for (wsid, val) in fw:
                        engobj.wait_ge(sems[wsid], val)
                getattr(block, engmap[e])(body)


class T:
    __slots__ = ("h", "b", "psum")

    def __init__(self, h, name="", psum=False):
        self.h = h
        self.b = Buf(name)
        self.psum = psum


    def __getitem__(self, k):
        return self.h[k]


def _rw(r, w):
    rb = [t.b for t in r if not t.psum]
    wb = [t.b for t in w] + [t.b for t in r if t.psum]
    return rb, wb


class KB:
    def __init__(self, nc):
        self.nc = nc
        self.S = Sched(nc)
        self.stacks = []
        self.rots = []
        self.uid = 0
        self.ps = [T(nc.alloc_psum_tensor(f"psb{i}", [128, 512], F32), f"ps{i}", psum=True) for i in range(8)]
        self.ps_i = 0
        self.ev_i = 0
        self.dq_i = 0

    @contextmanager
    def scope(self):
        st = ExitStack()
        self.stacks.append(st)
        self.rots.append({})
        try:
            yield
        finally:
            self.S.barrier()
            self.stacks.pop()
            self.rots.pop()
            st.close()

    def sb(self, shape, dtype=F32, name=None):
        self.uid += 1
        nm = f"{name or 't'}_{self.uid}"
        if self.stacks:
            h = self.stacks[-1].enter_context(self.nc.sbuf_tensor(nm, list(shape), dtype))
        else:
            h = self.nc.alloc_sbuf_tensor(nm, list(shape), dtype)
        return T(h, nm)

    def rot(self, key, shape, dtype=F32, n=2):
        d = self.rots[-1]
        if key not in d:
            d[key] = [[self.sb(shape, dtype, key) for _ in range(n)], 0]
        lst, i = d[key]
        d[key][1] = (i + 1) % n
        return lst[i]

    def psum(self, hold=False, nbanks=6):
        held = getattr(self, "held", None)
        if held is None:
            held = self.held = set()
        for _ in range(8):
            i = self.ps_i
            self.ps_i = (self.ps_i + 1) % nbanks
            if i not in held:
                if hold:
                    held.add(i)
                return self.ps[i]
        raise RuntimeError("no free PSUM bank")

    def psfree(self, t):
        self.held.discard(self.ps.index(t))

    @staticmethod
    def interleave(gens):
        gens = list(gens)
        while gens:
            for g in list(gens):
                try:
                    next(g)
                except StopIteration:
                    gens.remove(g)

    def psacc(self):
        self.acc_i = 1 - getattr(self, "acc_i", 0)
        return self.ps[6 + self.acc_i]

    def evq(self):
        self.ev_i += 1
        return "act" if self.ev_i % 2 else "dve"

    def dq(self):
        self.dq_i += 1
        return "sp" if self.dq_i % 2 else "act"

    def mm(self, out, lhsT, rhs, start, stop, r, w):
        self.S.op("pe", lambda e: e.matmul(out, lhsT=lhsT, rhs=rhs, start=start, stop=stop),
                  *_rw(r, w))

    def tr(self, out, in_, ident, r, w):
        self.S.op("pe", lambda e: e.transpose(out, in_, ident), *_rw(r, w))

    def act(self, out, in_, func, r, w, bias=None, scale=None, accum=None):
        kw = {}
        if bias is not None:
            kw["bias"] = bias
        if scale is not None:
            kw["scale"] = scale
        if accum is not None:
            kw["accum_out"] = accum
        self.S.op("act", lambda e: e.activation(out, in_, func, **kw), *_rw(r, w))

    def tt(self, eng, out, a, b, op, r, w):
        self.S.op(eng, lambda e: e.tensor_tensor(out, a, b, op), *_rw(r, w))

    def ts(self, eng, out, a, s1, s2, op0, op1, r, w):
        if op1 is None:
            self.S.op(eng, lambda e: e.tensor_scalar(out, a, s1, None, op0), *_rw(r, w))
        else:
            self.S.op(eng, lambda e: e.tensor_scalar(out, a, s1, s2, op0, op1), *_rw(r, w))

    def stt(self, eng, out, in0, scalar, in1, op0, op1, r, w):
        self.S.op(eng, lambda e: e.scalar_tensor_tensor(out, in0, scalar, in1, op0, op1),
                  *_rw(r, w))

    def cp(self, eng, out, in_, r, w):
        if eng == "act":
            self.S.op("act", lambda e: e.copy(out, in_), *_rw(r, w))
        else:
            self.S.op(eng, lambda e: e.tensor_copy(out, in_), *_rw(r, w))

    def red(self, out, in_, op, r, w, axis=AX.X):
        self.S.op("dve", lambda e: e.tensor_reduce(out, in_, axis, op), *_rw(r, w))

    def memset(self, eng, ap, val, w):
        self.S.op(eng, lambda e: e.memset(ap, val), [], [t.b for t in w])

    def dma(self, eng, out, in_, r, w, **kw):
        self.S.dma(eng, out, in_, [t.b for t in r], [t.b for t in w], **kw)


def bc(ap, shape):
    return ap.to_broadcast(list(shape))


class Prog:
    def __init__(self, debug=None):
        self.debug = debug
        nc = bass.Bass("TRN2", target_bir_lowering=False)
        self.nc = nc
        self.K = KB(nc)

        def inp(name, shape):
            return T(nc.dram_tensor(name, list(shape), F32, kind="ExternalInput"), name)

        def outp(name, shape):
            return T(nc.dram_tensor(name, list(shape), F32, kind="ExternalOutput"), name)

        def scr(name, shape, dtype=F32):
            kind = "ExternalOutput" if (debug and name in debug) else "Internal"
            return T(nc.dram_tensor(name, list(shape), dtype, kind=kind), name)
        self.I = {}
        for name, shape in IN_SHAPES.items():
            self.I[name] = inp(name, shape)
        self.O = {}
        for name, shape in OUT_SHAPES.items():
            self.O[name] = outp(name, shape)
        self.modrow = scr("modrow", [DEPTH, 2, 6 * D])
        if debug and "projin" in debug:
            self.proj = inp("proj", [NTOK, NIN])
        else:
            self.proj = scr("proj", [NTOK, NIN])
        self.brdbg = scr("brdbg", [4, NTOK, MIXW]) if (debug and "brdbg" in debug) else None
        self.xres = [scr("xres0", [NTOK, D]), scr("xres1", [NTOK, D])]
        self.x1res = scr("x1res", [NTOK, D])
        self.rpbpad = scr("rpbpad", [DEPTH, 60, 128])
        self.ye = scr("ye", [2, NEXP, 128, D], BF16)
        self.h2res = scr("h2res", [NTOK, D], BF16)
        self.consts()

    def consts(self):
        K = self.K
        io = K.sb([128, 128], F32, "io")
        K.S.op("pool", lambda e: e.iota(io[:], [[1, 128]], base=0, channel_multiplier=-1,
                                         allow_small_or_imprecise_dtypes=True), [], [io.b])
        self.ident = K.sb([128, 128], F32, "ident")
        self.U = K.sb([128, 128], F32, "U")
        self.Lo = K.sb([128, 128], F32, "Lo")
        self.Us = K.sb([128, 128], F32, "Us")
        self.Ls = K.sb([128, 128], F32, "Ls")
        self.ones = K.sb([128, 128], F32, "ones")
        for t, op in ((self.ident, ALU.is_equal), (self.U, ALU.is_ge), (self.Lo, ALU.is_le),
                      (self.Us, ALU.is_gt), (self.Ls, ALU.is_lt)):
            K.S.op("dve", lambda e, t=t, op=op: e.tensor_single_scalar(t[:], io[:], 0.0, op), [io.b], [t.b])
        K.memset("dve", self.ones[:], 1.0, [self.ones])
        self.pidx = K.sb([128, 1], F32, "pidx")
        K.S.op("pool", lambda e: e.iota(self.pidx[:], [[0, 1]], base=0, channel_multiplier=1,
                                         allow_small_or_imprecise_dtypes=True), [], [self.pidx.b])
        self.fidx = K.sb([128, 128], F32, "fidx")
        K.S.op("pool", lambda e: e.iota(self.fidx[:], [[1, 128]], base=0, channel_multiplier=0,
                                         allow_small_or_imprecise_dtypes=True), [], [self.fidx.b])
        self.sel8 = K.sb([8, 8, 128], F32, "sel8")
        K.cp("dve", self.sel8[:], bc(self.ident[0:8, 0:8].unsqueeze(2), [8, 8, 128]), [self.ident], [self.sel8])
        self.modT = K.sb([128, DEPTH, 2, 48], F32, "modT")
        self.rope_tables()

    def transpose_block(self, dst_ap, dst_t, src_ap, src_t, pin, fin, eng=None):
        K = self.K
        ps = K.psum()
        K.tr(ps[0:fin, 0:pin], src_ap, self.ident[0:pin, 0:pin], [src_t, self.ident], [ps])
        K.cp(eng or K.evq(), dst_ap, ps[0:fin, 0:pin], [ps], [dst_t])

    def phase0_mods(self):
        K = self.K
        I = self.I
        with K.scope():
            cv = K.sb([16, 128], F32, "cv")
            for wch in range(2):
                K.dma("sp", cv[wch:16:2, :], I["cvec"][wch].rearrange("(k p) -> k p", p=128), [I["cvec"]], [cv])
            sg = K.sb([16, 128], F32, "sg")
            K.act(sg[:], cv[:], AF.Sigmoid, [cv], [sg])
            K.tt("dve", cv[:], cv[:], sg[:], ALU.mult, [cv, sg], [cv])
            condT = K.sb([128, 16], F32, "condT")
            self.transpose_block(condT[:], condT, cv[:], cv, 16, 128)
            mrow = K.sb([2, 6 * D], F32, "mrow")
            bada = K.sb([2, 6 * D], F32, "bada")
            for l in range(DEPTH):
                K.dma("sp", bada[:], bc(I["b_ada"][l:l + 1, :], [2, 6 * D]), [I["b_ada"]], [bada])
                for cb in range(12):
                    wt = K.rot("wada", [128, 8, 512], F32, 2)
                    K.dma(K.dq(), wt[:], I["w_ada"][l, :, cb * 512:(cb + 1) * 512].rearrange("(k p) f -> p k f", p=128),
                          [I["w_ada"]], [wt])
                    ps = K.psum()
                    for k in range(8):
                        K.mm(ps[0:2, :], condT[:, 2 * k:2 * k + 2], wt[:, k, :], k == 0, k == 7, [condT, wt], [ps])
                    K.tt("dve", mrow[:, cb * 512:(cb + 1) * 512], ps[0:2, :], bada[:, cb * 512:(cb + 1) * 512],
                         ALU.add, [ps, bada], [mrow])
                K.dma("sp", self.modrow[l], mrow[:], [mrow], [self.modrow])
                ps = K.psum()
                for c in range(48):
                    K.tr(ps[:, 2 * c:2 * c + 2], mrow[0:2, c * 128:(c + 1) * 128], self.ident[0:2, 0:2],
                         [mrow, self.ident], [ps])
                K.cp("dve", self.modT[:, l, :, :], ps[:, 0:96].rearrange("p (c w) -> p w c", w=2), [ps], [self.modT])

    def phaseA(self, l, xsrc):
        K = self.K
        with K.scope():
            sc1p = K.sb([128, 2, 8], F32, "sc1p")
            K.ts("dve", sc1p[:], self.modT[:, l, :, 8:16], 1.0, None, ALU.add, None, [self.modT], [sc1p])
            for i in range(NT):
                wch = 0 if i < 8 else 1
                xt = K.rot("xt", [128, D], F32, 3)
                K.dma(K.dq(), xt[:], xsrc[i * 128:(i + 1) * 128, :], [xsrc], [xt])
                for kg in range(2):
                    ps = K.psum()
                    for kk in range(4):
                        k = kg * 4 + kk
                        K.tr(ps[:, kk * 128:(kk + 1) * 128], xt[:, k * 128:(k + 1) * 128], self.ident[:],
                             [xt, self.ident], [ps])
                    for kk in range(4):
                        k = kg * 4 + kk
                        if kk % 2 == 0:
                            K.act(self.hT[:, k, i * 128:(i + 1) * 128], ps[:, kk * 128:(kk + 1) * 128], AF.Identity,
                                  [ps, sc1p, self.modT], [self.hT],
                                  bias=self.modT[:, l, wch, k:k + 1], scale=sc1p[:, wch, k:k + 1])
                        else:
                            K.ts("dve", self.hT[:, k, i * 128:(i + 1) * 128], ps[:, kk * 128:(kk + 1) * 128],
                                 sc1p[:, wch, k:k + 1], self.modT[:, l, wch, k:k + 1], ALU.mult, ALU.add,
                                 [ps, sc1p, self.modT], [self.hT])

    def phaseB(self, l):
        K = self.K
        I = self.I
        with K.scope():
            ncb = (NIN + 511) // 512
            for cb in range(ncb):
                c0 = cb * 512
                cw = min(512, NIN - c0)
                wt = K.rot("win", [128, 8, 512], BF16, 3)
                K.dma("pool", wt[:, :, 0:cw], I["w_in"][l, :, c0:c0 + cw].rearrange("(k p) f -> p k f", p=128),
                      [I["w_in"]], [wt])
                for i in range(NT):
                    ps = K.psum()
                    for k in range(8):
                        K.mm(ps[:, 0:cw], self.hT[:, k, i * 128:(i + 1) * 128], wt[:, k, 0:cw], k == 0, k == 7,
                             [self.hT, wt], [ps])
                    st = K.rot("pst", [128, 512], F32, 4)
                    K.cp(K.evq(), st[:, 0:cw], ps[:, 0:cw], [ps], [st])
                    K.dma(K.dq(), self.proj[i * 128:(i + 1) * 128, c0:c0 + cw], st[:, 0:cw], [st], [self.proj])

    def seqs(self):
        return [(s * 256, 256, False, s) for s in range(4)] + [(1024, 1024, True, None)]

    def rope_tables(self):
        K = self.K
        self.cosF = K.sb([128, 8, 256], F32, "cosF")
        self.sinF = K.sb([128, 8, 256], F32, "sinF")
        with K.scope():
            invf = K.sb([128, 16], F32, "invf")
            K.act(invf[:], self.fidx[:, 0:16], AF.Exp, [self.fidx], [invf], scale=-math.log(10000.0) / 16.0)
            ge64 = K.sb([128, 1], F32, "ge64")
            K.ts("dve", ge64[:], self.pidx[:], 64.0, None, ALU.is_ge, None, [self.pidx], [ge64])
            pcol = K.sb([128, 1], F32, "pcol")
            K.stt("dve", pcol[:], ge64[:], -64.0, self.pidx[:], ALU.mult, ALU.add, [ge64, self.pidx], [pcol])
            rown = K.sb([128, 8], F32, "rown")
            K.S.op("pool", lambda e: e.iota(rown[:], [[2, 8]], base=0, channel_multiplier=0,
                                             allow_small_or_imprecise_dtypes=True), [], [rown.b])
            K.ts("dve", rown[:], rown[:], ge64[:, 0:1], None, ALU.add, None, [rown, ge64], [rown])
            ang = K.sb([128, 8, 2, 16], F32, "ang")
            for n in range(8):
                K.ts("dve", ang[:, n, 0, :], invf[:], rown[:, n:n + 1], None, ALU.mult, None, [invf, rown], [ang])
                K.ts("dve", ang[:, n, 1, :], invf[:], pcol[:, 0:1], None, ALU.mult, None, [invf, pcol], [ang])
            us = K.sb([128, 8, 2, 16], F32, "us")
            uc = K.sb([128, 8, 2, 16], F32, "uc")
            K.cp("dve", us[:], ang[:], [ang], [us])
            K.ts("dve", uc[:], ang[:], 0.5 * math.pi, None, ALU.add, None, [ang], [uc])
            ki = K.sb([128, 8, 2, 16], mybir.dt.int32, "ki")
            kf = K.sb([128, 8, 2, 16], F32, "kf")
            for u in (us, uc):
                K.ts("dve", kf[:], u[:], 1.0 / (2 * math.pi), None, ALU.mult, None, [u], [kf])
                K.cp("dve", ki[:], kf[:], [kf], [ki])
                K.cp("dve", kf[:], ki[:], [ki], [kf])
                K.stt("dve", u[:], kf[:], -2 * math.pi, u[:], ALU.mult, ALU.add, [kf, u], [u])
                K.ts("dve", kf[:], u[:], math.pi, -2 * math.pi, ALU.is_gt, ALU.mult, [u], [kf])
                K.tt("dve", u[:], u[:], kf[:], ALU.add, [u, kf], [u])
                K.ts("dve", kf[:], u[:], -math.pi, 2 * math.pi, ALU.is_lt, ALU.mult, [u], [kf])
                K.tt("dve", u[:], u[:], kf[:], ALU.add, [u, kf], [u])
                K.ts("dve", u[:], u[:], math.pi, -math.pi, ALU.min, ALU.max, [u], [u])
                K.act(u[:], u[:], AF.Sin, [u], [u])
            cF = self.cosF[:].rearrange("p n (h a b f) -> p n h a b f", h=4, a=2, b=2)
            sF = self.sinF[:].rearrange("p n (h a b f) -> p n h a b f", h=4, a=2, b=2)
            for h in range(4):
                for b in range(2):
                    K.cp("dve", cF[:, :, h, :, b, :], uc[:], [uc], [self.cosF])
                    K.ts("dve", sF[:, :, h, :, b, :], us[:], (-1.0 if b == 0 else 1.0), None, ALU.mult, None,
                         [us], [self.sinF])

    def rope(self, X, nt):
        K = self.K
        tmp = K.rot("ropeA", [128, nt, 256], F32, 1)
        t2 = K.rot("ropeB", [128, nt, 256], F32, 1)
        K.tt("dve", tmp[:], X[:], self.cosF[:, 0:nt, :], ALU.mult, [X, self.cosF], [tmp])
        X5 = X[:].rearrange("p n (g b f) -> p (n g) b f", b=2, f=16)
        S5 = self.sinF[:, 0:nt, :].rearrange("p n (g b f) -> p (n g) b f", b=2, f=16)
        T5 = t2[:].rearrange("p n (g b f) -> p (n g) b f", b=2, f=16)
        K.tt("pool", T5[:, :, 0, :], X5[:, :, 1, :], S5[:, :, 0, :], ALU.mult, [X, self.sinF], [t2])
        K.tt("pool", T5[:, :, 1, :], X5[:, :, 0, :], S5[:, :, 1, :], ALU.mult, [X, self.sinF], [t2])
        K.tt("dve", X[:], tmp[:], t2[:], ALU.add, [tmp, t2], [X])

    def load_tok(self, dst, tok0, nt, col0, width, eng=None):
        K = self.K
        K.dma(eng or K.dq(), dst[:, 0:nt, 0:width],
              self.proj[tok0:tok0 + nt * 128, col0:col0 + width].rearrange("(n p) c -> p n c", p=128),
              [self.proj], [dst])

    def tok2feat(self, dstT, src, nt, col0=0):
        K = self.K
        for n in range(nt):
            ps = K.psum()
            for h in range(4):
                K.tr(ps[0:64, h * 128:(h + 1) * 128], src[:, n, col0 + h * 64:col0 + (h + 1) * 64], self.ident[:],
                     [src, self.ident], [ps])
            K.cp(K.evq(), dstT[:, :, n * 128:(n + 1) * 128], ps[0:64, :].rearrange("p (h t) -> p h t", h=4),
                 [ps], [dstT])

    def head_norm(self, X, nt, centre):
        K = self.K
        G = nt * 4
        X3 = X[:].rearrange("p n (h d) -> p (n h) d", h=4)
        st = K.rot("hn_st", [128, G], F32, 2)
        if centre:
            K.red(st[:], X3, ALU.add, [X], [st])
            K.ts("dve", st[:], st[:], -1.0 / 64.0, None, ALU.mult, None, [st], [st])
            K.tt("dve", X3, X3, bc(st[:].unsqueeze(2), [128, G, 64]), ALU.add, [X, st], [X])
        sq = K.rot("ropeA", [128, nt, 256], F32, 1)
        K.tt("pool", sq[:], X[:], X[:], ALU.mult, [X], [sq])
        ms = K.rot("hn_ms", [128, G], F32, 2)
        K.red(ms[:], sq[:].rearrange("p n (h d) -> p (n h) d", h=4), ALU.add, [sq], [ms])
        K.ts("dve", ms[:], ms[:], 1.0 / 64.0, LN_EPS, ALU.mult, ALU.add, [ms], [ms])
        K.act(ms[:], ms[:], AF.Sqrt, [ms], [ms])
        K.S.op("dve", lambda e, ms=ms: e.reciprocal(ms[:], ms[:]), [ms.b], [ms.b])
        K.tt("dve", X3, X3, bc(ms[:].unsqueeze(2), [128, G, 64]), ALU.mult, [X, ms], [X])

    def store_branch(self, z, tok0, nt, src):
        K = self.K
        for n in range(nt):
            ps = K.psum()
            for kk in range(2):
                K.tr(ps[:, kk * 128:(kk + 1) * 128], src[:, n, kk * 128:(kk + 1) * 128], self.ident[:],
                     [src, self.ident], [ps])
            t0 = tok0 + n * 128
            K.cp(K.evq(), self.brT[:, 2 * z:2 * z + 2, t0:t0 + 128], ps[:, 0:256].rearrange("p (k t) -> p k t", k=2),
                 [ps], [self.brT])
        if self.brdbg is not None:
            K.dma("sp", self.brdbg[z, tok0:tok0 + nt * 128, :].rearrange("(n p) c -> p n c", p=128), src[:, 0:nt, :],
                  [src], [self.brdbg])

    def mlstm(self, l, tok0, T_, latent, pb):
        K = self.K
        I = self.I
        nt = T_ // 128
        LN8 = math.log(0.125)
        with K.scope():
            q = K.sb([128, nt, 256], F32, "mq")
            k = K.sb([128, nt, 256], F32, "mk")
            og = K.sb([128, nt, 256], F32, "mo")
            vaug = K.sb([128, nt, 4, 65], F32, "mv")
            ifg = K.sb([128, nt, 16], F32, "mifg")
            self.load_tok(q, tok0, nt, COLS['m_q'], 256)
            self.load_tok(k, tok0, nt, COLS['m_k'], 256)
            self.load_tok(og, tok0, nt, COLS['m_o'], 256)
            self.load_tok(ifg, tok0, nt, COLS['m_i'], 16)
            K.memset("pool", vaug[:], 1.0, [vaug])
            for n in range(nt):
                K.dma(K.dq(), vaug[:, n, :, 0:64],
                      self.proj[tok0 + n * 128:tok0 + (n + 1) * 128, COLS['m_v']:COLS['m_v'] + 256].rearrange(
                          "p (h d) -> p h d", h=4), [self.proj], [vaug])
            bigf = K.sb([128, 16], F32, "bigf")
            K.dma("sp", bigf[:, 0:8], bc(I["b_ig"][l:l + 1, :], [128, 8]), [I["b_ig"]], [bigf])
            K.dma("sp", bigf[:, 8:16], bc(I["b_fg"][l:l + 1, :], [128, 8]), [I["b_fg"]], [bigf])
            if latent:
                with K.scope():
                    self.rope(q, nt)
                    self.rope(k, nt)
            qT = K.sb([64, 4, T_], F32, "mqT")
            qTb = K.sb([64, 4, T_], BF16, "mqTb")
            kT = K.sb([64, 4, T_], BF16, "mkT")
            self.tok2feat(qT, q, nt)
            K.cp("pool", qTb[:], qT[:], [qT], [qTb])
            self.tok2feat(kT, k, nt)
            K.tt("dve", ifg[:], ifg[:], bc(bigf[:].unsqueeze(1), [128, nt, 16]), ALU.add, [ifg, bigf], [ifg])
            lf = K.sb([128, nt, 8], F32, "mlf")
            K.act(lf[:], ifg[:, :, 8:16], AF.Exp, [ifg], [lf], scale=-1.0)
            K.act(lf[:], lf[:], AF.Ln, [lf], [lf], bias=1.0)
            K.ts("dve", lf[:], lf[:], -1.0, None, ALU.mult, None, [lf], [lf])
            Bcol = K.sb([128, nt, 8], F32, "mB")
            Bmat = K.sb([8, T_], F32, "mBmat")
            with K.scope():
                lfT = K.sb([8, T_], F32, "mlfT")
                for cg in range((nt + 3) // 4):
                    ps = K.psum()
                    nn_ = min(4, nt - cg * 4)
                    for i_ in range(nn_):
                        K.tr(ps[0:8, i_ * 128:(i_ + 1) * 128], lf[:, cg * 4 + i_, :], self.ident[:], [lf, self.ident], [ps])
                    K.cp("dve", lfT[:, cg * 512:cg * 512 + nn_ * 128], ps[0:8, 0:nn_ * 128], [ps], [lfT])
                one8 = K.sb([8, T_], F32, "mone8")
                K.memset("dve", one8[:], 1.0, [one8])
                Bp = K.sb([8, T_], F32, "mBp")
                K.S.op("dve", lambda e: e.tensor_tensor_scan(Bp[:], one8[:], lfT[:], 0.0, ALU.mult, ALU.add),
                       [one8.b, lfT.b], [Bp.b])
                isf = K.sb([8, 1], F32, "misf")
                K.ts("dve", isf[:], self.pidx[0:8, :], 4.0, None, ALU.is_lt, None, [self.pidx], [isf])
                K.tt("dve", lfT[:], lfT[:], Bp[:], ALU.subtract, [lfT, Bp], [lfT])
                K.ts("dve", lfT[:], lfT[:], Bp[:, T_ - 1:T_], None, ALU.add, None, [lfT, Bp], [lfT])
                K.tt("dve", Bp[:], Bp[:], lfT[:], ALU.subtract, [Bp, lfT], [Bp])
                K.stt("dve", Bmat[:], Bp[:], isf[:, 0:1], lfT[:], ALU.mult, ALU.add, [Bp, isf, lfT], [Bmat])
            ps = K.psum()
            for n in range(nt):
                K.tr(ps[:, n * 8:(n + 1) * 8], Bmat[:, n * 128:(n + 1) * 128], self.ident[0:8, 0:8], [Bmat, self.ident], [ps])
            K.cp("act", Bcol[:], ps[:, 0:nt * 8].rearrange("p (n j) -> p n j", j=8), [ps], [Bcol])
            cb = K.sb([128, nt, 8], F32, "mcb")
            K.tt("dve", cb[:], ifg[:, :, 0:8], Bcol[:], ALU.subtract, [ifg, Bcol], [cb])
            K.ts("dve", cb[:], cb[:], LN8, None, ALU.add, None, [cb], [cb])
            if latent:
                m0b = K.sb([128, 8], F32, "m0b")
                K.dma("sp", m0b[:], bc(I["sm"][l:l + 1, :], [128, 8]), [I["sm"]], [m0b])
                c0a = K.sb([64, 8, 65], F32, "c0a")
                K.dma("sp", c0a[:, :, 0:64], I["sC"][l].rearrange("d h k v -> k (d h) v"), [I["sC"]], [c0a])
                K.dma("sp", c0a[:, :, 64], I["sn"][l].rearrange("d h k -> k (d h)"), [I["sn"]], [c0a],
                      allow_slow_non_contiguous=True)
            hsum = K.sb([128, nt, 256], F32, "mhs")
            K.memset("pool", hsum[:], 0.0, [hsum])
            C = dict(nt=nt, T=T_, latent=latent, qT=qT, qTb=qTb, kT=kT, vaug=vaug, Bcol=Bcol, Bmat=Bmat, cb=cb, hsum=hsum,
                     m0b=(m0b if latent else None), c0a=(c0a if latent else None))
            with K.scope():
                nch = (T_ + 511) // 512
                for dr in range(2):
                    K.interleave([self._mlstm_pre(dr * 4 + h, C) for h in range(4)])
                    for c in range(nch):
                        K.interleave([self._mlstm_unit(dr * 4 + h, c, C) for h in range(4)])
            self.head_norm(hsum, nt, True)
            K.act(og[:], og[:], AF.Sigmoid, [og], [og])
            K.tt("dve", hsum[:], hsum[:], og[:], ALU.mult, [hsum, og], [hsum])
            self.store_branch(0, tok0, nt, hsum)
            if not latent:
                self.mlstm_state(l, pb, nt, T_, k, vaug, ifg, lf)

    def _mlstm_pre(self, j, C):
        K = self.K
        nt, T_, latent = C["nt"], C["T"], C["latent"]
        qT, Bcol, m0b = C["qT"], C["Bcol"], C["m0b"]
        h = j % 4
        sx = f"_{h}"
        Brow = K.rot("mBrow" + sx, [128, T_], F32, 1)
        C["Brow", j] = Brow
        Bmat = C["Bmat"]
        for c0 in range(0, T_, 512):
            w = min(512, T_ - c0)
            ps = K.psum(hold=True, nbanks=8)
            K.mm(ps[:, 0:w], self.sel8[:, j, :], Bmat[:, c0:c0 + w], True, True, [self.sel8, Bmat], [ps])
            yield
            K.cp("act", Brow[:, c0:c0 + w], ps[:, 0:w], [ps], [Brow])
            K.psfree(ps)
        if latent:
            qTw = K.rot("mqTw" + sx, [64, T_], F32, 1)
            C["qTw", j] = qTw
            K.act(qTw[:], Brow[0:64, :], AF.Exp, [Brow, m0b], [qTw], bias=m0b[0:64, j:j + 1])
            K.tt("dve", qTw[:], qT[:, h, :], qTw[:], ALU.mult, [qT, qTw], [qTw])

    def _mlstm_unit(self, j, c, C):
        K = self.K
        nt, T_, latent = C["nt"], C["T"], C["latent"]
        qT, kT, vaug, cb, hsum, c0a = (C[k_] for k_ in ("qTb", "kT", "vaug", "cb", "hsum", "c0a"))
        Brow = C["Brow", j]
        dr, h = j // 4, j % 4
        fwd = dr == 0
        sx = f"_{h}"
        c0 = c * 512
        c1 = min(T_, c0 + 512)
        order = list(range(nt)) if fwd else list(range(nt - 1, -1, -1))
        acc = K.psum(hold=True, nbanks=8)
        started = False
        steps = []
        for m in order:
            ta, tb = (m * 128, T_) if fwd else (0, (m + 1) * 128)
            ca, cb_ = max(ta, c0), min(tb, c1)
            if ca < cb_:
                steps.append((m, ca, cb_))
        for si, (m, ca, cb_) in enumerate(steps):
            w = cb_ - ca
            rc = m * 128 + 127 if fwd else m * 128
            refc = Brow[:, rc:rc + 1]
            st = K.rot("mst" + sx, [128, 2], F32, 3)
            K.ts("dve", st[:, 0:1], refc, -1.0, None, ALU.mult, None, [Brow], [st])
            K.act(st[:, 1:2], cb[:, m, j:j + 1], AF.Exp, [cb, Brow], [st], bias=refc)
            vs = K.rot("mvs" + sx, [128, 65], BF16, 3)
            K.ts("dve", vs[:], vaug[:, m, h, :], st[:, 1:2], None, ALU.mult, None, [vaug, st], [vs])
            ps = K.psum(hold=True, nbanks=8)
            K.mm(ps[:, 0:w], kT[:, h, m * 128:(m + 1) * 128], qT[:, h, ca:cb_], True, True, [kT, qT], [ps])
            E1 = K.rot("mE1" + sx, [128, 512], F32, 1)
            K.act(E1[:, 0:w], Brow[:, ca:cb_], AF.Exp, [Brow, st], [E1], bias=st[:, 0:1])
            yield
            Pm = K.rot("mPm" + sx, [128, 512], BF16, 2)
            K.tt("dve", Pm[:, 0:w], ps[:, 0:w], E1[:, 0:w], ALU.mult, [ps, E1], [Pm])
            K.psfree(ps)
            if ca <= m * 128 < cb_:
                off = m * 128 - ca
                if fwd:
                    K.S.op("pool", lambda e, Pm=Pm, off=off: e.affine_select(
                        Pm[:, off:off + 128], Pm[:, off:off + 128], [[1, 128]], ALU.is_ge, 0.0, base=0,
                        channel_multiplier=-1), [Pm.b], [Pm.b])
                else:
                    K.S.op("pool", lambda e, Pm=Pm, off=off: e.affine_select(
                        Pm[:, off:off + 128], Pm[:, off:off + 128], [[-1, 128]], ALU.is_ge, 0.0, base=0,
                        channel_multiplier=1), [Pm.b], [Pm.b])
            last = (si == len(steps) - 1) and not latent
            K.mm(acc[0:65, ca - c0:cb_ - c0], vs[:], Pm[:, 0:w], si == 0, last, [vs, Pm], [acc])
        if latent:
            qTw = C["qTw", j]
            K.mm(acc[0:65, 0:c1 - c0], c0a[:, j, :], qTw[:, c0:c1], False, True, [c0a, qTw], [acc])
        yield
        hTj = K.rot("mhTj" + sx, [65, 512], F32, 1)
        K.cp("act", hTj[0:65, 0:c1 - c0], acc[0:65, 0:c1 - c0], [acc], [hTj])
        K.psfree(acc)
        ntl = (c1 - c0) // 128
        n0 = c0 // 128
        ps = K.psum(hold=True, nbanks=8)
        for i_ in range(ntl):
            K.tr(ps[:, i_ * 65:(i_ + 1) * 65], hTj[0:65, i_ * 128:(i_ + 1) * 128], self.ident[0:65, 0:65],
                 [hTj, self.ident], [ps])
        yield
        X3 = ps[:, 0:ntl * 65].rearrange("p (n c) -> p n c", c=65)
        den = K.rot("mden" + sx, [128, 4], F32, 2)
        K.ts("dve", den[:, 0:ntl], X3[:, :, 64], -1.0, None, ALU.mult, None, [ps], [den])
        K.tt("dve", den[:, 0:ntl], den[:, 0:ntl], X3[:, :, 64], ALU.max, [ps, den], [den])
        K.ts("dve", den[:, 0:ntl], den[:, 0:ntl], 1.0, None, ALU.max, None, [den], [den])
        K.S.op("dve", lambda e, den=den: e.reciprocal(den[:, 0:ntl], den[:, 0:ntl]), [den.b], [den.b])
        tmp = K.rot("mtmp" + sx, [128, 4, 64], F32, 2)
        K.tt("dve", tmp[:, 0:ntl, :], X3[:, :, 0:64], bc(den[:, 0:ntl].unsqueeze(2), [128, ntl, 64]), ALU.mult,
             [ps, den], [tmp])
        K.psfree(ps)
        hs_ = hsum[:, n0:n0 + ntl, h * 64:(h + 1) * 64]
        K.tt("pool", hs_, hs_, tmp[:, 0:ntl, :], ALU.add, [hsum, tmp], [hsum])

    def mlstm_state(self, l, pb, nt, T_, k, vaug, ifg, lf):
        K = self.K
        LN8 = math.log(0.125)
        rows = K.sb([8, 2, T_], F32, "msrow")
        for which, c0 in ((0, None), (1, 0)):
            ps = K.psum()
            for n in range(nt):
                src = lf[:, n, :] if which == 0 else ifg[:, n, 0:8]
                K.tr(ps[0:8, n * 128:(n + 1) * 128], src, self.ident[:], [lf, ifg, self.ident], [ps])
            K.cp("dve", rows[:, which, :], ps[0:8, 0:T_], [ps], [rows])
        one8 = K.sb([8, T_], F32, "msone")
        K.memset("dve", one8[:], 1.0, [one8])
        Bp = K.sb([8, T_], F32, "msB")
        K.S.op("dve", lambda e: e.tensor_tensor_scan(Bp[:], one8[:], rows[:, 0, :], 0.0, ALU.mult, ALU.add),
               [one8.b, rows.b], [Bp.b])
        isf = K.sb([8, 1], F32, "msisf")
        K.ts("dve", isf[:], self.pidx[0:8, :], 4.0, None, ALU.is_lt, None, [self.pidx], [isf])
        a1 = K.sb([8, T_], F32, "msa1")
        a2 = K.sb([8, T_], F32, "msa2")
        K.ts("dve", a1[:], Bp[:], -1.0, Bp[:, T_ - 1:T_], ALU.mult, ALU.add, [Bp], [a1])
        K.tt("dve", a2[:], Bp[:], rows[:, 0, :], ALU.subtract, [Bp, rows], [a2])
        K.tt("dve", a1[:], a1[:], a2[:], ALU.subtract, [a1, a2], [a1])
        K.tt("dve", a2[:], a2[:], rows[:, 1, :], ALU.add, [a2, rows], [a2])
        lw = K.sb([8, T_], F32, "mslw")
        K.stt("dve", lw[:], a1[:], isf[:, 0:1], a2[:], ALU.mult, ALU.add, [a1, isf, a2], [lw])
        mnew = K.sb([8, 2], F32, "msm")
        K.red(mnew[:, 0:1], lw[:], ALU.max, [lw], [mnew])
        K.tt("dve", mnew[:, 0:1], mnew[:, 0:1], Bp[:, T_ - 1:T_], ALU.max, [mnew, Bp], [mnew])
        K.ts("dve", mnew[:, 1:2], mnew[:, 0:1], -1.0, LN8, ALU.mult, ALU.add, [mnew], [mnew])
        K.dma("sp", self.O["nm"][pb, l, :].rearrange("(j o) -> j o", o=1), mnew[:, 0:1], [mnew], [self.O["nm"]])
        wrow = K.sb([8, T_], F32, "mswr")
        K.act(wrow[:], lw[:], AF.Exp, [lw, mnew], [wrow], bias=mnew[:, 1:2])
        wcol = K.sb([128, nt, 8], F32, "mswc")
        ps = K.psum()
        for n in range(nt):
            K.tr(ps[:, n * 8:(n + 1) * 8], wrow[:, n * 128:(n + 1) * 128], self.ident[0:8, 0:8], [wrow, self.ident], [ps])
        K.cp("dve", wcol[:], ps[:, 0:nt * 8].rearrange("p (n j) -> p n j", j=8), [ps], [wcol])
        kw = K.sb([128, nt, 8, 64], F32, "mskw")
        for n in range(nt):
            for dr in range(2):
                K.tt("dve", kw[:, n, dr * 4:(dr + 1) * 4, :], k[:, n, :].rearrange("p (h d) -> p h d", h=4),
                     bc(wcol[:, n, dr * 4:(dr + 1) * 4].unsqueeze(2), [128, 4, 64]), ALU.mult, [k, wcol], [kw])
        cst = K.sb([64, 8, 65], F32, "mscst")
        for jg in range(2):
            ps = K.psum()
            for jj in range(4):
                j = jg * 4 + jj
                for n in range(nt):
                    K.mm(ps[0:64, jj * 65:(jj + 1) * 65], kw[:, n, j, :], vaug[:, n, j % 4, :], n == 0, n == nt - 1,
                         [kw, vaug], [ps])
            K.cp("act", cst[:, jg * 4:(jg + 1) * 4, :], ps[0:64, 0:260].rearrange("p (j c) -> p j c", j=4), [ps], [cst])
        K.dma("sp", self.O["nC"][pb, l].rearrange("j k v -> k j v"), cst[:, :, 0:64], [cst], [self.O["nC"]])
        K.dma("sp", self.O["nn"][pb, l].rearrange("j k -> k j"), cst[:, :, 64], [cst], [self.O["nn"]],
              allow_slow_non_contiguous=True)

    def lowrank(self, dst, srcT_tile, nt, W2pad, rows):
        pass

    def gla(self, l, tok0, T_, latent, pb):
        K = self.K
        I = self.I
        nt = T_ // 128
        with K.scope():
            q = K.sb([128, nt, 256], F32, "gq")
            k = K.sb([128, nt, 256], F32, "gk")
            v = K.sb([128, nt, 256], F32, "gv")
            gg = K.sb([128, nt, 256], F32, "ggg")
            ga = K.sb([128, nt, 32], F32, "gga")
            self.load_tok(q, tok0, nt, COLS['g_q'], 256)
            self.load_tok(k, tok0, nt, COLS['g_k'], 256)
            self.load_tok(v, tok0, nt, COLS['g_v'], 256)
            self.load_tok(gg, tok0, nt, COLS['g_g'], 256)
            self.load_tok(ga, tok0, nt, COLS['g_a'], 32)
            if latent:
                self.rope(q, nt)
                self.rope(k, nt)
            K.ts("dve", q[:], q[:], 0.125, None, ALU.mult, None, [q], [q])
            W2 = K.sb([32, 2, 256], F32, "gW2")
            K.memset("dve", W2[:], 0.0, [W2])
            K.dma("sp", W2[0:16, 0, :], I["w_gla_a2"][l, 0], [I["w_gla_a2"]], [W2])
            K.dma("sp", W2[16:32, 1, :], I["w_gla_a2"][l, 1], [I["w_gla_a2"]], [W2])
            bgl = K.sb([128, 2, 256], F32, "gbgl")
            K.dma("sp", bgl[:], bc(I["b_gla_a"][l:l + 1].rearrange("o d c -> o (d c)"), [128, 512]).rearrange(
                "p (d c) -> p d c", d=2), [I["b_gla_a"]], [bgl])
            g = K.sb([128, nt, 2, 256], F32, "gg_")
            for n in range(nt):
                gaT = K.rot("gaT", [32, 128], F32, 2)
                self.transpose_block(gaT[:], gaT, ga[:, n, :], ga, 128, 32)
                for dr in range(2):
                    ps = K.psum()
                    K.mm(ps[:, 0:256], gaT[:], W2[:, dr, :], True, True, [gaT, W2], [ps])
                    K.tt("dve", g[:, n, dr, :], ps[:, 0:256], bgl[:, dr, :], ALU.add, [ps, bgl], [g])
            K.act(g[:], g[:], AF.Exp, [g], [g], scale=-1.0)
            K.act(g[:], g[:], AF.Ln, [g], [g], bias=1.0)
            K.ts("dve", g[:], g[:], -1.0 / 16.0, None, ALU.mult, None, [g], [g])
            Sst = K.sb([64, 8, 64], F32, "gS")
            Sv = [T(Sst.h, f"gS{j}") for j in range(8)]
            if latent:
                K.dma("sp", Sst[:], I["sg"][l].rearrange("d h k v -> k (d h) v"), [I["sg"]], Sv)
            else:
                K.memset("dve", Sst[:], 0.0, Sv)
            osum = K.sb([128, nt, 256], F32, "gos")
            for dr in range(2):
                order = list(range(nt)) if dr == 0 else list(range(nt - 1, -1, -1))
                LT = self.U if dr == 0 else self.Lo
                for idx, n in enumerate(order):
                    zero_state = (not latent) and idx == 0
                    gs = g[:, n, dr, :]
                    psG = K.psum()
                    K.mm(psG[:, 0:256], LT[:], gs, True, True, [LT, g], [psG])
                    eG = K.rot("geG", [128, 256], F32, 2)
                    enG = K.rot("genG", [128, 256], F32, 2)
                    K.act(eG[:], psG[:, 0:256], AF.Exp, [psG], [eG])
                    K.act(enG[:], psG[:, 0:256], AF.Exp, [psG], [enG], scale=-1.0)
                    psT = K.psum()
                    K.mm(psT[:, 0:256], self.ones[:], gs, True, True, [self.ones, g], [psT])
                    eGt = K.rot("geGt", [128, 256], F32, 2)
                    K.act(eGt[:], psT[:, 0:256], AF.Exp, [psT], [eGt])
                    qt = K.rot("gqt", [128, 256], F32, 2)
                    kt = K.rot("gkt", [128, 256], F32, 2)
                    kh = K.rot("gkh", [128, 256], F32, 2)
                    K.tt("dve", qt[:], q[:, n, :], eG[:], ALU.mult, [q, eG], [qt])
                    K.tt("pool", kt[:], k[:, n, :], enG[:], ALU.mult, [k, enG], [kt])
                    K.tt("pool", kh[:], kt[:], eGt[:], ALU.mult, [kt, eGt], [kh])
                    psd = K.psum()
                    for h in range(4):
                        K.mm(psd[0:64, h:h + 1], g[:, n, dr, h * 64:(h + 1) * 64], self.ones[:, 0:1], True, True,
                             [g, self.ones], [psd])
                    dcol = K.rot("gdcol", [64, 4], F32, 2)
                    K.act(dcol[:], psd[0:64, 0:4], AF.Exp, [psd], [dcol])
                    qtT = K.rot("gqtT", [64, 4, 128], F32, 2)
                    ktT = K.rot("gktT", [64, 4, 128], F32, 2)
                    for (dstT, src_) in ((qtT, qt), (ktT, kt)):
                        ps = K.psum()
                        for h in range(4):
                            K.tr(ps[0:64, h * 128:(h + 1) * 128], src_[:, h * 64:(h + 1) * 64], self.ident[:],
                                 [src_, self.ident], [ps])
                        K.cp(K.evq(), dstT[:], ps[0:64, :].rearrange("p (h t) -> p h t", h=4), [ps], [dstT])
                    acc = K.psacc()
                    psU = K.psum()
                    for h in range(4):
                        j = dr * 4 + h
                        psA = K.psum()
                        K.mm(psA[:, 0:128], ktT[:, h, :], qtT[:, h, :], True, True, [ktT, qtT], [psA])
                        attm = K.rot("gatt", [128, 128], F32, 3)
                        K.tt("dve", attm[:], psA[:, 0:128], LT[:], ALU.mult, [psA, LT], [attm])
                        K.mm(acc[:, h * 64:(h + 1) * 64], attm[:], v[:, n, h * 64:(h + 1) * 64], True, zero_state,
                             [attm, v], [acc])
                        if not zero_state:
                            K.mm(acc[:, h * 64:(h + 1) * 64], qtT[:, h, :], Sst[:, j, :], False, True,
                                 [qtT, Sv[j]], [acc])
                        K.mm(psU[0:64, h * 64:(h + 1) * 64], kh[:, h * 64:(h + 1) * 64], v[:, n, h * 64:(h + 1) * 64],
                             True, True, [kh, v], [psU])
                        K.stt("dve", Sst[:, j, :], Sst[:, j, :], dcol[:, h:h + 1], psU[0:64, h * 64:(h + 1) * 64],
                              ALU.mult, ALU.add, [Sv[j], dcol, psU], [Sv[j]])
                    if dr == 0:
                        K.cp("act", osum[:, n, :], acc[:, 0:256], [acc], [osum])
                    else:
                        K.tt("dve", osum[:, n, :], osum[:, n, :], acc[:, 0:256], ALU.add, [osum, acc], [osum])
            self.head_norm(osum, nt, False)
            K.act(gg[:], gg[:], AF.Silu, [gg], [gg])
            K.tt("dve", osum[:], osum[:], gg[:], ALU.mult, [osum, gg], [osum])
            self.store_branch(1, tok0, nt, osum)
            if not latent:
                K.dma("sp", self.O["ng"][pb, l].rearrange("j k v -> k j v"), Sst[:], Sv, [self.O["ng"]])

    def rwkv(self, l, tok0, T_, latent, pb):
        K = self.K
        I = self.I
        nt = T_ // 128
        C0 = COLS['r_rkv']
        with K.scope():
            rkv = K.sb([128, nt, 768], F32, "rrkv")
            logw = K.sb([128, nt, 2, 256], F32, "rlogw")
            aa = K.sb([128, nt, 2, 256], F32, "raa")
            kvec = K.sb([128, 3, 256], F32, "rkvec")
            self._rwkv_pre(l, tok0, nt, rkv, logw, aa, kvec)
            self._rwkv_main(l, tok0, nt, latent, pb, rkv, logw, aa, kvec)

    def _rwkv_pre(self, l, tok0, nt, rkv, logw, aa, kvec):
        K = self.K
        I = self.I
        C0 = COLS['r_rkv']
        with K.scope():
            taps = K.sb([128, 3, 768], F32, "rtaps")
            K.dma("sp", taps[:], bc(I["shift_rwkv"][l:l + 1].rearrange("o s c -> o (s c)"), [128, 2304]).rearrange(
                "p (s c) -> p s c", s=3), [I["shift_rwkv"]], [taps])
            for n in range(nt):
                r0 = tok0 + n * 128
                xm = K.rot("rxm", [128, 768], F32, 2)
                xp = K.rot("rxp", [128, 768], F32, 2)
                K.dma(K.dq(), rkv[:, n, :], self.proj[r0:r0 + 128, C0:C0 + 768], [self.proj], [rkv])
                if n == 0:
                    K.memset("pool", xm[:], 0.0, [xm])
                    K.dma(K.dq(), xm[1:128, :], self.proj[r0:r0 + 127, C0:C0 + 768], [self.proj], [xm])
                else:
                    K.dma(K.dq(), xm[:], self.proj[r0 - 1:r0 + 127, C0:C0 + 768], [self.proj], [xm])
                if n == nt - 1:
                    K.memset("pool", xp[:], 0.0, [xp])
                    K.dma(K.dq(), xp[0:127, :], self.proj[r0 + 1:r0 + 128, C0:C0 + 768], [self.proj], [xp])
                else:
                    K.dma(K.dq(), xp[:], self.proj[r0 + 1:r0 + 129, C0:C0 + 768], [self.proj], [xp])
                K.tt("dve", rkv[:, n, :], rkv[:, n, :], taps[:, 1, :], ALU.mult, [rkv, taps], [rkv])
                K.tt("pool", xm[:], xm[:], taps[:, 0, :], ALU.mult, [xm, taps], [xm])
                K.tt("pool", xp[:], xp[:], taps[:, 2, :], ALU.mult, [xp, taps], [xp])
                K.tt("dve", rkv[:, n, :], rkv[:, n, :], xm[:], ALU.add, [rkv, xm], [rkv])
                K.tt("dve", rkv[:, n, :], rkv[:, n, :], xp[:], ALU.add, [rkv, xp], [rkv])
        with K.scope():
            lwag = K.sb([128, nt, 128], F32, "rlwag")
            self.load_tok(lwag, tok0, nt, COLS['r_w'], 128)
            K.act(lwag[:, :, 0:64], lwag[:, :, 0:64], AF.Tanh, [lwag], [lwag])
            Ww = K.sb([64, 2, 256], F32, "rWw")
            Wa = K.sb([64, 2, 256], F32, "rWa")
            K.memset("dve", Ww[:], 0.0, [Ww])
            K.memset("dve", Wa[:], 0.0, [Wa])
            for dr in range(2):
                K.dma("sp", Ww[dr * 32:(dr + 1) * 32, dr, :], I["w_w2"][l, dr], [I["w_w2"]], [Ww])
                K.dma("sp", Wa[dr * 32:(dr + 1) * 32, dr, :], I["w_a2"][l, dr], [I["w_a2"]], [Wa])
            w0b = K.sb([128, 2, 256], F32, "rw0b")
            a0b = K.sb([128, 2, 256], F32, "ra0b")
            K.dma("sp", w0b[:], bc(I["w0_rwkv"][l:l + 1].rearrange("o d c -> o (d c)"), [128, 512]).rearrange(
                "p (d c) -> p d c", d=2), [I["w0_rwkv"]], [w0b])
            K.dma("sp", a0b[:], bc(I["a0_rwkv"][l:l + 1].rearrange("o d c -> o (d c)"), [128, 512]).rearrange(
                "p (d c) -> p d c", d=2), [I["a0_rwkv"]], [a0b])
            for i_, nm in enumerate(("k_k", "k_a", "r_k")):
                K.dma("sp", kvec[:, i_, :], bc(I[nm][l:l + 1, :], [128, 256]), [I[nm]], [kvec])
            for n in range(nt):
                ps = K.psum()
                for i_ in range(2):
                    K.tr(ps[0:64, i_ * 128:(i_ + 1) * 128], lwag[:, n, i_ * 64:(i_ + 1) * 64], self.ident[:],
                         [lwag, self.ident], [ps])
                tT = K.rot("rtT", [64, 2, 128], F32, 2)
                K.cp(K.evq(), tT[:], ps[0:64, 0:256].rearrange("p (i t) -> p i t", i=2), [ps], [tT])
                for dr in range(2):
                    ps = K.psum()
                    K.mm(ps[:, 0:256], tT[:, 0, :], Ww[:, dr, :], True, True, [tT, Ww], [ps])
                    K.tt("dve", logw[:, n, dr, :], ps[:, 0:256], w0b[:, dr, :], ALU.add, [ps, w0b], [logw])
                    ps = K.psum()
                    K.mm(ps[:, 0:256], tT[:, 1, :], Wa[:, dr, :], True, True, [tT, Wa], [ps])
                    K.tt("dve", aa[:, n, dr, :], ps[:, 0:256], a0b[:, dr, :], ALU.add, [ps, a0b], [aa])
            K.act(logw[:], logw[:], AF.Sigmoid, [logw], [logw])
            K.ts("dve", logw[:], logw[:], -math.exp(-0.5), None, ALU.mult, None, [logw], [logw])
            K.act(aa[:], aa[:], AF.Sigmoid, [aa], [aa])

    def _rwkv_unit(self, dr, n, idx, C):
        K = self.K
        sx = f"_{dr}"
        fwd = dr == 0
        rkv, logw, aa, kvec, kap, Sst, Sv, ysum = (C[k_] for k_ in ("rkv", "logw", "aa", "kvec", "kap", "Sst", "Sv", "ysum"))
        zero_state = (not C["latent"]) and idx == 0
        LT = self.U if fwd else self.Lo
        mN = self.Ls if fwd else self.Us
        mNT = self.Us if fwd else self.Ls
        mMT = self.U if fwd else self.Lo
        lw_ = logw[:, n, dr, :]
        a_ = aa[:, n, dr, :]
        r_ = rkv[:, n, 0:256]
        k_ = rkv[:, n, 256:512]
        v_ = rkv[:, n, 512:768]
        Sj = Sv[dr * 4:(dr + 1) * 4]
        S4 = Sst[:, dr * 4:(dr + 1) * 4, :]

        def m4(mask):
            return bc(mask[:].unsqueeze(1), [128, 4, 128])
        psL = K.psum(hold=True, nbanks=8)
        psT = K.psum(hold=True, nbanks=8)
        K.mm(psL[:, 0:256], LT[:], lw_, True, True, [LT, logw], [psL])
        K.mm(psT[:, 0:256], self.ones[:], lw_, True, True, [self.ones, logw], [psT])
        for h in range(4):
            K.mm(psT[0:64, 256 + h:257 + h], logw[:, n, dr, h * 64:(h + 1) * 64], self.ones[:, 0:1], True, True,
                 [logw, self.ones], [psT])
        yield
        eL = K.rot("reL" + sx, [128, 256], F32, 1)
        enL = K.rot("renL" + sx, [128, 256], F32, 1)
        eLex = K.rot("reLex" + sx, [128, 256], F32, 1)
        eLt = K.rot("reLt" + sx, [128, 256], F32, 1)
        pcol = K.rot("rpcol" + sx, [64, 4], F32, 2)
        K.act(eL[:], psL[:, 0:256], AF.Exp, [psL], [eL])
        K.act(enL[:], psL[:, 0:256], AF.Exp, [psL], [enL], scale=-1.0)
        K.tt("dve", eLex[:], psL[:, 0:256], lw_, ALU.subtract, [psL, logw], [eLex])
        K.act(eLex[:], eLex[:], AF.Exp, [eLex], [eLex])
        K.act(eLt[:], psT[:, 0:256], AF.Exp, [psT], [eLt])
        K.act(pcol[:], psT[0:64, 256:260], AF.Exp, [psT], [pcol])
        K.psfree(psL)
        K.psfree(psT)
        Bt = K.rot("rBt" + sx, [128, 256], F32, 1)
        Kt = K.rot("rKt" + sx, [128, 256], F32, 1)
        At, Rt, Bh, Kh = eLex, eL, enL, eLt
        K.stt("dve", At[:], kap[:, n, :], -1.0, eLex[:], ALU.mult, ALU.mult, [kap, eLex], [At])
        K.tt("pool", Bt[:], a_, kap[:, n, :], ALU.mult, [aa, kap], [Bt])
        K.tt("pool", Bt[:], Bt[:], enL[:], ALU.mult, [Bt, enL], [Bt])
        K.stt("dve", Kt[:], a_, -1.0, kvec[:, 1, :], ALU.add, ALU.mult, [aa, kvec], [Kt])
        K.stt("dve", Kt[:], Kt[:], 1.0, k_, ALU.add, ALU.mult, [Kt, rkv], [Kt])
        K.tt("pool", Kt[:], Kt[:], enL[:], ALU.mult, [Kt, enL], [Kt])
        K.tt("pool", Rt[:], r_, eL[:], ALU.mult, [rkv, eL], [Rt])
        K.tt("pool", Bh[:], Bt[:], eLt[:], ALU.mult, [Bt, eLt, enL], [Bh])
        K.tt("pool", Kh[:], Kt[:], eLt[:], ALU.mult, [Kt, eLt], [Kh])
        TT = {}
        for pair in ((("A", At), ("B", Bt)), (("K", Kt), ("R", Rt))):
            pss = []
            for nm_, src_ in pair:
                ps = K.psum(hold=True, nbanks=8)
                for h in range(4):
                    K.tr(ps[0:64, h * 128:(h + 1) * 128], src_[:, h * 64:(h + 1) * 64], self.ident[:],
                         [src_, self.ident], [ps])
                pss.append(ps)
            yield
            for (nm_, src_), ps in zip(pair, pss):
                dstT = K.rot("rT" + nm_ + sx, [64, 4, 128], BF16, 1)
                K.cp(K.evq(), dstT[:], ps[0:64, :].rearrange("p (h t) -> p h t", h=4), [ps], [dstT])
                K.psfree(ps)
                TT[nm_] = dstT
        AtT, BtT, KtT, RtT = TT["A"], TT["B"], TT["K"], TT["R"]
        Bhb = K.rot("rBhb" + sx, [128, 256], BF16, 1)
        Khb = K.rot("rKhb" + sx, [128, 256], BF16, 1)
        vb = K.rot("rvb" + sx, [128, 256], BF16, 1)
        K.cp("pool", Bhb[:], Bh[:], [Bh], [Bhb])
        K.cp("pool", Khb[:], Kh[:], [Kh], [Khb])
        K.cp("pool", vb[:], v_, [rkv], [vb])
        Sb = K.rot("rSb" + sx, [64, 4, 64], BF16, 1)
        if not zero_state:
            K.cp("act", Sb[:], S4, Sj, [Sb])
        v4 = lambda ps_: ps_[:].rearrange("p (h t) -> p h t", h=4)

        def prod_mm(lt, rt):
            ps_ = K.psum(hold=True, nbanks=8)
            for h in range(4):
                K.mm(ps_[:, h * 128:(h + 1) * 128], lt[:, h, :], rt[:, h, :], True, True, [lt, rt], [ps_])
            return ps_

        def prod_ev(ps_, nm_, mask, n_=1):
            o_ = K.rot(nm_ + sx, [128, 4, 128], BF16, n_)
            K.tt("dve", o_[:], v4(ps_), m4(mask), ALU.mult, [ps_, mask], [o_])
            K.psfree(ps_)
            return o_
        p1 = prod_mm(AtT, BtT)
        p2 = prod_mm(BtT, AtT)
        yield
        Nk = prod_ev(p1, "rN", mN, 2)
        Ntk = prod_ev(p2, "rNt", mNT, 2)
        Xt = K.rot("rXt" + sx, [128, 4, 128], F32, 2)
        K.tt("pool", Xt[:], Ntk[:], m4(self.ident), ALU.add, [Ntk, self.ident], [Xt])
        Xtb = K.rot("rXtb" + sx, [128, 4, 128], BF16, 1)
        K.cp("act", Xtb[:], Xt[:], [Xt], [Xtb])
        for it in range(6):
            ps1 = prod_mm(Ntk, Nk)
            ps2 = prod_mm(Nk, Ntk) if it < 5 else None
            yield
            Nk2 = K.rot("rN" + sx, [128, 4, 128], BF16, 2)
            K.cp("act", Nk2[:], v4(ps1), [ps1], [Nk2])
            K.psfree(ps1)
            if it < 5:
                Ntk2 = K.rot("rNt" + sx, [128, 4, 128], BF16, 2)
                K.cp("dve", Ntk2[:], v4(ps2), [ps2], [Ntk2])
                K.psfree(ps2)
            ps3 = prod_mm(Nk2, Xtb)
            yield
            Xt2 = K.rot("rXt" + sx, [128, 4, 128], F32, 2)
            K.tt("dve", Xt2[:], Xt[:], v4(ps3), ALU.add, [Xt, ps3], [Xt2])
            K.psfree(ps3)
            Xt = Xt2
            Xtb = K.rot("rXtb" + sx, [128, 4, 128], BF16, 1)
            K.cp("act", Xtb[:], Xt[:], [Xt], [Xtb])
            Nk = Nk2
            if it < 5:
                Ntk = Ntk2
        p3 = prod_mm(KtT, AtT)
        p4 = prod_mm(BtT, RtT)
        p5 = prod_mm(KtT, RtT)
        yield
        NakT = prod_ev(p3, "rN", mNT, 2)
        MrbT = prod_ev(p4, "rNt", mMT, 2)
        MrkT = prod_ev(p5, "rN", mMT, 2)
        psR = K.psum(hold=True, nbanks=8)
        for h in range(4):
            hs = slice(h * 64, (h + 1) * 64)
            if not zero_state:
                K.mm(psR[:, hs], AtT[:, h, :], Sb[:, h, :], True, False, [AtT, Sb], [psR])
            K.mm(psR[:, hs], NakT[:, h, :], vb[:, hs], zero_state, True, [NakT, vb], [psR])
        yield
        Rsb = K.rot("rRsb" + sx, [128, 256], BF16, 1)
        K.cp("act", Rsb[:], psR[:, 0:256], [psR], [Rsb])
        K.psfree(psR)
        psU = K.psum(hold=True, nbanks=8)
        for h in range(4):
            hs = slice(h * 64, (h + 1) * 64)
            K.mm(psU[:, hs], Xtb[:, h, :], Rsb[:, hs], True, True, [Xtb, Rsb], [psU])
        yield
        Usb = K.rot("rUsb" + sx, [128, 256], BF16, 1)
        K.cp("act", Usb[:], psU[:, 0:256], [psU], [Usb])
        K.psfree(psU)
        psY = K.psum(hold=True, nbanks=8)
        psS = K.psum(hold=True, nbanks=8)
        for h in range(4):
            hs = slice(h * 64, (h + 1) * 64)
            if not zero_state:
                K.mm(psY[:, hs], RtT[:, h, :], Sb[:, h, :], True, False, [RtT, Sb], [psY])
            K.mm(psY[:, hs], MrbT[:, h, :], Usb[:, hs], zero_state, False, [MrbT, Usb], [psY])
            K.mm(psY[:, hs], MrkT[:, h, :], vb[:, hs], False, True, [MrkT, vb], [psY])
            K.mm(psS[0:64, hs], Bhb[:, hs], Usb[:, hs], True, False, [Bhb, Usb], [psS])
            K.mm(psS[0:64, hs], Khb[:, hs], vb[:, hs], False, True, [Khb, vb], [psS])
        yield
        K.tt("dve", ysum[:, n, :], ysum[:, n, :], psY[:, 0:256], ALU.add, [ysum, psY], [ysum])
        K.psfree(psY)
        K.tt("dve", S4, S4, bc(pcol[:].unsqueeze(2), [64, 4, 64]), ALU.mult, Sj + [pcol], Sj)
        K.tt("dve", S4, S4, psS[0:64, 0:256].rearrange("p (h v) -> p h v", h=4), ALU.add, Sj + [psS], Sj)
        K.psfree(psS)

    def _rwkv_main(self, l, tok0, nt, latent, pb, rkv, logw, aa, kvec):
        K = self.K
        I = self.I
        if True:
            kap = K.sb([128, nt, 256], F32, "rkap")
            kk_ = rkv[:, :, 256:512]
            K.tt("dve", kap[:], kk_, bc(kvec[:, 0, :].unsqueeze(1), [128, nt, 256]), ALU.mult, [rkv, kvec], [kap])
            ss = K.sb([128, nt * 4], F32, "rss")
            with K.scope():
                sq = K.rot("ropeA", [128, nt, 256], F32, 1)
                K.tt("pool", sq[:], kap[:], kap[:], ALU.mult, [kap], [sq])
                K.red(ss[:], sq[:].rearrange("p n (h d) -> p (n h) d", h=4), ALU.add, [sq], [ss])
            K.ts("dve", ss[:], ss[:], LN_EPS, None, ALU.add, None, [ss], [ss])
            K.act(ss[:], ss[:], AF.Sqrt, [ss], [ss])
            K.S.op("dve", lambda e: e.reciprocal(ss[:], ss[:]), [ss.b], [ss.b])
            kap3 = kap[:].rearrange("p n (h d) -> p (n h) d", h=4)
            K.tt("dve", kap3, kap3, bc(ss[:].unsqueeze(2), [128, nt * 4, 64]), ALU.mult, [kap, ss], [kap])
            Sst = K.sb([64, 8, 64], F32, "rS")
            Sv = [T(Sst.h, f"rS{j}") for j in range(8)]
            if latent:
                R0 = K.sb([64, 8, 64], F32, "rR0")
                K.dma("sp", R0[:], I["sr"][l].rearrange("d h v k -> v (d h) k"), [I["sr"]], [R0])
                for jg in range(2):
                    ps = K.psum()
                    for jj in range(4):
                        K.tr(ps[0:64, jj * 64:(jj + 1) * 64], R0[:, jg * 4 + jj, :], self.ident[0:64, 0:64],
                             [R0, self.ident], [ps])
                    K.cp("dve", Sst[:, jg * 4:(jg + 1) * 4, :], ps[0:64, 0:256].rearrange("p (j v) -> p j v", j=4),
                         [ps], Sv[jg * 4:(jg + 1) * 4])
            else:
                K.memset("dve", Sst[:], 0.0, Sv)
            ysum = K.sb([128, nt, 256], F32, "rys")
            K.memset("pool", ysum[:], 0.0, [ysum])
            ctx = dict(nt=nt, latent=latent, rkv=rkv, logw=logw, aa=aa, kvec=kvec, kap=kap, Sst=Sst, Sv=Sv, ysum=ysum)
            with K.scope():
                for idx in range(nt):
                    K.interleave([self._rwkv_unit(0, idx, idx, ctx), self._rwkv_unit(1, nt - 1 - idx, idx, ctx)])
            self.head_norm(ysum, nt, True)
            bt = K.rot("ropeA", [128, nt, 256], F32, 1)
            K.tt("dve", bt[:], rkv[:, :, 0:256], rkv[:, :, 256:512], ALU.mult, [rkv], [bt])
            K.tt("dve", bt[:], bt[:], bc(kvec[:, 2, :].unsqueeze(1), [128, nt, 256]), ALU.mult, [bt, kvec], [bt])
            bs = K.sb([128, nt * 4], F32, "rbs")
            K.red(bs[:], bt[:].rearrange("p n (h d) -> p (n h) d", h=4), ALU.add, [bt], [bs])
            bt4 = bt[:].rearrange("p n (h d) -> p n h d", h=4)
            K.tt("dve", bt4, rkv[:, :, 512:768].rearrange("p n (h d) -> p n h d", h=4),
                 bc(bs[:].rearrange("p (n h) -> p n h", h=4).unsqueeze(3), [128, nt, 4, 64]), ALU.mult, [rkv, bs], [bt])
            K.tt("dve", ysum[:], ysum[:], bt[:], ALU.add, [ysum, bt], [ysum])
            Wg = K.sb([64, 256], F32, "rWg")
            K.dma("sp", Wg[:], I["w_g2"][l], [I["w_g2"]], [Wg])
            rg = K.sb([128, nt, 64], F32, "rrg")
            self.load_tok(rg, tok0, nt, COLS['r_g'], 64)
            K.act(rg[:], rg[:], AF.Sigmoid, [rg], [rg])
            for n in range(nt):
                rgT = K.rot("rrgT", [64, 128], F32, 2)
                self.transpose_block(rgT[:], rgT, rg[:, n, :], rg, 128, 64)
                ps = K.psum()
                K.mm(ps[:, 0:256], rgT[:], Wg[:], True, True, [rgT, Wg], [ps])
                K.tt("dve", ysum[:, n, :], ysum[:, n, :], ps[:, 0:256], ALU.mult, [ysum, ps], [ysum])
            self.store_branch(2, tok0, nt, ysum)
            if not latent:
                So = K.sb([64, 8, 64], F32, "rSo")
                for jg in range(2):
                    ps = K.psum()
                    for jj in range(4):
                        K.tr(ps[0:64, jj * 64:(jj + 1) * 64], Sst[:, jg * 4 + jj, :], self.ident[0:64, 0:64],
                             [Sv[jg * 4 + jj], self.ident], [ps])
                    K.cp("dve", So[:, jg * 4:(jg + 1) * 4, :], ps[0:64, 0:256].rearrange("p (j k) -> p j k", j=4),
                         [ps], [So])
                K.dma("sp", self.O["nr"][pb, l].rearrange("j v k -> v j k"), So[:], [So], [self.O["nr"]])

    def na_tables(self, l, Toe):
        K = self.K
        I = self.I
        with K.scope():
            z = K.sb([60, 128], F32, "naz")
            K.memset("dve", z[:], 0.0, [z])
            K.dma("sp", self.rpbpad[l], z[:], [z], [self.rpbpad])
            K.dma("sp", self.rpbpad[l, :, 48:79], I["rpb"][l], [I["rpb"]], [self.rpbpad])
            Tp = K.sb([64, 60, 64], F32, "naTp")
            src_ap = bass.AP(self.rpbpad.h, l * 60 * 128, [[1, 64], [128, 60], [1, 64]])
            K.dma("sp", Tp[:], src_ap, [self.rpbpad], [Tp])
            val = K.sb([64, 128], F32, "naval")
            K.S.op("pool", lambda e: e.iota(val[:], [[1, 128]], base=0, channel_multiplier=1,
                                             allow_small_or_imprecise_dtypes=True), [], [val.b])
            J2 = K.sb([64, 128], F32, "naJ2")
            J2b = K.sb([64, 128], F32, "naJ2b")
            K.ts("dve", J2[:], val[:], 63.0, None, ALU.is_equal, None, [val], [J2])
            K.ts("dve", J2b[:], val[:], 127.0, None, ALU.is_equal, None, [val], [J2b])
            K.tt("dve", J2[:], J2[:], J2b[:], ALU.add, [J2, J2b], [J2])
            ge64 = K.sb([128, 1], F32, "nage")
            K.ts("dve", ge64[:], self.pidx[:], 64.0, None, ALU.is_ge, None, [self.pidx], [ge64])
            cs = K.sb([128, 1], F32, "nacs")
            K.stt("dve", cs[:], ge64[:], -64.0, self.pidx[:], ALU.mult, ALU.add, [ge64, self.pidx], [cs])
            K.ts("dve", cs[:], cs[:], -8.0, 0.0, ALU.add, ALU.max, [cs], [cs])
            K.ts("dve", cs[:], cs[:], 48.0, None, ALU.min, None, [cs], [cs])
            dl = K.sb([128, 64], F32, "nadl")
            ok2 = K.sb([128, 64], F32, "naok2")
            K.ts("dve", dl[:], self.fidx[:, 0:64], cs[:, 0:1], None, ALU.subtract, None, [self.fidx, cs], [dl])
            K.ts("dve", ok2[:], dl[:], 16.0, None, ALU.is_lt, None, [dl], [ok2])
            K.ts("dve", dl[:], dl[:], 0.0, None, ALU.is_ge, None, [dl], [dl])
            K.tt("dve", dl[:], dl[:], ok2[:], ALU.mult, [dl, ok2], [dl])
            K.ts("dve", dl[:], dl[:], -NEG, NEG, ALU.mult, ALU.add, [dl], [dl])
            Tpf = Tp[:].rearrange("p j c -> p (j c)")
            Tf = Toe[:].rearrange("p h x -> p (h x)")
            for cb in range(8):
                c0 = cb * 512
                cw = min(512, 3840 - c0)
                ps = K.psum()
                K.mm(ps[:, 0:cw], J2[:], Tpf[:, c0:c0 + cw], True, True, [J2, Tp], [ps])
                K.tt("dve", Tf[:, c0:c0 + cw].rearrange("p (j c) -> p j c", c=64),
                     ps[:, 0:cw].rearrange("p (j c) -> p j c", c=64),
                     bc(dl[:].unsqueeze(1), [128, cw // 64, 64]), ALU.add, [ps, dl], [Toe])

    def na(self, l, tok0, T_, latent, pb):
        K = self.K
        I = self.I
        nt = T_ // 128
        C0 = COLS['n_qkv']
        with K.scope():
            q = K.sb([128, nt, 256], F32, "nq")
            k = K.sb([128, nt, 256], F32, "nk")
            v = K.sb([128, nt, 256], BF16, "nv")
            self.load_tok(q, tok0, nt, C0, 256)
            self.load_tok(k, tok0, nt, C0 + 256, 256)
            self.load_tok(v, tok0, nt, C0 + 512, 256, eng="pool")
            qT = K.sb([64, 4, T_], BF16, "nqT")
            kT = K.sb([64, 4, T_], BF16, "nkT")
            self.tok2feat(qT, q, nt)
            self.tok2feat(kT, k, nt)
            nout = K.sb([128, nt, 256], F32, "nout")
            if not latent:
                K.dma("sp", self.O["nak"][pb, l].rearrange("h s d -> s h d"),
                      self.proj[tok0:tok0 + 256, C0 + 256:C0 + 512].rearrange("s (h d) -> s h d", h=4),
                      [self.proj], [self.O["nak"]])
                K.dma("act", self.O["nav"][pb, l].rearrange("h s d -> s h d"),
                      self.proj[tok0:tok0 + 256, C0 + 512:C0 + 768].rearrange("s (h d) -> s h d", h=4),
                      [self.proj], [self.O["nav"]])
                for h in range(4):
                    hs = slice(h * 64, (h + 1) * 64)
                    for n in range(nt):
                        ps = K.psum()
                        K.mm(ps[:, 0:T_], qT[:, h, n * 128:(n + 1) * 128], kT[:, h, :], True, True, [qT, kT], [ps])
                        st = K.rot("nst", [128, 2], F32, 3)
                        K.red(st[:, 0:1], ps[:, 0:T_], ALU.max, [ps], [st])
                        K.ts("dve", st[:, 0:1], st[:, 0:1], -0.125, None, ALU.mult, None, [st], [st])
                        p = K.rot("np", [128, T_], F32, 2)
                        K.act(p[:], ps[:, 0:T_], AF.Exp, [ps, st], [p, st], bias=st[:, 0:1], scale=0.125,
                              accum=st[:, 1:2])
                        ps2 = K.psum()
                        for m in range(nt):
                            K.tr(ps2[:, m * 128:(m + 1) * 128], p[:, m * 128:(m + 1) * 128], self.ident[:],
                                 [p, self.ident], [ps2])
                        pT = K.rot("npT", [128, nt, 128], BF16, 2)
                        K.cp(K.evq(), pT[:], ps2[:, 0:T_].rearrange("p (m t) -> p m t", m=nt), [ps2], [pT])
                        acc = K.psum()
                        for m in range(nt):
                            K.mm(acc[:, 0:64], pT[:, m, :], v[:, m, hs], m == 0, m == nt - 1, [pT, v], [acc])
                        K.S.op("dve", lambda e, st=st: e.reciprocal(st[:, 1:2], st[:, 1:2]), [st.b], [st.b])
                        K.ts("dve", nout[:, n, hs], acc[:, 0:64], st[:, 1:2], None, ALU.mult, None, [acc, st], [nout])
            else:
                Toe = K.sb([128, 4, 960], F32, "naToe")
                self.na_tables(l, Toe)
                kc = K.sb([128, 2, 4, 64], F32, "nakc")
                vc = K.sb([128, 2, 4, 64], BF16, "navc")
                for c in range(2):
                    K.dma("sp", kc[:, c], I["cak"][l, :, c * 128:(c + 1) * 128, :].rearrange("h p d -> p h d"),
                          [I["cak"]], [kc])
                    K.dma("pool", vc[:, c], I["cav"][l, :, c * 128:(c + 1) * 128, :].rearrange("h p d -> p h d"),
                          [I["cav"]], [vc])
                kcT = K.sb([64, 4, 256], BF16, "nakcT")
                for c in range(2):
                    ps = K.psum()
                    for h in range(4):
                        K.tr(ps[0:64, h * 128:(h + 1) * 128], kc[:, c, h, :], self.ident[:], [kc, self.ident], [ps])
                    K.cp(K.evq(), kcT[:, :, c * 128:(c + 1) * 128], ps[0:64, :].rearrange("p (h t) -> p h t", h=4),
                         [ps], [kcT])
                for h in range(4):
                    hs = slice(h * 64, (h + 1) * 64)
                    for n in range(nt):
                        rs_ = [min(max(2 * n + hf - 4, 0), 8) for hf in range(2)]
                        kt0 = rs_[0] // 2
                        ntl = (rs_[1] + 7) // 2 - kt0 + 1
                        LW = ntl * 128
                        psA = K.psum()
                        psB = K.psum()
                        wa = min(LW, 512)
                        K.mm(psA[:, 0:wa], qT[:, h, n * 128:(n + 1) * 128], kT[:, h, kt0 * 128:kt0 * 128 + wa], True, True,
                             [qT, kT], [psA])
                        if LW > 512:
                            K.mm(psB[:, 0:LW - 512], qT[:, h, n * 128:(n + 1) * 128],
                                 kT[:, h, kt0 * 128 + 512:kt0 * 128 + LW], True, True, [qT, kT], [psB])
                        K.mm(psB[:, 128:384], qT[:, h, n * 128:(n + 1) * 128], kcT[:, h, :], True, True, [qT, kcT], [psB])
                        scb = K.rot("nscb", [128, 640 + 256], F32, 2)
                        K.memset("pool", scb[:, 0:LW], NEG, [scb])
                        for hf in range(2):
                            r = 2 * n + hf
                            a = (rs_[hf] - kt0 * 2) * 64
                            b = a + 512
                            tof = (rs_[hf] - r + 7) * 64
                            pp = slice(hf * 64, (hf + 1) * 64)
                            if a < 512:
                                e_ = min(b, 512)
                                K.stt("dve", scb[pp, a:e_], psA[pp, a:e_], 0.125, Toe[pp, h, tof:tof + (e_ - a)],
                                      ALU.mult, ALU.add, [psA, Toe], [scb])
                            if b > 512:
                                s_ = max(a, 512)
                                K.stt("dve", scb[pp, s_:b], psB[pp, s_ - 512:b - 512], 0.125,
                                      Toe[pp, h, tof + (s_ - a):tof + 512], ALU.mult, ALU.add, [psB, Toe], [scb])
                        K.ts("dve", scb[:, LW:LW + 256], psB[:, 128:384], 0.125, None, ALU.mult, None, [psB], [scb])
                        st = K.rot("nst", [128, 2], F32, 3)
                        K.red(st[:, 0:1], scb[:, 0:LW + 256], ALU.max, [scb], [st])
                        K.ts("dve", st[:, 0:1], st[:, 0:1], -1.0, None, ALU.mult, None, [st], [st])
                        K.act(scb[:, 0:LW + 256], scb[:, 0:LW + 256], AF.Exp, [scb, st], [scb, st], bias=st[:, 0:1],
                              accum=st[:, 1:2])
                        nb = ntl + 2
                        pT = K.rot("npT", [128, 7, 128], BF16, 2)
                        for g0 in range(0, nb, 4):
                            g1 = min(nb, g0 + 4)
                            ps2 = K.psum()
                            for c in range(g0, g1):
                                K.tr(ps2[:, (c - g0) * 128:(c - g0 + 1) * 128], scb[:, c * 128:(c + 1) * 128],
                                     self.ident[:], [scb, self.ident], [ps2])
                            K.cp(K.evq(), pT[:, g0:g1, :],
                                 ps2[:, 0:(g1 - g0) * 128].rearrange("p (m t) -> p m t", t=128), [ps2], [pT])
                        acc = K.psum()
                        for c in range(nb):
                            rhs = v[:, kt0 + c, hs] if c < ntl else vc[:, c - ntl, h, :]
                            K.mm(acc[:, 0:64], pT[:, c, :], rhs, c == 0, c == nb - 1, [pT, v, vc], [acc])
                        K.S.op("dve", lambda e, st=st: e.reciprocal(st[:, 1:2], st[:, 1:2]), [st.b], [st.b])
                        K.ts("dve", nout[:, n, hs], acc[:, 0:64], st[:, 1:2], None, ALU.mult, None, [acc, st], [nout])
            self.store_branch(3, tok0, nt, nout)

    def phaseC(self, l):
        for (tok0, T_, latent, pb) in self.seqs():
            if self.debug and "only_s" in self.debug and not latent:
                continue
            if self.debug and "only_p" in self.debug and (latent or pb != 0):
                continue
            for br in ("mlstm", "gla", "rwkv", "na"):
                if self.debug and ("br_" + br) not in self.debug and any(d.startswith("br_") for d in self.debug):
                    continue
                if hasattr(self, br):
                    getattr(self, br)(l, tok0, T_, latent, pb)

    def layer_norm(self, y, which_ln, lnb_t, out):
        K = self.K
        st = K.rot("ln_st", [128, 2, 6], F32, 2)
        for hh in range(2):
            K.S.op("dve", lambda e, hh=hh, st=st: e.bn_stats(st[:, hh, :], y[:, hh * 512:(hh + 1) * 512]), [y.b], [st.b])
        mv = K.rot("ln_mv", [128, 2], F32, 2)
        K.S.op("dve", lambda e, st=st, mv=mv: e.bn_aggr(mv[:], st[:].rearrange("p a b -> p (a b)")), [st.b], [mv.b])
        K.ts("dve", mv[:, 1:2], mv[:, 1:2], LN_EPS, None, ALU.add, None, [mv], [mv])
        K.act(mv[:, 1:2], mv[:, 1:2], AF.Sqrt, [mv], [mv])
        K.S.op("dve", lambda e, mv=mv: e.reciprocal(mv[:, 1:2], mv[:, 1:2]), [mv.b], [mv.b])
        nb = K.rot("ln_nb", [128, 1], F32, 2)
        K.stt("dve", nb[:], mv[:, 0:1], -1.0, mv[:, 1:2], ALU.mult, ALU.mult, [mv], [nb])
        K.act(y[:], y[:], AF.Identity, [y, mv, nb], [y], bias=nb[:, 0:1], scale=mv[:, 1:2])
        K.tt("dve", y[:], y[:], lnb_t[:, 0, :], ALU.mult, [y, lnb_t], [y])
        K.tt("dve", out[:], y[:], lnb_t[:, 1, :], ALU.add, [y, lnb_t], [out])

    def bcast_rows(self, dst_ap, dst_t, src_row_ap, src_t, width):
        self.K.dma("sp", dst_ap, bc(src_row_ap, [128, width]), [src_t], [dst_t])

    def phaseD(self, l, xsrc):
        K = self.K
        I = self.I
        with K.scope():
            wbr = K.sb([128, 4, 2, D], BF16, "wbr")
            for z in range(4):
                K.dma("pool", wbr[:, z], I["w_br"][l, z].rearrange("(k p) d -> p k d", p=128), [I["w_br"]], [wbr])
            wout = K.sb([128, 8, D], BF16, "wout")
            K.dma("pool", wout[:], I["w_out"][l].rearrange("(k p) d -> p k d", p=128), [I["w_out"]], [wout])
            wr = K.sb([128, 8, NEXP], F32, "wr")
            K.dma("sp", wr[:], I["w_router"][l].rearrange("(k p) e -> p k e", p=128), [I["w_router"]], [wr])
            g1b = K.sb([128, 2, D], F32, "g1b")
            sc2b = K.sb([128, 2, D], F32, "sc2b")
            sh2b = K.sb([128, 2, D], F32, "sh2b")
            for w_ in range(2):
                self.bcast_rows(g1b[:, w_, :], g1b, self.modrow[l, w_:w_ + 1, 2 * D:3 * D], self.modrow, D)
                self.bcast_rows(sh2b[:, w_, :], sh2b, self.modrow[l, w_:w_ + 1, 3 * D:4 * D], self.modrow, D)
                self.bcast_rows(sc2b[:, w_, :], sc2b, self.modrow[l, w_:w_ + 1, 4 * D:5 * D], self.modrow, D)
            K.ts("dve", sc2b[:], sc2b[:], 1.0, None, ALU.add, None, [sc2b], [sc2b])
            ln1 = K.sb([128, 2, D], F32, "ln1")
            self.bcast_rows(ln1[:, 0, :], ln1, I["ln_g"][l, 0:1, :], I["ln_g"], D)
            self.bcast_rows(ln1[:, 1, :], ln1, I["ln_b"][l, 0:1, :], I["ln_b"], D)
            W = (wbr, wout, wr, g1b, sc2b, sh2b, ln1)
            for i0_ in range(0, NT, 2):
                K.interleave([self._phaseD_tile(l, i0_, xsrc, W), self._phaseD_tile(l, i0_ + 1, xsrc, W)])

    def _phaseD_tile(self, l, i, xsrc, W):
        K = self.K
        wbr, wout, wr, g1b, sc2b, sh2b, ln1 = W
        MC = COLS['merge']
        px = f"_{i % 2}"
        wch = 0 if i < 8 else 1
        ts_ = slice(i * 128, (i + 1) * 128)
        gates = K.rot("dgates" + px, [128, 4 * D], BF16, 1)
        K.dma("pool", gates[:], self.proj[ts_, MC:MC + 4 * D], [self.proj], [gates])
        K.act(gates[:], gates[:], AF.Sigmoid, [gates], [gates])
        xt = K.rot("dxt" + px, [128, D], F32, 1)
        K.dma(K.dq(), xt[:], xsrc[ts_, :], [xsrc], [xt])
        msum = K.rot("dmsum" + px, [128, D], F32, 1)
        tmp = K.rot("dtmp" + px, [128, D], F32, 1)
        for z in range(4):
            for hh in range(2):
                cs_ = slice(hh * 512, (hh + 1) * 512)
                ps = K.psum()
                for kk in range(2):
                    K.mm(ps[:], self.brT[:, 2 * z + kk, ts_], wbr[:, z, kk, cs_], kk == 0, kk == 1,
                         [self.brT, wbr], [ps])
                gsl = gates[:, z * D + hh * 512:z * D + (hh + 1) * 512]
                if z == 0:
                    K.tt("dve", msum[:, cs_], ps[:], gsl, ALU.mult, [ps, gates], [msum])
                else:
                    K.tt("dve", tmp[:, cs_], ps[:], gsl, ALU.mult, [ps, gates], [tmp])
                    K.tt("dve", msum[:, cs_], msum[:, cs_], tmp[:, cs_], ALU.add, [msum, tmp], [msum])
        yield
        msT = K.rot("dmsT" + px, [128, 8, 128], BF16, 1)
        for kg in range(2):
            ps = K.psum()
            for kk in range(4):
                k = kg * 4 + kk
                K.tr(ps[:, kk * 128:(kk + 1) * 128], msum[:, k * 128:(k + 1) * 128], self.ident[:],
                     [msum, self.ident], [ps])
            K.cp(K.evq(), msT[:, kg * 4:(kg + 1) * 4, :], ps[:].rearrange("p (k t) -> p k t", k=4), [ps], [msT])
        yield
        y = K.rot("dy" + px, [128, D], F32, 1)
        for hh in range(2):
            cs_ = slice(hh * 512, (hh + 1) * 512)
            ps = K.psum()
            for k in range(8):
                K.mm(ps[:], msT[:, k, :], wout[:, k, cs_], k == 0, k == 7, [msT, wout], [ps])
            K.tt("dve", y[:, cs_], ps[:], g1b[:, wch, cs_], ALU.mult, [ps, g1b], [y])
        K.stt("dve", y[:], xt[:], ALPHA, y[:], ALU.mult, ALU.add, [xt, y], [y])
        yield
        x1 = K.rot("dx1" + px, [128, D], F32, 1)
        self.layer_norm(y, 0, ln1, x1)
        K.dma("sp", self.x1res[ts_, :], x1[:], [x1], [self.x1res])
        h2 = K.rot("dh2" + px, [128, D], F32, 1)
        K.tt("dve", h2[:], x1[:], sc2b[:, wch, :], ALU.mult, [x1, sc2b], [h2])
        K.tt("dve", h2[:], h2[:], sh2b[:, wch, :], ALU.add, [h2, sh2b], [h2])
        h2b = K.rot("dh2b" + px, [128, D], BF16, 1)
        K.cp("act", h2b[:], h2[:], [h2], [h2b])
        K.dma("act", self.h2res[ts_, :], h2b[:], [h2b], [self.h2res])
        yield
        h2T = K.rot("dh2T" + px, [128, 8, 128], F32, 1)
        for kg in range(2):
            ps = K.psum()
            for kk in range(4):
                k = kg * 4 + kk
                K.tr(ps[:, kk * 128:(kk + 1) * 128], h2[:, k * 128:(k + 1) * 128], self.ident[:],
                     [h2, self.ident], [ps])
            K.cp(K.evq(), h2T[:, kg * 4:(kg + 1) * 4, :], ps[:].rearrange("p (k t) -> p k t", k=4), [ps], [h2T])
        yield
        ps = K.psum()
        for k in range(8):
            K.mm(ps[:, 0:NEXP], h2T[:, k, :], wr[:, k, :], k == 0, k == 7, [h2T, wr], [ps])
        st = K.rot("dst" + px, [128, 2], F32, 1)
        K.red(st[:, 0:1], ps[:, 0:NEXP], ALU.max, [ps], [st])
        K.ts("dve", st[:, 0:1], st[:, 0:1], -1.0, None, ALU.mult, None, [st], [st])
        ex = K.rot("dex" + px, [128, NEXP], F32, 1)
        K.act(ex[:], ps[:, 0:NEXP], AF.Exp, [ps, st], [ex, st], bias=st[:, 0:1], accum=st[:, 1:2])
        K.S.op("dve", lambda e, st=st: e.reciprocal(st[:, 1:2], st[:, 1:2]), [st.b], [st.b])
        K.ts("dve", self.affall[:, i, :], ex[:], st[:, 1:2], None, ALU.mult, None, [ex, st], [self.affall])

    def phaseE(self, l, xdst):
        K = self.K
        I = self.I
        sets = [(s * 256, 256, 32) for s in range(4)] + [(1024, 1024, 128)]
        with K.scope():
            slot = K.sb([16, NTOK], F32, "eslot")
            gatev = K.sb([16, NTOK], F32, "egate")
            slotT = K.sb([128, NT, NEXP], F32, "eslotT")
            with K.scope():
                affT = K.sb([16, NTOK], F32, "eaffT")
                work = K.sb([16, NTOK], F32, "ework")
                cum = K.sb([16, NTOK], F32, "ecum")
                one16 = K.sb([16, 1024], F32, "eone")
                K.memset("dve", one16[:], 1.0, [one16])
                for ig in range(4):
                    ps = K.psum()
                    for ii in range(4):
                        i = ig * 4 + ii
                        K.tr(ps[0:16, ii * 128:(ii + 1) * 128], self.affall[:, i, :], self.ident[:],
                             [self.affall, self.ident], [ps])
                    K.cp("dve", affT[:, ig * 512:(ig + 1) * 512], ps[0:16, :], [ps], [affT])
                K.cp("act", work[:], affT[:], [affT], [work])
                for (c0, T_, cap) in sets:
                    for it in range(cap // 8):
                        m8 = K.rot("em8", [16, 8], F32, 2)
                        K.S.op("dve", lambda e, m8=m8, c0=c0, T_=T_: e.max(m8[:], work[:, c0:c0 + T_]), [work.b], [m8.b])
                        K.S.op("dve", lambda e, m8=m8, c0=c0, T_=T_: e.match_replace(
                            work[:, c0:c0 + T_], m8[:], work[:, c0:c0 + T_], -1.0), [work.b, m8.b], [work.b])
                K.ts("dve", work[:], work[:], 0.0, None, ALU.is_lt, None, [work], [work])
                K.tt("dve", gatev[:], affT[:], work[:], ALU.mult, [affT, work], [gatev])
                for (c0, T_, cap) in sets:
                    K.S.op("dve", lambda e, c0=c0, T_=T_: e.tensor_tensor_scan(
                        cum[:, c0:c0 + T_], one16[:, 0:T_], work[:, c0:c0 + T_], 0.0, ALU.mult, ALU.add),
                        [one16.b, work.b], [cum.b])
                K.ts("dve", cum[:], cum[:], 999.0, None, ALU.add, None, [cum], [cum])
                K.tt("dve", cum[:], cum[:], work[:], ALU.mult, [cum, work], [cum])
                K.ts("dve", slot[:], cum[:], -1000.0, None, ALU.add, None, [cum], [slot])
                ps = K.psum()
                for i in range(NT):
                    K.tr(ps[:, i * 16:(i + 1) * 16], slot[:, i * 128:(i + 1) * 128], self.ident[0:16, 0:16],
                         [slot, self.ident], [ps])
                K.cp("dve", slotT[:], ps[:, 0:256].rearrange("p (i e) -> p i e", e=16), [ps], [slotT])
            with K.scope():
                h2tok = K.sb([128, NT, D], BF16, "eh2")
                for ig in range(4):
                    K.dma(K.dq(), h2tok[:, ig * 4:(ig + 1) * 4, :],
                          self.h2res[ig * 512:(ig + 1) * 512, :].rearrange("(i p) d -> p i d", p=128), [self.h2res], [h2tok])
                for e_ in range(NEXP):
                    PTs = K.rot("ePTs", [128, 8, 128], BF16, 2)
                    PTp = K.rot("ePTp", [128, 8, 32], BF16, 2)
                    K.tt("dve", PTs[:], bc(self.fidx[:].unsqueeze(1), [128, 8, 128]),
                         bc(slotT[:, 8:16, e_].unsqueeze(2), [128, 8, 128]), ALU.is_equal, [self.fidx, slotT], [PTs])
                    K.tt("dve", PTp[:], bc(self.fidx[:, 0:32].unsqueeze(1), [128, 8, 32]),
                         bc(slotT[:, 0:8, e_].unsqueeze(2), [128, 8, 32]), ALU.is_equal, [self.fidx, slotT], [PTp])
                    xeT = K.rot("exeT", [128, 8, 256], BF16, 2)
                    for kg in range(2):
                        ps = K.psum()
                        ps2 = K.psum()
                        for kk in range(4):
                            k = kg * 4 + kk
                            for i in range(8, 16):
                                K.mm(ps[:, kk * 128:(kk + 1) * 128], h2tok[:, i, k * 128:(k + 1) * 128], PTs[:, i - 8, :],
                                     i == 8, i == 15, [h2tok, PTs], [ps])
                            for s in range(4):
                                for ii in range(2):
                                    i = 2 * s + ii
                                    K.mm(ps2[:, kk * 128 + s * 32:kk * 128 + (s + 1) * 32],
                                         h2tok[:, i, k * 128:(k + 1) * 128], PTp[:, i, :], ii == 0, ii == 1,
                                         [h2tok, PTp], [ps2])
                        K.cp("act", xeT[:, kg * 4:(kg + 1) * 4, 0:128], ps[:].rearrange("p (k c) -> p k c", k=4),
                             [ps], [xeT])
                        K.cp("dve", xeT[:, kg * 4:(kg + 1) * 4, 128:256], ps2[:].rearrange("p (k c) -> p k c", k=4),
                             [ps2], [xeT])
                    hmT = K.rot("ehmT", [128, 16, 256], BF16, 2)
                    for fg in range(4):
                        wu = K.rot("ewu", [128, 8, 2, 512], BF16, 2)
                        for ab in range(2):
                            c0 = ab * FF + fg * 512
                            K.dma("pool", wu[:, :, ab, :],
                                  I["w_up"][l, e_, :, c0:c0 + 512].rearrange("(k p) f -> p k f", p=128), [I["w_up"]], [wu])
                        for f4 in range(4):
                            fc = fg * 4 + f4
                            psa = K.psum()
                            psb = K.psum()
                            for k in range(8):
                                K.mm(psa[:, 0:256], wu[:, k, 0, f4 * 128:(f4 + 1) * 128], xeT[:, k, :], k == 0, k == 7,
                                     [wu, xeT], [psa])
                            for k in range(8):
                                K.mm(psb[:, 0:256], wu[:, k, 1, f4 * 128:(f4 + 1) * 128], xeT[:, k, :], k == 0, k == 7,
                                     [wu, xeT], [psb])
                            sa = K.rot("esa", [128, 256], F32, 2)
                            K.act(sa[:], psa[:, 0:256], AF.Silu, [psa], [sa])
                            K.tt("dve", hmT[:, fc, :], sa[:], psb[:, 0:256], ALU.mult, [sa, psb], [hmT])
                    yeb = K.rot("eyeb", [128, 2, D], BF16, 2)
                    for hh in range(2):
                        wd = K.rot("ewd", [128, 16, 512], BF16, 2)
                        K.dma("pool", wd[:], I["w_down"][l, e_, :, hh * 512:(hh + 1) * 512].rearrange("(k p) d -> p k d", p=128),
                              [I["w_down"]], [wd])
                        for grp in range(2):
                            ps = K.psum()
                            for fc in range(16):
                                K.mm(ps[:], hmT[:, fc, grp * 128:(grp + 1) * 128], wd[:, fc, :], fc == 0, fc == 15,
                                     [hmT, wd], [ps])
                            K.cp(K.evq(), yeb[:, grp, hh * 512:(hh + 1) * 512], ps[:], [ps], [yeb])
                    for grp in range(2):
                        K.dma(K.dq(), self.ye[grp, e_], yeb[:, grp, :], [yeb], [self.ye])
            with K.scope():
                g2b = K.sb([128, 2, D], F32, "g2b")
                for w_ in range(2):
                    self.bcast_rows(g2b[:, w_, :], g2b, self.modrow[l, w_:w_ + 1, 5 * D:6 * D], self.modrow, D)
                ln2 = K.sb([128, 2, D], F32, "ln2")
                self.bcast_rows(ln2[:, 0, :], ln2, I["ln_g"][l, 1:2, :], I["ln_g"], D)
                self.bcast_rows(ln2[:, 1, :], ln2, I["ln_b"][l, 1:2, :], I["ln_b"], D)
                cidx = K.sb([128, 5], F32, "ecidx")
                for s in range(4):
                    K.ts("dve", cidx[:, s:s + 1], self.pidx[:], -32.0 * s, None, ALU.add, None, [self.pidx], [cidx])
                K.cp("dve", cidx[:, 4:5], self.pidx[:], [self.pidx], [cidx])
                id16 = self.ident[0:16, 0:16]
                for grp in (1, 0):
                    yeg = K.rot("eyeg", [128, NEXP, D], BF16, 1)
                    for eg in range(4):
                        K.dma(K.dq(), yeg[:, eg * 4:(eg + 1) * 4, :],
                              self.ye[grp, eg * 4:(eg + 1) * 4].rearrange("e c d -> c e d"), [self.ye], [yeg])
                    tiles = range(8, 16) if grp == 0 else range(0, 8)
                    for i in tiles:
                        wch = 0 if i < 8 else 1
                        ts_ = slice(i * 128, (i + 1) * 128)
                        ccol = 4 if i >= 8 else i // 2
                        Rs = K.rot("eRs", [16, NEXP, 128], F32, 2)
                        Rg = K.rot("eRg", [16, NEXP, 128], F32, 2)
                        K.tt("dve", Rs[:], bc(slot[:, ts_].unsqueeze(1), [16, NEXP, 128]),
                             bc(id16.unsqueeze(2), [16, NEXP, 128]), ALU.mult, [slot, self.ident], [Rs])
                        K.tt("dve", Rg[:], bc(gatev[:, ts_].unsqueeze(1), [16, NEXP, 128]),
                             bc(id16.unsqueeze(2), [16, NEXP, 128]), ALU.mult, [gatev, self.ident], [Rg])
                        PTg = K.rot("ePTg", [128, NEXP, 128], BF16, 2)
                        for eg in range(4):
                            pss = K.psum()
                            psg = K.psum()
                            K.mm(pss[:], self.ones[0:16, :], Rs[:, eg * 4:(eg + 1) * 4, :].rearrange("p e t -> p (e t)"),
                                 True, True, [self.ones, Rs], [pss])
                            K.mm(psg[:], self.ones[0:16, :], Rg[:, eg * 4:(eg + 1) * 4, :].rearrange("p e t -> p (e t)"),
                                 True, True, [self.ones, Rg], [psg])
                            gsb = K.rot("egsb", [128, 512], F32, 2)
                            K.cp("act", gsb[:], psg[:], [psg], [gsb])
                            K.stt("dve", PTg[:, eg * 4:(eg + 1) * 4, :].rearrange("p e t -> p (e t)"), pss[:],
                                  cidx[:, ccol:ccol + 1], gsb[:], ALU.is_equal, ALU.mult, [pss, cidx, gsb], [PTg])
                        x1 = K.rot("ex1", [128, D], F32, 2)
                        K.dma(K.dq(), x1[:], self.x1res[ts_, :], [self.x1res], [x1])
                        y = K.rot("ey", [128, D], F32, 2)
                        for hh in range(2):
                            cs_ = slice(hh * 512, (hh + 1) * 512)
                            ps = K.psum()
                            for e_ in range(NEXP):
                                K.mm(ps[:], PTg[:, e_, :], yeg[:, e_, cs_], e_ == 0, e_ == NEXP - 1, [PTg, yeg], [ps])
                            K.tt("dve", y[:, cs_], ps[:], g2b[:, wch, cs_], ALU.mult, [ps, g2b], [y])
                        K.stt("dve", y[:], x1[:], ALPHA, y[:], ALU.mult, ALU.add, [x1, y], [y])
                        xo = K.rot("exo", [128, D], F32, 2)
                        self.layer_norm(y, 1, ln2, xo)
                        K.dma(K.dq(), xdst[ts_, :], xo[:], [xo], [xdst])

    def build(self):
        self.phase0_mods()
        if self.debug and "stop0" in self.debug:
            return self.finish()
        for l in range(DEPTH):
            xsrc = self.I["xin"] if l == 0 else self.xres[(l - 1) % 2]
            if not (self.debug and "projin" in self.debug):
                with self.K.scope():
                    self.hT = self.K.sb([128, 8, NTOK], BF16, "hT")
                    self.phaseA(l, xsrc)
                    self.phaseB(l)
            if self.debug and "stopB" in self.debug:
                return self.finish()
            if l == 0:
                self.affall = self.K.sb([128, NT, NEXP], F32, "affall")
            with self.K.scope():
                self.brT = self.K.sb([128, 8, NTOK], BF16, "brT")
                self.phaseC(l)
                if self.debug and "stopC" in self.debug:
                    return self.finish()
                self.phaseD(l, xsrc)
                if self.debug and "stopD" in self.debug:
                    return self.finish()
            xdst = self.O["y"] if l == DEPTH - 1 else self.xres[l % 2]
            self.phaseE(l, xdst)
            if self.debug and "stopE" in self.debug:
                return self.finish()
        return self.finish()

    def finish(self):
        self.K.S.emit()
        return self.nc


IN_SHAPES = {
    "xin": [NTOK, D], "cvec": [2, D],
    "sC": [DEPTH, 2, H, HD, HD], "sn": [DEPTH, 2, H, HD], "sm": [DEPTH, 8],
    "sg": [DEPTH, 2, H, HD, HD], "sr": [DEPTH, 2, H, HD, HD],
    "cak": [DEPTH, H, 256, HD], "cav": [DEPTH, H, 256, HD],
    "w_ada": [DEPTH, D, 6 * D], "b_ada": [DEPTH, 6 * D], "w_in": [DEPTH, D, NIN],
    "b_ig": [DEPTH, 8], "b_fg": [DEPTH, 8], "w_gla_a2": [DEPTH, 2, 16, MIXW], "b_gla_a": [DEPTH, 2, MIXW],
    "shift_rwkv": [DEPTH, 3, 768], "w0_rwkv": [DEPTH, 2, MIXW], "w_w2": [DEPTH, 2, 32, MIXW],
    "a0_rwkv": [DEPTH, 2, MIXW], "w_a2": [DEPTH, 2, 32, MIXW], "w_g2": [DEPTH, 64, MIXW],
    "k_k": [DEPTH, MIXW], "k_a": [DEPTH, MIXW], "r_k": [DEPTH, MIXW], "rpb": [DEPTH, 60, 31],
    "w_br": [DEPTH, 4, MIXW, D], "w_out": [DEPTH, D, D], "ln_g": [DEPTH, 2, D], "ln_b": [DEPTH, 2, D],
    "w_router": [DEPTH, D, NEXP], "w_up": [DEPTH, NEXP, D, 2 * FF], "w_down": [DEPTH, NEXP, FF, D],
}
OUT_SHAPES = {
    "y": [NTOK, D], "nC": [4, DEPTH, 8, HD, HD], "nn": [4, DEPTH, 8, HD], "nm": [4, DEPTH, 8],
    "ng": [4, DEPTH, 8, HD, HD], "nr": [4, DEPTH, 8, HD, HD],
    "nak": [4, DEPTH, H, 256, HD], "nav": [4, DEPTH, H, 256, HD],
}


def make_in_maps(inputs):
    f = lambda a: np.ascontiguousarray(np.asarray(a, dtype=np.float32))
    shared = {}
    for name in ("w_ada", "b_ada", "w_in", "w_gla_a2", "b_gla_a", "shift_rwkv", "w0_rwkv", "w_w2", "a0_rwkv",
                 "w_a2", "w_g2", "k_k", "k_a", "r_k", "w_br", "w_out", "ln_g", "ln_b", "w_router", "w_up", "w_down"):
        shared[name] = f(inputs[name])
    shared["b_ig"] = f(inputs["b_ig"]).reshape(DEPTH, 8)
    shared["b_fg"] = f(inputs["b_fg"]).reshape(DEPTH, 8)
    shared["rpb"] = f(inputs["rpb"]).reshape(DEPTH, 60, 31)
    xp = f(inputs["x_prompt"])
    xs = f(inputs["x_sample"])
    maps = []
    for i in range(NCORES):
        m = dict(shared)
        m["xin"] = np.ascontiguousarray(np.concatenate([xp[4 * i:4 * i + 4].reshape(1024, D), xs[i]], axis=0))
        m["cvec"] = np.ascontiguousarray(np.stack([f(inputs["c_ctx"]), f(inputs["c"])[i]], axis=0))
        m["sC"] = f(inputs["state_mlstm_C"][i])
        m["sn"] = f(inputs["state_mlstm_n"][i])
        m["sm"] = f(inputs["state_mlstm_m"][i]).reshape(DEPTH, 8)
        m["sg"] = f(inputs["state_gla"][i])
        m["sr"] = f(inputs["state_rwkv"][i])
        m["cak"] = f(inputs["cache_na_k"][i])
        m["cav"] = f(inputs["cache_na_v"][i])
        maps.append(m)
    return maps


def kernel(**inputs):
    prog = Prog()
    nc = prog.build()
    maps = make_in_maps(inputs)
    res = run_bass_kernel_spmd(nc, maps, core_ids=list(range(NCORES)))
    R = res.results
    y = np.stack([r["y"] for r in R], 0)
    y_prompt = y[:, :1024].reshape(32, 256, D)
    y_sample = y[:, 1024:].reshape(8, 1024, D)
    cat = lambda k: np.concatenate([r[k] for r in R], 0)
    nC = cat("nC").reshape(32, DEPTH, 2, H, HD, HD)
    nn = cat("nn").reshape(32, DEPTH, 2, H, HD)
    nm = cat("nm").reshape(32, DEPTH, 2, H)
    ng = cat("ng").reshape(32, DEPTH, 2, H, HD, HD)
    nr = cat("nr").reshape(32, DEPTH, 2, H, HD, HD)
    nak = cat("nak")
    nav = cat("nav")
    return tuple(np.ascontiguousarray(a, dtype=np.float32) for a in (y_prompt, y_sample, nC, nn, nm, ng, nr, nak, nav))
```

```python
import math
from contextlib import ExitStack, contextmanager
import numpy as np
import concourse.bass as bass
import concourse.mybir as mybir
from concourse.bass_utils import run_bass_kernel_spmd

F32 = mybir.dt.float32
BF16 = mybir.dt.bfloat16
AF = mybir.ActivationFunctionType
ALU = mybir.AluOpType
AX = mybir.AxisListType

ENGS = ("pe", "act", "dve", "pool", "sp")
N_DMA_SLOTS = 12

NCORES = 8
DEPTH = 2
D = 1024
NT = 16
NTOK = 2048
NIN = 7920
H = 4
HD = 64
MIXW = 256
NEXP = 16
FF = 2048
LN_EPS = 1e-5
ALPHA = (2 * DEPTH) ** 0.25
NEG = -30000.0

COLS = {}
_off = 0
for _n, _w in (('m_q', 256), ('m_k', 256), ('m_v', 256), ('m_o', 256), ('m_i', 8), ('m_f', 8),
               ('g_q', 256), ('g_k', 256), ('g_v', 256), ('g_g', 256), ('g_a', 32),
               ('r_rkv', 768), ('r_w', 64), ('r_a', 64), ('r_g', 64), ('n_qkv', 768), ('merge', 4096)):
    COLS[_n] = _off
    _off += _w
assert _off == NIN


class Buf:
    __slots__ = ("name", "last_w", "readers")

    def __init__(self, name=""):
        self.name = name
        self.last_w = None
        self.readers = []


class Sched:
    def __init__(self, nc):
        self.nc = nc
        self.q = {e: [] for e in ENGS}
        self.cnt = {e: 0 for e in ENGS}
        self.sems = {}
        self.seen = {e: {} for e in ENGS}
        self.pending = {e: [] for e in ENGS}
        self.dma_slot = {e: 0 for e in ENGS}
        self.dma_cnt = {}
        for e in ENGS:
            self.sems[("c", e)] = nc.alloc_semaphore(name=f"c_{e}")
        for e in ("sp", "act", "pool"):
            for s in range(N_DMA_SLOTS):
                self.sems[("d", e, s)] = nc.alloc_semaphore(name=f"d_{e}_{s}")
                self.dma_cnt[(e, s)] = 0
        self.n_ops = 0

    def _collect(self, eng, reads, writes, extra):
        need = {}

        def add(tok):
            if tok is None:
                return
            sid, val = tok
            if need.get(sid, 0) < val:
                need[sid] = val
        for b in reads:
            add(b.last_w)
        for b in writes:
            add(b.last_w)
            for t in b.readers:
                add(t)
        for t in extra:
            add(t)
        for t in self.pending[eng]:
            add(t)
        self.pending[eng] = []
        waits = []
        seen = self.seen[eng]
        for sid, val in need.items():
            if sid == ("c", "pe") and eng == "pe":
                continue
            if seen.get(sid, 0) >= val:
                continue
            seen[sid] = val
            waits.append((sid, val))
        return waits

    def _commit(self, tok, reads, writes):
        for b in reads:
            b.readers.append(tok)
            if len(b.readers) > 64:
                mx = {}
                for sid, val in b.readers:
                    if mx.get(sid, 0) < val:
                        mx[sid] = val
                b.readers = list(mx.items())
        for b in writes:
            b.last_w = tok
            b.readers = []

    def op(self, eng, fn, reads=(), writes=()):
        waits = self._collect(eng, reads, writes, ())
        self.cnt[eng] += 1
        tok = (("c", eng), self.cnt[eng])
        self.q[eng].append((waits, fn, ("c", eng), 1))
        self._commit(tok, reads, writes)
        self.n_ops += 1
        return tok

    def dma(self, eng, out_ap, in_ap, reads=(), writes=(), **kw):
        slot = self.dma_slot[eng]
        self.dma_slot[eng] = (slot + 1) % N_DMA_SLOTS
        sid = ("d", eng, slot)
        prev = self.dma_cnt[(eng, slot)]
        extra = [(sid, 16 * prev)] if prev > 0 else []
        waits = self._collect(eng, reads, writes, extra)
        self.dma_cnt[(eng, slot)] = prev + 1
        tok = (sid, 16 * (prev + 1))

        def fn(e, out_ap=out_ap, in_ap=in_ap, kw=kw):
            return e.dma_start(out=out_ap, in_=in_ap, **kw)
        self.q[eng].append((waits, fn, sid, 16))
        self._commit(tok, reads, writes)
        self.n_ops += 1
        return tok

    def all_tokens(self):
        toks = []
        for e in ENGS:
            if self.cnt[e] > 0:
                toks.append((("c", e), self.cnt[e]))
        for (e, s), c in self.dma_cnt.items():
            if c > 0:
                toks.append((("d", e, s), 16 * c))
        return toks

    def barrier(self):
        toks = self.all_tokens()
        for e in ENGS:
            self.pending[e] = list(toks)

    def emit(self):
        nc = self.nc
        final_waits = self.all_tokens()
        engmap = {"pe": "tensor", "act": "scalar", "dve": "vector", "pool": "gpsimd", "sp": "sync"}
        with nc.Block() as block:
            for e in ENGS:
                items = self.q[e]
                fw = final_waits if e == "sp" else []
                sems = self.sems

                def body(engobj, items=items, fw=fw, sems=sems):
                    for waits, fn, sid, inc in items:
                        for (wsid, val) in waits:
                            engobj.wait_ge(sems[wsid], val)
                        ins = fn(engobj)
                        ins.then_inc(sems[sid], inc)
                    for (wsid, val) in fw:
                        engobj.wait_ge(sems[wsid], val)
                getattr(block, engmap[e])(body)


class T:
    __slots__ = ("h", "b", "psum")

    def __init__(self, h, name="", psum=False):
        self.h = h
        self.b = Buf(name)
        self.psum = psum


    def __getitem__(self, k):
        return self.h[k]


def _rw(r, w):
    rb = [t.b for t in r if not t.psum]
    wb = [t.b for t in w] + [t.b for t in r if t.psum]
    return rb, wb


class KB:
    def __init__(self, nc):
        self.nc = nc
        self.S = Sched(nc)
        self.stacks = []
        self.rots = []
        self.uid = 0
        self.ps = [T(nc.alloc_psum_tensor(f"psb{i}", [128, 512], F32), f"ps{i}", psum=True) for i in range(8)]
        self.ps_i = 0
        self.ev_i = 0
        self.dq_i = 0

    @contextmanager
    def scope(self):
        st = ExitStack()
        self.stacks.append(st)
        self.rots.append({})
        try:
            yield
        finally:
            self.S.barrier()
            self.stacks.pop()
            self.rots.pop()
            st.close()

    def sb(self, shape, dtype=F32, name=None):
        self.uid += 1
        nm = f"{name or 't'}_{self.uid}"
        if self.stacks:
            h = self.stacks[-1].enter_context(self.nc.sbuf_tensor(nm, list(shape), dtype))
        else:
            h = self.nc.alloc_sbuf_tensor(nm, list(shape), dtype)
        return T(h, nm)

    def rot(self, key, shape, dtype=F32, n=2):
        d = self.rots[-1]
        if key not in d:
            d[key] = [[self.sb(shape, dtype, key) for _ in range(n)], 0]
        lst, i = d[key]
        d[key][1] = (i + 1) % n
        return lst[i]

    def psum(self, hold=False, nbanks=6):
        held = getattr(self, "held", None)
        if held is None:
            held = self.held = set()
        for _ in range(8):
            i = self.ps_i
            self.ps_i = (self.ps_i + 1) % nbanks
            if i not in held:
                if hold:
                    held.add(i)
                return self.ps[i]
        raise RuntimeError("no free PSUM bank")

    def psfree(self, t):
        self.held.discard(self.ps.index(t))

    @staticmethod
    def interleave(gens):
        gens = list(gens)
        while gens:
            for g in list(gens):
                try:
                    next(g)
                except StopIteration:
                    gens.remove(g)

    def psacc(self):
        self.acc_i = 1 - getattr(self, "acc_i", 0)
        return self.ps[6 + self.acc_i]

    def evq(self):
        self.ev_i += 1
        return "act" if self.ev_i % 2 else "dve"

    def dq(self):
        self.dq_i += 1
        return "sp" if self.dq_i % 2 else "act"

    def mm(self, out, lhsT, rhs, start, stop, r, w):
        self.S.op("pe", lambda e: e.matmul(out, lhsT=lhsT, rhs=rhs, start=start, stop=stop),
                  *_rw(r, w))

    def tr(self, out, in_, ident, r, w):
        self.S.op("pe", lambda e: e.transpose(out, in_, ident), *_rw(r, w))

    def act(self, out, in_, func, r, w, bias=None, scale=None, accum=None):
        kw = {}
        if bias is not None:
            kw["bias"] = bias
        if scale is not None:
            kw["scale"] = scale
        if accum is not None:
            kw["accum_out"] = accum
        self.S.op("act", lambda e: e.activation(out, in_, func, **kw), *_rw(r, w))

    def tt(self, eng, out, a, b, op, r, w):
        self.S.op(eng, lambda e: e.tensor_tensor(out, a, b, op), *_rw(r, w))

    def ts(self, eng, out, a, s1, s2, op0, op1, r, w):
        if op1 is None:
            self.S.op(eng, lambda e: e.tensor_scalar(out, a, s1, None, op0), *_rw(r, w))
        else:
            self.S.op(eng, lambda e: e.tensor_scalar(out, a, s1, s2, op0, op1), *_rw(r, w))

    def stt(self, eng, out, in0, scalar, in1, op0, op1, r, w):
        self.S.op(eng, lambda e: e.scalar_tensor_tensor(out, in0, scalar, in1, op0, op1),
                  *_rw(r, w))

    def cp(self, eng, out, in_, r, w):
        if eng == "act":
            self.S.op("act", lambda e: e.copy(out, in_), *_rw(r, w))
        else:
            self.S.op(eng, lambda e: e.tensor_copy(out, in_), *_rw(r, w))

    def red(self, out, in_, op, r, w, axis=AX.X):
        self.S.op("dve", lambda e: e.tensor_reduce(out, in_, axis, op), *_rw(r, w))

    def memset(self, eng, ap, val, w):
        self.S.op(eng, lambda e: e.memset(ap, val), [], [t.b for t in w])

    def dma(self, eng, out, in_, r, w, **kw):
        self.S.dma(eng, out, in_, [t.b for t in r], [t.b for t in w], **kw)


def bc(ap, shape):
    return ap.to_broadcast(list(shape))


class Prog:
    def __init__(self, debug=None):
        self.debug = debug
        nc = bass.Bass("TRN2", target_bir_lowering=False)
        self.nc = nc
        self.K = KB(nc)

        def inp(name, shape):
            return T(nc.dram_tensor(name, list(shape), F32, kind="ExternalInput"), name)

        def outp(name, shape):
            return T(nc.dram_tensor(name, list(shape), F32, kind="ExternalOutput"), name)

        def scr(name, shape, dtype=F32):
            kind = "ExternalOutput" if (debug and name in debug) else "Internal"
            return T(nc.dram_tensor(name, list(shape), dtype, kind=kind), name)
        self.I = {}
        for name, shape in IN_SHAPES.items():
            self.I[name] = inp(name, shape)
        self.O = {}
        for name, shape in OUT_SHAPES.items():
            self.O[name] = outp(name, shape)
        self.modrow = scr("modrow", [DEPTH, 2, 6 * D])
        if debug and "projin" in debug:
            self.proj = inp("proj", [NTOK, NIN])
        else:
            self.proj = scr("proj", [NTOK, NIN])
        self.brdbg = scr("brdbg", [4, NTOK, MIXW]) if (debug and "brdbg" in debug) else None
        self.xres = [scr("xres0", [NTOK, D]), scr("xres1", [NTOK, D])]
        self.x1res = scr("x1res", [NTOK, D])
        self.rpbpad = scr("rpbpad", [DEPTH, 60, 128])
        self.ye = scr("ye", [2, NEXP, 128, D], BF16)
        self.h2res = scr("h2res", [NTOK, D], BF16)
        self.consts()

    def consts(self):
        K = self.K
        io = K.sb([128, 128], F32, "io")
        K.S.op("pool", lambda e: e.iota(io[:], [[1, 128]], base=0, channel_multiplier=-1,
                                         allow_small_or_imprecise_dtypes=True), [], [io.b])
        self.ident = K.sb([128, 128], F32, "ident")
        self.U = K.sb([128, 128], F32, "U")
        self.Lo = K.sb([128, 128], F32, "Lo")
        self.Us = K.sb([128, 128], F32, "Us")
        self.Ls = K.sb([128, 128], F32, "Ls")
        self.ones = K.sb([128, 128], F32, "ones")
        for t, op in ((self.ident, ALU.is_equal), (self.U, ALU.is_ge), (self.Lo, ALU.is_le),
                      (self.Us, ALU.is_gt), (self.Ls, ALU.is_lt)):
            K.S.op("dve", lambda e, t=t, op=op: e.tensor_single_scalar(t[:], io[:], 0.0, op), [io.b], [t.b])
        K.memset("dve", self.ones[:], 1.0, [self.ones])
        self.pidx = K.sb([128, 1], F32, "pidx")
        K.S.op("pool", lambda e: e.iota(self.pidx[:], [[0, 1]], base=0, channel_multiplier=1,
                                         allow_small_or_imprecise_dtypes=True), [], [self.pidx.b])
        self.fidx = K.sb([128, 128], F32, "fidx")
        K.S.op("pool", lambda e: e.iota(self.fidx[:], [[1, 128]], base=0, channel_multiplier=0,
                                         allow_small_or_imprecise_dtypes=True), [], [self.fidx.b])
        self.sel8 = K.sb([8, 8, 128], F32, "sel8")
        K.cp("dve", self.sel8[:], bc(self.ident[0:8, 0:8].unsqueeze(2), [8, 8, 128]), [self.ident], [self.sel8])
        self.modT = K.sb([128, DEPTH, 2, 48], F32, "modT")
        self.rope_tables()

    def transpose_block(self, dst_ap, dst_t, src_ap, src_t, pin, fin, eng=None):
        K = self.K
        ps = K.psum()
        K.tr(ps[0:fin, 0:pin], src_ap, self.ident[0:pin, 0:pin], [src_t, self.ident], [ps])
        K.cp(eng or K.evq(), dst_ap, ps[0:fin, 0:pin], [ps], [dst_t])

    def phase0_mods(self):
        K = self.K
        I = self.I
        with K.scope():
            cv = K.sb([16, 128], F32, "cv")
            for wch in range(2):
                K.dma("sp", cv[wch:16:2, :], I["cvec"][wch].rearrange("(k p) -> k p", p=128), [I["cvec"]], [cv])
            sg = K.sb([16, 128], F32, "sg")
            K.act(sg[:], cv[:], AF.Sigmoid, [cv], [sg])
            K.tt("dve", cv[:], cv[:], sg[:], ALU.mult, [cv, sg], [cv])
            condT = K.sb([128, 16], F32, "condT")
            self.transpose_block(condT[:], condT, cv[:], cv, 16, 128)
            mrow = K.sb([2, 6 * D], F32, "mrow")
            bada = K.sb([2, 6 * D], F32, "bada")
            for l in range(DEPTH):
                K.dma("sp", bada[:], bc(I["b_ada"][l:l + 1, :], [2, 6 * D]), [I["b_ada"]], [bada])
                for cb in range(12):
                    wt = K.rot("wada", [128, 8, 512], F32, 2)
                    K.dma(K.dq(), wt[:], I["w_ada"][l, :, cb * 512:(cb + 1) * 512].rearrange("(k p) f -> p k f", p=128),
                          [I["w_ada"]], [wt])
                    ps = K.psum()
                    for k in range(8):
                        K.mm(ps[0:2, :], condT[:, 2 * k:2 * k + 2], wt[:, k, :], k == 0, k == 7, [condT, wt], [ps])
                    K.tt("dve", mrow[:, cb * 512:(cb + 1) * 512], ps[0:2, :], bada[:, cb * 512:(cb + 1) * 512],
                         ALU.add, [ps, bada], [mrow])
                K.dma("sp", self.modrow[l], mrow[:], [mrow], [self.modrow])
                ps = K.psum()
                for c in range(48):
                    K.tr(ps[:, 2 * c:2 * c + 2], mrow[0:2, c * 128:(c + 1) * 128], self.ident[0:2, 0:2],
                         [mrow, self.ident], [ps])
                K.cp("dve", self.modT[:, l, :, :], ps[:, 0:96].rearrange("p (c w) -> p w c", w=2), [ps], [self.modT])

    def phaseA(self, l, xsrc):
        K = self.K
        with K.scope():
            sc1p = K.sb([128, 2, 8], F32, "sc1p")
            K.ts("dve", sc1p[:], self.modT[:, l, :, 8:16], 1.0, None, ALU.add, None, [self.modT], [sc1p])
            for i in range(NT):
                wch = 0 if i < 8 else 1
                xt = K.rot("xt", [128, D], F32, 3)
                K.dma(K.dq(), xt[:], xsrc[i * 128:(i + 1) * 128, :], [xsrc], [xt])
                for kg in range(2):
                    ps = K.psum()
                    for kk in range(4):
                        k = kg * 4 + kk
                        K.tr(ps[:, kk * 128:(kk + 1) * 128], xt[:, k * 128:(k + 1) * 128], self.ident[:],
                             [xt, self.ident], [ps])
                    for kk in range(4):
                        k = kg * 4 + kk
                        if kk % 2 == 0:
                            K.act(self.hT[:, k, i * 128:(i + 1) * 128], ps[:, kk * 128:(kk + 1) * 128], AF.Identity,
                                  [ps, sc1p, self.modT], [self.hT],
                                  bias=self.modT[:, l, wch, k:k + 1], scale=sc1p[:, wch, k:k + 1])
                        else:
                            K.ts("dve", self.hT[:, k, i * 128:(i + 1) * 128], ps[:, kk * 128:(kk + 1) * 128],
                                 sc1p[:, wch, k:k + 1], self.modT[:, l, wch, k:k + 1], ALU.mult, ALU.add,
                                 [ps, sc1p, self.modT], [self.hT])

    def phaseB(self, l):
        K = self.K
        I = self.I
        with K.scope():
            ncb = (NIN + 511) // 512
            for cb in range(ncb):
                c0 = cb * 512
                cw = min(512, NIN - c0)
                wt = K.rot("win", [128, 8, 512], BF16, 3)
                K.dma("pool", wt[:, :, 0:cw], I["w_in"][l, :, c0:c0 + cw].rearrange("(k p) f -> p k f", p=128),
                      [I["w_in"]], [wt])
                for i in range(NT):
                    ps = K.psum()
                    for k in range(8):
                        K.mm(ps[:, 0:cw], self.hT[:, k, i * 128:(i + 1) * 128], wt[:, k, 0:cw], k == 0, k == 7,
                             [self.hT, wt], [ps])
                    st = K.rot("pst", [128, 512], F32, 4)
                    K.cp(K.evq(), st[:, 0:cw], ps[:, 0:cw], [ps], [st])
                    K.dma(K.dq(), self.proj[i * 128:(i + 1) * 128, c0:c0 + cw], st[:, 0:cw], [st], [self.proj])

    def seqs(self):
        return [(s * 256, 256, False, s) for s in range(4)] + [(1024, 1024, True, None)]

    def rope_tables(self):
        K = self.K
        self.cosF = K.sb([128, 8, 256], F32, "cosF")
        self.sinF = K.sb([128, 8, 256], F32, "sinF")
        with K.scope():
            invf = K.sb([128, 16], F32, "invf")
            K.act(invf[:], self.fidx[:, 0:16], AF.Exp, [self.fidx], [invf], scale=-math.log(10000.0) / 16.0)
            ge64 = K.sb([128, 1], F32, "ge64")
            K.ts("dve", ge64[:], self.pidx[:], 64.0, None, ALU.is_ge, None, [self.pidx], [ge64])
            pcol = K.sb([128, 1], F32, "pcol")
            K.stt("dve", pcol[:], ge64[:], -64.0, self.pidx[:], ALU.mult, ALU.add, [ge64, self.pidx], [pcol])
            rown = K.sb([128, 8], F32, "rown")
            K.S.op("pool", lambda e: e.iota(rown[:], [[2, 8]], base=0, channel_multiplier=0,
                                             allow_small_or_imprecise_dtypes=True), [], [rown.b])
            K.ts("dve", rown[:], rown[:], ge64[:, 0:1], None, ALU.add, None, [rown, ge64], [rown])
            ang = K.sb([128, 8, 2, 16], F32, "ang")
            for n in range(8):
                K.ts("dve", ang[:, n, 0, :], invf[:], rown[:, n:n + 1], None, ALU.mult, None, [invf, rown], [ang])
                K.ts("dve", ang[:, n, 1, :], invf[:], pcol[:, 0:1], None, ALU.mult, None, [invf, pcol], [ang])
            us = K.sb([128, 8, 2, 16], F32, "us")
            uc = K.sb([128, 8, 2, 16], F32, "uc")
            K.cp("dve", us[:], ang[:], [ang], [us])
            K.ts("dve", uc[:], ang[:], 0.5 * math.pi, None, ALU.add, None, [ang], [uc])
            ki = K.sb([128, 8, 2, 16], mybir.dt.int32, "ki")
            kf = K.sb([128, 8, 2, 16], F32, "kf")
            for u in (us, uc):
                K.ts("dve", kf[:], u[:], 1.0 / (2 * math.pi), None, ALU.mult, None, [u], [kf])
                K.cp("dve", ki[:], kf[:], [kf], [ki])
                K.cp("dve", kf[:], ki[:], [ki], [kf])
                K.stt("dve", u[:], kf[:], -2 * math.pi, u[:], ALU.mult, ALU.add, [kf, u], [u])
                K.ts("dve", kf[:], u[:], math.pi, -2 * math.pi, ALU.is_gt, ALU.mult, [u], [kf])
                K.tt("dve", u[:], u[:], kf[:], ALU.add, [u, kf], [u])
                K.ts("dve", kf[:], u[:], -math.pi, 2 * math.pi, ALU.is_lt, ALU.mult, [u], [kf])
                K.tt("dve", u[:], u[:], kf[:], ALU.add, [u, kf], [u])
                K.ts("dve", u[:], u[:], math.pi, -math.pi, ALU.min, ALU.max, [u], [u])
                K.act(u[:], u[:], AF.Sin, [u], [u])
            cF = self.cosF[:].rearrange("p n (h a b f) -> p n h a b f", h=4, a=2, b=2)
            sF = self.sinF[:].rearrange("p n (h a b f) -> p n h a b f", h=4, a=2, b=2)
            for h in range(4):
                for b in range(2):
                    K.cp("dve", cF[:, :, h, :, b, :], uc[:], [uc], [self.cosF])
                    K.ts("dve", sF[:, :, h, :, b, :], us[:], (-1.0 if b == 0 else 1.0), None, ALU.mult, None,
                         [us], [self.sinF])

    def rope(self, X, nt):
        K = self.K
        tmp = K.rot("ropeA", [128, nt, 256], F32, 1)
        t2 = K.rot("ropeB", [128, nt, 256], F32, 1)
        K.tt("dve", tmp[:], X[:], self.cosF[:, 0:nt, :], ALU.mult, [X, self.cosF], [tmp])
        X5 = X[:].rearrange("p n (g b f) -> p (n g) b f", b=2, f=16)
        S5 = self.sinF[:, 0:nt, :].rearrange("p n (g b f) -> p (n g) b f", b=2, f=16)
        T5 = t2[:].rearrange("p n (g b f) -> p (n g) b f", b=2, f=16)
        K.tt("pool", T5[:, :, 0, :], X5[:, :, 1, :], S5[:, :, 0, :], ALU.mult, [X, self.sinF], [t2])
        K.tt("pool", T5[:, :, 1, :], X5[:, :, 0, :], S5[:, :, 1, :], ALU.mult, [X, self.sinF], [t2])
        K.tt("dve", X[:], tmp[:], t2[:], ALU.add, [tmp, t2], [X])

    def load_tok(self, dst, tok0, nt, col0, width, eng=None):
        K = self.K
        K.dma(eng or K.dq(), dst[:, 0:nt, 0:width],
              self.proj[tok0:tok0 + nt * 128, col0:col0 + width].rearrange("(n p) c -> p n c", p=128),
              [self.proj], [dst])

    def tok2feat(self, dstT, src, nt, col0=0):
        K = self.K
        for n in range(nt):
            ps = K.psum()
            for h in range(4):
                K.tr(ps[0:64, h * 128:(h + 1) * 128], src[:, n, col0 + h * 64:col0 + (h + 1) * 64], self.ident[:],
                     [src, self.ident], [ps])
            K.cp(K.evq(), dstT[:, :, n * 128:(n + 1) * 128], ps[0:64, :].rearrange("p (h t) -> p h t", h=4),
                 [ps], [dstT])

    def head_norm(self, X, nt, centre):
        K = self.K
        G = nt * 4
        X3 = X[:].rearrange("p n (h d) -> p (n h) d", h=4)
        st = K.rot("hn_st", [128, G], F32, 2)
        if centre:
            K.red(st[:], X3, ALU.add, [X], [st])
            K.ts("dve", st[:], st[:], -1.0 / 64.0, None, ALU.mult, None, [st], [st])
            K.tt("dve", X3, X3, bc(st[:].unsqueeze(2), [128, G, 64]), ALU.add, [X, st], [X])
        sq = K.rot("ropeA", [128, nt, 256], F32, 1)
        K.tt("pool", sq[:], X[:], X[:], ALU.mult, [X], [sq])
        ms = K.rot("hn_ms", [128, G], F32, 2)
        K.red(ms[:], sq[:].rearrange("p n (h d) -> p (n h) d", h=4), ALU.add, [sq], [ms])
        K.ts("dve", ms[:], ms[:], 1.0 / 64.0, LN_EPS, ALU.mult, ALU.add, [ms], [ms])
        K.act(ms[:], ms[:], AF.Sqrt, [ms], [ms])
        K.S.op("dve", lambda e, ms=ms: e.reciprocal(ms[:], ms[:]), [ms.b], [ms.b])
        K.tt("dve", X3, X3, bc(ms[:].unsqueeze(2), [128, G, 64]), ALU.mult, [X, ms], [X])

    def store_branch(self, z, tok0, nt, src):
        K = self.K
        for n in range(nt):
            ps = K.psum()
            for kk in range(2):
                K.tr(ps[:, kk * 128:(kk + 1) * 128], src[:, n, kk * 128:(kk + 1) * 128], self.ident[:],
                     [src, self.ident], [ps])
            t0 = tok0 + n * 128
            K.cp(K.evq(), self.brT[:, 2 * z:2 * z + 2, t0:t0 + 128], ps[:, 0:256].rearrange("p (k t) -> p k t", k=2),
                 [ps], [self.brT])
        if self.brdbg is not None:
            K.dma("sp", self.brdbg[z, tok0:tok0 + nt * 128, :].rearrange("(n p) c -> p n c", p=128), src[:, 0:nt, :],
                  [src], [self.brdbg])

    def mlstm(self, l, tok0, T_, latent, pb):
        K = self.K
        I = self.I
        nt = T_ // 128
        LN8 = math.log(0.125)
        with K.scope():
            q = K.sb([128, nt, 256], F32, "mq")
            k = K.sb([128, nt, 256], F32, "mk")
            og = K.sb([128, nt, 256], F32, "mo")
            vaug = K.sb([128, nt, 4, 65], F32, "mv")
            ifg = K.sb([128, nt, 16], F32, "mifg")
            self.load_tok(q, tok0, nt, COLS['m_q'], 256)
            self.load_tok(k, tok0, nt, COLS['m_k'], 256)
            self.load_tok(og, tok0, nt, COLS['m_o'], 256)
            self.load_tok(ifg, tok0, nt, COLS['m_i'], 16)
            K.memset("pool", vaug[:], 1.0, [vaug])
            for n in range(nt):
                K.dma(K.dq(), vaug[:, n, :, 0:64],
                      self.proj[tok0 + n * 128:tok0 + (n + 1) * 128, COLS['m_v']:COLS['m_v'] + 256].rearrange(
                          "p (h d) -> p h d", h=4), [self.proj], [vaug])
            bigf = K.sb([128, 16], F32, "bigf")
            K.dma("sp", bigf[:, 0:8], bc(I["b_ig"][l:l + 1, :], [128, 8]), [I["b_ig"]], [bigf])
            K.dma("sp", bigf[:, 8:16], bc(I["b_fg"][l:l + 1, :], [128, 8]), [I["b_fg"]], [bigf])
            if latent:
                with K.scope():
                    self.rope(q, nt)
                    self.rope(k, nt)
            qT = K.sb([64, 4, T_], F32, "mqT")
            qTb = K.sb([64, 4, T_], BF16, "mqTb")
            kT = K.sb([64, 4, T_], BF16, "mkT")
            self.tok2feat(qT, q, nt)
            K.cp("pool", qTb[:], qT[:], [qT], [qTb])
            self.tok2feat(kT, k, nt)
            K.tt("dve", ifg[:], ifg[:], bc(bigf[:].unsqueeze(1), [128, nt, 16]), ALU.add, [ifg, bigf], [ifg])
            lf = K.sb([128, nt, 8], F32, "mlf")
            K.act(lf[:], ifg[:, :, 8:16], AF.Exp, [ifg], [lf], scale=-1.0)
            K.act(lf[:], lf[:], AF.Ln, [lf], [lf], bias=1.0)
            K.ts("dve", lf[:], lf[:], -1.0, None, ALU.mult, None, [lf], [lf])
            Bcol = K.sb([128, nt, 8], F32, "mB")
            Bmat = K.sb([8, T_], F32, "mBmat")
            with K.scope():
                lfT = K.sb([8, T_], F32, "mlfT")
                for cg in range((nt + 3) // 4):
                    ps = K.psum()
                    nn_ = min(4, nt - cg * 4)
                    for i_ in range(nn_):
                        K.tr(ps[0:8, i_ * 128:(i_ + 1) * 128], lf[:, cg * 4 + i_, :], self.ident[:], [lf, self.ident], [ps])
                    K.cp("dve", lfT[:, cg * 512:cg * 512 + nn_ * 128], ps[0:8, 0:nn_ * 128], [ps], [lfT])
                one8 = K.sb([8, T_], F32, "mone8")
                K.memset("dve", one8[:], 1.0, [one8])
                Bp = K.sb([8, T_], F32, "mBp")
                K.S.op("dve", lambda e: e.tensor_tensor_scan(Bp[:], one8[:], lfT[:], 0.0, ALU.mult, ALU.add),
                       [one8.b, lfT.b], [Bp.b])
                isf = K.sb([8, 1], F32, "misf")
                K.ts("dve", isf[:], self.pidx[0:8, :], 4.0, None, ALU.is_lt, None, [self.pidx], [isf])
                K.tt("dve", lfT[:], lfT[:], Bp[:], ALU.subtract, [lfT, Bp], [lfT])
                K.ts("dve", lfT[:], lfT[:], Bp[:, T_ - 1:T_], None, ALU.add, None, [lfT, Bp], [lfT])
                K.tt("dve", Bp[:], Bp[:], lfT[:], ALU.subtract, [Bp, lfT], [Bp])
                K.stt("dve", Bmat[:], Bp[:], isf[:, 0:1], lfT[:], ALU.mult, ALU.add, [Bp, isf, lfT], [Bmat])
            ps = K.psum()
            for n in range(nt):
                K.tr(ps[:, n * 8:(n + 1) * 8], Bmat[:, n * 128:(n + 1) * 128], self.ident[0:8, 0:8], [Bmat, self.ident], [ps])
            K.cp("act", Bcol[:], ps[:, 0:nt * 8].rearrange("p (n j) -> p n j", j=8), [ps], [Bcol])
            cb = K.sb([128, nt, 8], F32, "mcb")
            K.tt("dve", cb[:], ifg[:, :, 0:8], Bcol[:], ALU.subtract, [ifg, Bcol], [cb])
            K.ts("dve", cb[:], cb[:], LN8, None, ALU.add, None, [cb], [cb])
            if latent:
                m0b = K.sb([128, 8], F32, "m0b")
                K.dma("sp", m0b[:], bc(I["sm"][l:l + 1, :], [128, 8]), [I["sm"]], [m0b])
                c0a = K.sb([64, 8, 65], F32, "c0a")
                K.dma("sp", c0a[:, :, 0:64], I["sC"][l].rearrange("d h k v -> k (d h) v"), [I["sC"]], [c0a])
                K.dma("sp", c0a[:, :, 64], I["sn"][l].rearrange("d h k -> k (d h)"), [I["sn"]], [c0a],
                      allow_slow_non_contiguous=True)
            hsum = K.sb([128, nt, 256], F32, "mhs")
            K.memset("pool", hsum[:], 0.0, [hsum])
            C = dict(nt=nt, T=T_, latent=latent, qT=qT, qTb=qTb, kT=kT, vaug=vaug, Bcol=Bcol, Bmat=Bmat, cb=cb, hsum=hsum,
                     m0b=(m0b if latent else None), c0a=(c0a if latent else None))
            with K.scope():
                nch = (T_ + 511) // 512
                for dr in range(2):
                    K.interleave([self._mlstm_pre(dr * 4 + h, C) for h in range(4)])
                    for c in range(nch):
                        K.interleave([self._mlstm_unit(dr * 4 + h, c, C) for h in range(4)])
            self.head_norm(hsum, nt, True)
            K.act(og[:], og[:], AF.Sigmoid, [og], [og])
            K.tt("dve", hsum[:], hsum[:], og[:], ALU.mult, [hsum, og], [hsum])
            self.store_branch(0, tok0, nt, hsum)
            if not latent:
                self.mlstm_state(l, pb, nt, T_, k, vaug, ifg, lf)

    def _mlstm_pre(self, j, C):
        K = self.K
        nt, T_, latent = C["nt"], C["T"], C["latent"]
        qT, Bcol, m0b = C["qT"], C["Bcol"], C["m0b"]
        h = j % 4
        sx = f"_{h}"
        Brow = K.rot("mBrow" + sx, [128, T_], F32, 1)
        C["Brow", j] = Brow
        Bmat = C["Bmat"]
        for c0 in range(0, T_, 512):
            w = min(512, T_ - c0)
            ps = K.psum(hold=True, nbanks=8)
            K.mm(ps[:, 0:w], self.sel8[:, j, :], Bmat[:, c0:c0 + w], True, True, [self.sel8, Bmat], [ps])
            yield
            K.cp("act", Brow[:, c0:c0 + w], ps[:, 0:w], [ps], [Brow])
            K.psfree(ps)
        if latent:
            qTw = K.rot("mqTw" + sx, [64, T_], F32, 1)
            C["qTw", j] = qTw
            K.act(qTw[:], Brow[0:64, :], AF.Exp, [Brow, m0b], [qTw], bias=m0b[0:64, j:j + 1])
            K.tt("dve", qTw[:], qT[:, h, :], qTw[:], ALU.mult, [qT, qTw], [qTw])

    def _mlstm_unit(self, j, c, C):
        K = self.K
        nt, T_, latent = C["nt"], C["T"], C["latent"]
        qT, kT, vaug, cb, hsum, c0a = (C[k_] for k_ in ("qTb", "kT", "vaug", "cb", "hsum", "c0a"))
        Brow = C["Brow", j]
        dr, h = j // 4, j % 4
        fwd = dr == 0
        sx = f"_{h}"
        c0 = c * 512
        c1 = min(T_, c0 + 512)
        order = list(range(nt)) if fwd else list(range(nt - 1, -1, -1))
        acc = K.psum(hold=True, nbanks=8)
        started = False
        steps = []
        for m in order:
            ta, tb = (m * 128, T_) if fwd else (0, (m + 1) * 128)
            ca, cb_ = max(ta, c0), min(tb, c1)
            if ca < cb_:
                steps.append((m, ca, cb_))
        for si, (m, ca, cb_) in enumerate(steps):
            w = cb_ - ca
            rc = m * 128 + 127 if fwd else m * 128
            refc = Brow[:, rc:rc + 1]
            st = K.rot("mst" + sx, [128, 2], F32, 3)
            K.ts("dve", st[:, 0:1], refc, -1.0, None, ALU.mult, None, [Brow], [st])
            K.act(st[:, 1:2], cb[:, m, j:j + 1], AF.Exp, [cb, Brow], [st], bias=refc)
            vs = K.rot("mvs" + sx, [128, 65], BF16, 3)
            K.ts("dve", vs[:], vaug[:, m, h, :], st[:, 1:2], None, ALU.mult, None, [vaug, st], [vs])
            ps = K.psum(hold=True, nbanks=8)
            K.mm(ps[:, 0:w], kT[:, h, m * 128:(m + 1) * 128], qT[:, h, ca:cb_], True, True, [kT, qT], [ps])
            E1 = K.rot("mE1" + sx, [128, 512], F32, 1)
            K.act(E1[:, 0:w], Brow[:, ca:cb_], AF.Exp, [Brow, st], [E1], bias=st[:, 0:1])
            yield
            Pm = K.rot("mPm" + sx, [128, 512], BF16, 2)
            K.tt("dve", Pm[:, 0:w], ps[:, 0:w], E1[:, 0:w], ALU.mult, [ps, E1], [Pm])
            K.psfree(ps)
            if ca <= m * 128 < cb_:
                off = m * 128 - ca
                if fwd:
                    K.S.op("pool", lambda e, Pm=Pm, off=off: e.affine_select(
                        Pm[:, off:off + 128], Pm[:, off:off + 128], [[1, 128]], ALU.is_ge, 0.0, base=0,
                        channel_multiplier=-1), [Pm.b], [Pm.b])
                else:
                    K.S.op("pool", lambda e, Pm=Pm, off=off: e.affine_select(
                        Pm[:, off:off + 128], Pm[:, off:off + 128], [[-1, 128]], ALU.is_ge, 0.0, base=0,
                        channel_multiplier=1), [Pm.b], [Pm.b])
            last = (si == len(steps) - 1) and not latent
            K.mm(acc[0:65, ca - c0:cb_ - c0], vs[:], Pm[:, 0:w], si == 0, last, [vs, Pm], [acc])
        if latent:
            qTw = C["qTw", j]
            K.mm(acc[0:65, 0:c1 - c0], c0a[:, j, :], qTw[:, c0:c1], False, True, [c0a, qTw], [acc])
        yield
        hTj = K.rot("mhTj" + sx, [65, 512], F32, 1)
        K.cp("act", hTj[0:65, 0:c1 - c0], acc[0:65, 0:c1 - c0], [acc], [hTj])
        K.psfree(acc)
        ntl = (c1 - c0) // 128
        n0 = c0 // 128
        ps = K.psum(hold=True, nbanks=8)
        for i_ in range(ntl):
            K.tr(ps[:, i_ * 65:(i_ + 1) * 65], hTj[0:65, i_ * 128:(i_ + 1) * 128], self.ident[0:65, 0:65],
                 [hTj, self.ident], [ps])
        yield
        X3 = ps[:, 0:ntl * 65].rearrange("p (n c) -> p n c", c=65)
        den = K.rot("mden" + sx, [128, 4], F32, 2)
        K.ts("dve", den[:, 0:ntl], X3[:, :, 64], -1.0, None, ALU.mult, None, [ps], [den])
        K.tt("dve", den[:, 0:ntl], den[:, 0:ntl], X3[:, :, 64], ALU.max, [ps, den], [den])
        K.ts("dve", den[:, 0:ntl], den[:, 0:ntl], 1.0, None, ALU.max, None, [den], [den])
        K.S.op("dve", lambda e, den=den: e.reciprocal(den[:, 0:ntl], den[:, 0:ntl]), [den.b], [den.b])
        tmp = K.rot("mtmp" + sx, [128, 4, 64], F32, 2)
        K.tt("dve", tmp[:, 0:ntl, :], X3[:, :, 0:64], bc(den[:, 0:ntl].unsqueeze(2), [128, ntl, 64]), ALU.mult,
             [ps, den], [tmp])
        K.psfree(ps)
        hs_ = hsum[:, n0:n0 + ntl, h * 64:(h + 1) * 64]
        K.tt("pool", hs_, hs_, tmp[:, 0:ntl, :], ALU.add, [hsum, tmp], [hsum])

    def mlstm_state(self, l, pb, nt, T_, k, vaug, ifg, lf):
        K = self.K
        LN8 = math.log(0.125)
        rows = K.sb([8, 2, T_], F32, "msrow")
        for which, c0 in ((0, None), (1, 0)):
            ps = K.psum()
            for n in range(nt):
                src = lf[:, n, :] if which == 0 else ifg[:, n, 0:8]
                K.tr(ps[0:8, n * 128:(n + 1) * 128], src, self.ident[:], [lf, ifg, self.ident], [ps])
            K.cp("dve", rows[:, which, :], ps[0:8, 0:T_], [ps], [rows])
        one8 = K.sb([8, T_], F32, "msone")
        K.memset("dve", one8[:], 1.0, [one8])
        Bp = K.sb([8, T_], F32, "msB")
        K.S.op("dve", lambda e: e.tensor_tensor_scan(Bp[:], one8[:], rows[:, 0, :], 0.0, ALU.mult, ALU.add),
               [one8.b, rows.b], [Bp.b])
        isf = K.sb([8, 1], F32, "msisf")
        K.ts("dve", isf[:], self.pidx[0:8, :], 4.0, None, ALU.is_lt, None, [self.pidx], [isf])
        a1 = K.sb([8, T_], F32, "msa1")
        a2 = K.sb([8, T_], F32, "msa2")
        K.ts("dve", a1[:], Bp[:], -1.0, Bp[:, T_ - 1:T_], ALU.mult, ALU.add, [Bp], [a1])
        K.tt("dve", a2[:], Bp[:], rows[:, 0, :], ALU.subtract, [Bp, rows], [a2])
        K.tt("dve", a1[:], a1[:], a2[:], ALU.subtract, [a1, a2], [a1])
        K.tt("dve", a2[:], a2[:], rows[:, 1, :], ALU.add, [a2, rows], [a2])
        lw = K.sb([8, T_], F32, "mslw")
        K.stt("dve", lw[:], a1[:], isf[:, 0:1], a2[:], ALU.mult, ALU.add, [a1, isf, a2], [lw])
        mnew = K.sb([8, 2], F32, "msm")
        K.red(mnew[:, 0:1], lw[:], ALU.max, [lw], [mnew])
        K.tt("dve", mnew[:, 0:1], mnew[:, 0:1], Bp[:, T_ - 1:T_], ALU.max, [mnew, Bp], [mnew])
        K.ts("dve", mnew[:, 1:2], mnew[:, 0:1], -1.0, LN8, ALU.mult, ALU.add, [mnew], [mnew])
        K.dma("sp", self.O["nm"][pb, l, :].rearrange("(j o) -> j o", o=1), mnew[:, 0:1], [mnew], [self.O["nm"]])
        wrow = K.sb([8, T_], F32, "mswr")
        K.act(wrow[:], lw[:], AF.Exp, [lw, mnew], [wrow], bias=mnew[:, 1:2])
        wcol = K.sb([128, nt, 8], F32, "mswc")
        ps = K.psum()
        for n in range(nt):
            K.tr(ps[:, n * 8:(n + 1) * 8], wrow[:, n * 128:(n + 1) * 128], self.ident[0:8, 0:8], [wrow, self.ident], [ps])
        K.cp("dve", wcol[:], ps[:, 0:nt * 8].rearrange("p (n j) -> p n j", j=8), [ps], [wcol])
        kw = K.sb([128, nt, 8, 64], F32, "mskw")
        for n in range(nt):
            for dr in range(2):
                K.tt("dve", kw[:, n, dr * 4:(dr + 1) * 4, :], k[:, n, :].rearrange("p (h d) -> p h d", h=4),
                     bc(wcol[:, n, dr * 4:(dr + 1) * 4].unsqueeze(2), [128, 4, 64]), ALU.mult, [k, wcol], [kw])
        cst = K.sb([64, 8, 65], F32, "mscst")
        for jg in range(2):
            ps = K.psum()
            for jj in range(4):
                j = jg * 4 + jj
                for n in range(nt):
                    K.mm(ps[0:64, jj * 65:(jj + 1) * 65], kw[:, n, j, :], vaug[:, n, j % 4, :], n == 0, n == nt - 1,
                         [kw, vaug], [ps])
            K.cp("act", cst[:, jg * 4:(jg + 1) * 4, :], ps[0:64, 0:260].rearrange("p (j c) -> p j c", j=4), [ps], [cst])
        K.dma("sp", self.O["nC"][pb, l].rearrange("j k v -> k j v"), cst[:, :, 0:64], [cst], [self.O["nC"]])
        K.dma("sp", self.O["nn"][pb, l].rearrange("j k -> k j"), cst[:, :, 64], [cst], [self.O["nn"]],
              allow_slow_non_contiguous=True)

    def lowrank(self, dst, srcT_tile, nt, W2pad, rows):
        pass

    def _gla_unit(self, dr, n, idx, C):
        K = self.K
        q, k, v, g, Sst, Sv, osum = (C[k_] for k_ in ("q", "k", "v", "g", "Sst", "Sv", "osum"))
        sx = f"_{dr}"
        fwd = dr == 0
        zero_state = (not C["latent"]) and idx == 0
        LT = self.U if fwd else self.Lo
        gs = g[:, n, dr, :]
        Sj = Sv[dr * 4:(dr + 1) * 4]
        S4 = Sst[:, dr * 4:(dr + 1) * 4, :]
        psG = K.psum(hold=True, nbanks=8)
        psT = K.psum(hold=True, nbanks=8)
        K.mm(psG[:, 0:256], LT[:], gs, True, True, [LT, g], [psG])
        K.mm(psT[:, 0:256], self.ones[:], gs, True, True, [self.ones, g], [psT])
        for h in range(4):
            K.mm(psT[0:64, 256 + h:257 + h], g[:, n, dr, h * 64:(h + 1) * 64], self.ones[:, 0:1], True, True,
                 [g, self.ones], [psT])
        yield
        eG = K.rot("geG" + sx, [128, 256], F32, 1)
        enG = K.rot("genG" + sx, [128, 256], F32, 1)
        eGt = K.rot("geGt" + sx, [128, 256], F32, 1)
        dcol = K.rot("gdcol" + sx, [64, 4], F32, 2)
        K.act(eG[:], psG[:, 0:256], AF.Exp, [psG], [eG])
        K.act(enG[:], psG[:, 0:256], AF.Exp, [psG], [enG], scale=-1.0)
        K.act(eGt[:], psT[:, 0:256], AF.Exp, [psT], [eGt])
        K.act(dcol[:], psT[0:64, 256:260], AF.Exp, [psT], [dcol])
        K.psfree(psG)
        K.psfree(psT)
        qt, kt, kh = eG, enG, eGt
        K.tt("dve", qt[:], q[:, n, :], eG[:], ALU.mult, [q, eG], [qt])
        K.tt("dve", kt[:], k[:, n, :], enG[:], ALU.mult, [k, enG], [kt])
        K.tt("pool", kh[:], kt[:], eGt[:], ALU.mult, [kt, eGt], [kh])
        vb = K.rot("gvb" + sx, [128, 256], BF16, 1)
        K.cp("pool", vb[:], v[:, n, :], [v], [vb])
        psq = K.psum(hold=True, nbanks=8)
        psk = K.psum(hold=True, nbanks=8)
        for (ps_, src_) in ((psq, qt), (psk, kt)):
            for h in range(4):
                K.tr(ps_[0:64, h * 128:(h + 1) * 128], src_[:, h * 64:(h + 1) * 64], self.ident[:], [src_, self.ident], [ps_])
        yield
        v4 = lambda ps_: ps_[0:64, :].rearrange("p (h t) -> p h t", h=4)
        qtT = K.rot("gqtT" + sx, [64, 4, 128], F32, 1)
        qtTb = K.rot("gqtTb" + sx, [64, 4, 128], BF16, 1)
        ktTb = K.rot("gktTb" + sx, [64, 4, 128], BF16, 1)
        K.cp("act", qtT[:], v4(psq), [psq], [qtT])
        K.cp("dve", qtTb[:], v4(psq), [psq], [qtTb])
        K.cp("act", ktTb[:], v4(psk), [psk], [ktTb])
        K.psfree(psq)
        K.psfree(psk)
        psA = K.psum(hold=True, nbanks=8)
        for h in range(4):
            K.mm(psA[:, h * 128:(h + 1) * 128], ktTb[:, h, :], qtTb[:, h, :], True, True, [ktTb, qtTb], [psA])
        yield
        attm = K.rot("gatt" + sx, [128, 4, 128], BF16, 1)
        K.tt("dve", attm[:], psA[:].rearrange("p (h t) -> p h t", h=4), bc(LT[:].unsqueeze(1), [128, 4, 128]), ALU.mult,
             [psA, LT], [attm])
        K.psfree(psA)
        acc = K.psum(hold=True, nbanks=8)
        psU = K.psum(hold=True, nbanks=8)
        for h in range(4):
            hs = slice(h * 64, (h + 1) * 64)
            K.mm(acc[:, hs], attm[:, h, :], vb[:, hs], True, zero_state, [attm, vb], [acc])
            if not zero_state:
                K.mm(acc[:, hs], qtT[:, h, :], Sst[:, dr * 4 + h, :], False, True, [qtT, Sj[h]], [acc])
            K.mm(psU[0:64, hs], kh[:, hs], v[:, n, hs], True, True, [kh, v], [psU])
        yield
        K.tt("dve", osum[:, n, :], osum[:, n, :], acc[:, 0:256], ALU.add, [osum, acc], [osum])
        K.psfree(acc)
        K.tt("dve", S4, S4, bc(dcol[:].unsqueeze(2), [64, 4, 64]), ALU.mult, Sj + [dcol], Sj)
        K.tt("dve", S4, S4, psU[0:64, 0:256].rearrange("p (h v) -> p h v", h=4), ALU.add, Sj + [psU], Sj)
        K.psfree(psU)

    def gla(self, l, tok0, T_, latent, pb):
        K = self.K
        I = self.I
        nt = T_ // 128
        with K.scope():
            q = K.sb([128, nt, 256], F32, "gq")
            k = K.sb([128, nt, 256], F32, "gk")
            v = K.sb([128, nt, 256], F32, "gv")
            gg = K.sb([128, nt, 256], F32, "ggg")
            ga = K.sb([128, nt, 32], F32, "gga")
            self.load_tok(q, tok0, nt, COLS['g_q'], 256)
            self.load_tok(k, tok0, nt, COLS['g_k'], 256)
            self.load_tok(v, tok0, nt, COLS['g_v'], 256)
            self.load_tok(gg, tok0, nt, COLS['g_g'], 256)
            self.load_tok(ga, tok0, nt, COLS['g_a'], 32)
            if latent:
                self.rope(q, nt)
                self.rope(k, nt)
            K.ts("dve", q[:], q[:], 0.125, None, ALU.mult, None, [q], [q])
            W2 = K.sb([32, 2, 256], F32, "gW2")
            K.memset("dve", W2[:], 0.0, [W2])
            K.dma("sp", W2[0:16, 0, :], I["w_gla_a2"][l, 0], [I["w_gla_a2"]], [W2])
            K.dma("sp", W2[16:32, 1, :], I["w_gla_a2"][l, 1], [I["w_gla_a2"]], [W2])
            bgl = K.sb([128, 2, 256], F32, "gbgl")
            K.dma("sp", bgl[:], bc(I["b_gla_a"][l:l + 1].rearrange("o d c -> o (d c)"), [128, 512]).rearrange(
                "p (d c) -> p d c", d=2), [I["b_gla_a"]], [bgl])
            g = K.sb([128, nt, 2, 256], F32, "gg_")
            for n in range(nt):
                gaT = K.rot("gaT", [32, 128], F32, 2)
                self.transpose_block(gaT[:], gaT, ga[:, n, :], ga, 128, 32)
                for dr in range(2):
                    ps = K.psum()
                    K.mm(ps[:, 0:256], gaT[:], W2[:, dr, :], True, True, [gaT, W2], [ps])
                    K.tt("dve", g[:, n, dr, :], ps[:, 0:256], bgl[:, dr, :], ALU.add, [ps, bgl], [g])
            K.act(g[:], g[:], AF.Exp, [g], [g], scale=-1.0)
            K.act(g[:], g[:], AF.Ln, [g], [g], bias=1.0)
            K.ts("dve", g[:], g[:], -1.0 / 16.0, None, ALU.mult, None, [g], [g])
            Sst = K.sb([64, 8, 64], F32, "gS")
            Sv = [T(Sst.h, f"gS{j}") for j in range(8)]
            if latent:
                K.dma("sp", Sst[:], I["sg"][l].rearrange("d h k v -> k (d h) v"), [I["sg"]], Sv)
            else:
                K.memset("dve", Sst[:], 0.0, Sv)
            osum = K.sb([128, nt, 256], F32, "gos")
            K.memset("pool", osum[:], 0.0, [osum])
            C = dict(nt=nt, latent=latent, q=q, k=k, v=v, g=g, Sst=Sst, Sv=Sv, osum=osum)
            with K.scope():
                for idx in range(nt):
                    K.interleave([self._gla_unit(0, idx, idx, C), self._gla_unit(1, nt - 1 - idx, idx, C)])
            self.head_norm(osum, nt, False)
            K.act(gg[:], gg[:], AF.Silu, [gg], [gg])
            K.tt("dve", osum[:], osum[:], gg[:], ALU.mult, [osum, gg], [osum])
            self.store_branch(1, tok0, nt, osum)
            if not latent:
                K.dma("sp", self.O["ng"][pb, l].rearrange("j k v -> k j v"), Sst[:], Sv, [self.O["ng"]])

    def rwkv(self, l, tok0, T_, latent, pb):
        K = self.K
        I = self.I
        nt = T_ // 128
        C0 = COLS['r_rkv']
        with K.scope():
            rkv = K.sb([128, nt, 768], F32, "rrkv")
            logw = K.sb([128, nt, 2, 256], F32, "rlogw")
            aa = K.sb([128, nt, 2, 256], F32, "raa")
            kvec = K.sb([128, 3, 256], F32, "rkvec")
            self._rwkv_pre(l, tok0, nt, rkv, logw, aa, kvec)
            self._rwkv_main(l, tok0, nt, latent, pb, rkv, logw, aa, kvec)

    def _rwkv_pre(self, l, tok0, nt, rkv, logw, aa, kvec):
        K = self.K
        I = self.I
        C0 = COLS['r_rkv']
        with K.scope():
            taps = K.sb([128, 3, 768], F32, "rtaps")
            K.dma("sp", taps[:], bc(I["shift_rwkv"][l:l + 1].rearrange("o s c -> o (s c)"), [128, 2304]).rearrange(
                "p (s c) -> p s c", s=3), [I["shift_rwkv"]], [taps])
            for n in range(nt):
                r0 = tok0 + n * 128
                xm = K.rot("rxm", [128, 768], F32, 2)
                xp = K.rot("rxp", [128, 768], F32, 2)
                K.dma(K.dq(), rkv[:, n, :], self.proj[r0:r0 + 128, C0:C0 + 768], [self.proj], [rkv])
                if n == 0:
                    K.memset("pool", xm[:], 0.0, [xm])
                    K.dma(K.dq(), xm[1:128, :], self.proj[r0:r0 + 127, C0:C0 + 768], [self.proj], [xm])
                else:
                    K.dma(K.dq(), xm[:], self.proj[r0 - 1:r0 + 127, C0:C0 + 768], [self.proj], [xm])
                if n == nt - 1:
                    K.memset("pool", xp[:], 0.0, [xp])
                    K.dma(K.dq(), xp[0:127, :], self.proj[r0 + 1:r0 + 128, C0:C0 + 768], [self.proj], [xp])
                else:
                    K.dma(K.dq(), xp[:], self.proj[r0 + 1:r0 + 129, C0:C0 + 768], [self.proj], [xp])
                K.tt("dve", rkv[:, n, :], rkv[:, n, :], taps[:, 1, :], ALU.mult, [rkv, taps], [rkv])
                K.tt("pool", xm[:], xm[:], taps[:, 0, :], ALU.mult, [xm, taps], [xm])
                K.tt("pool", xp[:], xp[:], taps[:, 2, :], ALU.mult, [xp, taps], [xp])
                K.tt("dve", rkv[:, n, :], rkv[:, n, :], xm[:], ALU.add, [rkv, xm], [rkv])
                K.tt("dve", rkv[:, n, :], rkv[:, n, :], xp[:], ALU.add, [rkv, xp], [rkv])
        with K.scope():
            lwag = K.sb([128, nt, 128], F32, "rlwag")
            self.load_tok(lwag, tok0, nt, COLS['r_w'], 128)
            K.act(lwag[:, :, 0:64], lwag[:, :, 0:64], AF.Tanh, [lwag], [lwag])
            Ww = K.sb([64, 2, 256], F32, "rWw")
            Wa = K.sb([64, 2, 256], F32, "rWa")
            K.memset("dve", Ww[:], 0.0, [Ww])
            K.memset("dve", Wa[:], 0.0, [Wa])
            for dr in range(2):
                K.dma("sp", Ww[dr * 32:(dr + 1) * 32, dr, :], I["w_w2"][l, dr], [I["w_w2"]], [Ww])
                K.dma("sp", Wa[dr * 32:(dr + 1) * 32, dr, :], I["w_a2"][l, dr], [I["w_a2"]], [Wa])
            w0b = K.sb([128, 2, 256], F32, "rw0b")
            a0b = K.sb([128, 2, 256], F32, "ra0b")
            K.dma("sp", w0b[:], bc(I["w0_rwkv"][l:l + 1].rearrange("o d c -> o (d c)"), [128, 512]).rearrange(
                "p (d c) -> p d c", d=2), [I["w0_rwkv"]], [w0b])
            K.dma("sp", a0b[:], bc(I["a0_rwkv"][l:l + 1].rearrange("o d c -> o (d c)"), [128, 512]).rearrange(
                "p (d c) -> p d c", d=2), [I["a0_rwkv"]], [a0b])
            for i_, nm in enumerate(("k_k", "k_a", "r_k")):
                K.dma("sp", kvec[:, i_, :], bc(I[nm][l:l + 1, :], [128, 256]), [I[nm]], [kvec])
            for n in range(nt):
                ps = K.psum()
                for i_ in range(2):
                    K.tr(ps[0:64, i_ * 128:(i_ + 1) * 128], lwag[:, n, i_ * 64:(i_ + 1) * 64], self.ident[:],
                         [lwag, self.ident], [ps])
                tT = K.rot("rtT", [64, 2, 128], F32, 2)
                K.cp(K.evq(), tT[:], ps[0:64, 0:256].rearrange("p (i t) -> p i t", i=2), [ps], [tT])
                for dr in range(2):
                    ps = K.psum()
                    K.mm(ps[:, 0:256], tT[:, 0, :], Ww[:, dr, :], True, True, [tT, Ww], [ps])
                    K.tt("dve", logw[:, n, dr, :], ps[:, 0:256], w0b[:, dr, :], ALU.add, [ps, w0b], [logw])
                    ps = K.psum()
                    K.mm(ps[:, 0:256], tT[:, 1, :], Wa[:, dr, :], True, True, [tT, Wa], [ps])
                    K.tt("dve", aa[:, n, dr, :], ps[:, 0:256], a0b[:, dr, :], ALU.add, [ps, a0b], [aa])
            K.act(logw[:], logw[:], AF.Sigmoid, [logw], [logw])
            K.ts("dve", logw[:], logw[:], -math.exp(-0.5), None, ALU.mult, None, [logw], [logw])
            K.act(aa[:], aa[:], AF.Sigmoid, [aa], [aa])

    def _rwkv_unit(self, dr, n, idx, C):
        K = self.K
        sx = f"_{dr}"
        fwd = dr == 0
        rkv, logw, aa, kvec, kap, Sst, Sv, ysum = (C[k_] for k_ in ("rkv", "logw", "aa", "kvec", "kap", "Sst", "Sv", "ysum"))
        zero_state = (not C["latent"]) and idx == 0
        LT = self.U if fwd else self.Lo
        mN = self.Ls if fwd else self.Us
        mNT = self.Us if fwd else self.Ls
        mMT = self.U if fwd else self.Lo
        lw_ = logw[:, n, dr, :]
        a_ = aa[:, n, dr, :]
        r_ = rkv[:, n, 0:256]
        k_ = rkv[:, n, 256:512]
        v_ = rkv[:, n, 512:768]
        Sj = Sv[dr * 4:(dr + 1) * 4]
        S4 = Sst[:, dr * 4:(dr + 1) * 4, :]

        def m4(mask):
            return bc(mask[:].unsqueeze(1), [128, 4, 128])
        psL = K.psum(hold=True, nbanks=8)
        psT = K.psum(hold=True, nbanks=8)
        K.mm(psL[:, 0:256], LT[:], lw_, True, True, [LT, logw], [psL])
        K.mm(psT[:, 0:256], self.ones[:], lw_, True, True, [self.ones, logw], [psT])
        for h in range(4):
            K.mm(psT[0:64, 256 + h:257 + h], logw[:, n, dr, h * 64:(h + 1) * 64], self.ones[:, 0:1], True, True,
                 [logw, self.ones], [psT])
        yield
        eL = K.rot("reL" + sx, [128, 256], F32, 1)
        enL = K.rot("renL" + sx, [128, 256], F32, 1)
        eLex = K.rot("reLex" + sx, [128, 256], F32, 1)
        eLt = K.rot("reLt" + sx, [128, 256], F32, 1)
        pcol = K.rot("rpcol" + sx, [64, 4], F32, 2)
        K.act(eL[:], psL[:, 0:256], AF.Exp, [psL], [eL])
        K.act(enL[:], psL[:, 0:256], AF.Exp, [psL], [enL], scale=-1.0)
        K.tt("dve", eLex[:], psL[:, 0:256], lw_, ALU.subtract, [psL, logw], [eLex])
        K.act(eLex[:], eLex[:], AF.Exp, [eLex], [eLex])
        K.act(eLt[:], psT[:, 0:256], AF.Exp, [psT], [eLt])
        K.act(pcol[:], psT[0:64, 256:260], AF.Exp, [psT], [pcol])
        K.psfree(psL)
        K.psfree(psT)
        Bt = K.rot("rBt" + sx, [128, 256], F32, 1)
        Kt = K.rot("rKt" + sx, [128, 256], F32, 1)
        At, Rt, Bh, Kh = eLex, eL, enL, eLt
        K.stt("dve", At[:], kap[:, n, :], -1.0, eLex[:], ALU.mult, ALU.mult, [kap, eLex], [At])
        K.tt("pool", Bt[:], a_, kap[:, n, :], ALU.mult, [aa, kap], [Bt])
        K.tt("pool", Bt[:], Bt[:], enL[:], ALU.mult, [Bt, enL], [Bt])
        K.stt("dve", Kt[:], a_, -1.0, kvec[:, 1, :], ALU.add, ALU.mult, [aa, kvec], [Kt])
        K.stt("dve", Kt[:], Kt[:], 1.0, k_, ALU.add, ALU.mult, [Kt, rkv], [Kt])
        K.tt("pool", Kt[:], Kt[:], enL[:], ALU.mult, [Kt, enL], [Kt])
        K.tt("pool", Rt[:], r_, eL[:], ALU.mult, [rkv, eL], [Rt])
        K.tt("pool", Bh[:], Bt[:], eLt[:], ALU.mult, [Bt, eLt, enL], [Bh])
        K.tt("pool", Kh[:], Kt[:], eLt[:], ALU.mult, [Kt, eLt], [Kh])
        TT = {}
        for pair in ((("A", At), ("B", Bt)), (("K", Kt), ("R", Rt))):
            pss = []
            for nm_, src_ in pair:
                ps = K.psum(hold=True, nbanks=8)
                for h in range(4):
                    K.tr(ps[0:64, h * 128:(h + 1) * 128], src_[:, h * 64:(h + 1) * 64], self.ident[:],
                         [src_, self.ident], [ps])
                pss.append(ps)
            yield
            for (nm_, src_), ps in zip(pair, pss):
                dstT = K.rot("rT" + nm_ + sx, [64, 4, 128], BF16, 1)
                K.cp(K.evq(), dstT[:], ps[0:64, :].rearrange("p (h t) -> p h t", h=4), [ps], [dstT])
                K.psfree(ps)
                TT[nm_] = dstT
        AtT, BtT, KtT, RtT = TT["A"], TT["B"], TT["K"], TT["R"]
        Bhb = K.rot("rBhb" + sx, [128, 256], BF16, 1)
        Khb = K.rot("rKhb" + sx, [128, 256], BF16, 1)
        vb = K.rot("rvb" + sx, [128, 256], BF16, 1)
        K.cp("pool", Bhb[:], Bh[:], [Bh], [Bhb])
        K.cp("pool", Khb[:], Kh[:], [Kh], [Khb])
        K.cp("pool", vb[:], v_, [rkv], [vb])
        Sb = K.rot("rSb" + sx, [64, 4, 64], BF16, 1)
        if not zero_state:
            K.cp("act", Sb[:], S4, Sj, [Sb])
        v4 = lambda ps_: ps_[:].rearrange("p (h t) -> p h t", h=4)

        def prod_mm(lt, rt):
            ps_ = K.psum(hold=True, nbanks=8)
            for h in range(4):
                K.mm(ps_[:, h * 128:(h + 1) * 128], lt[:, h, :], rt[:, h, :], True, True, [lt, rt], [ps_])
            return ps_

        def prod_ev(ps_, nm_, mask, n_=1):
            o_ = K.rot(nm_ + sx, [128, 4, 128], BF16, n_)
            K.tt("dve", o_[:], v4(ps_), m4(mask), ALU.mult, [ps_, mask], [o_])
            K.psfree(ps_)
            return o_
        p1 = prod_mm(AtT, BtT)
        p2 = prod_mm(BtT, AtT)
        yield
        Nk = prod_ev(p1, "rN", mN, 2)
        Ntk = prod_ev(p2, "rNt", mNT, 2)
        Xt = K.rot("rXt" + sx, [128, 4, 128], F32, 2)
        K.tt("pool", Xt[:], Ntk[:], m4(self.ident), ALU.add, [Ntk, self.ident], [Xt])
        Xtb = K.rot("rXtb" + sx, [128, 4, 128], BF16, 1)
        K.cp("act", Xtb[:], Xt[:], [Xt], [Xtb])
        for it in range(6):
            ps1 = prod_mm(Ntk, Nk)
            ps2 = prod_mm(Nk, Ntk) if it < 5 else None
            yield
            Nk2 = K.rot("rN" + sx, [128, 4, 128], BF16, 2)
            K.cp("act", Nk2[:], v4(ps1), [ps1], [Nk2])
            K.psfree(ps1)
            if it < 5:
                Ntk2 = K.rot("rNt" + sx, [128, 4, 128], BF16, 2)
                K.cp("dve", Ntk2[:], v4(ps2), [ps2], [Ntk2])
                K.psfree(ps2)
            ps3 = prod_mm(Nk2, Xtb)
            yield
            Xt2 = K.rot("rXt" + sx, [128, 4, 128], F32, 2)
            K.tt("dve", Xt2[:], Xt[:], v4(ps3), ALU.add, [Xt, ps3], [Xt2])
            K.psfree(ps3)
            Xt = Xt2
            Xtb = K.rot("rXtb" + sx, [128, 4, 128], BF16, 1)
            K.cp("act", Xtb[:], Xt[:], [Xt], [Xtb])
            Nk = Nk2
            if it < 5:
                Ntk = Ntk2
        p3 = prod_mm(KtT, AtT)
        p4 = prod_mm(BtT, RtT)
        p5 = prod_mm(KtT, RtT)
        yield
        NakT = prod_ev(p3, "rN", mNT, 2)
        MrbT = prod_ev(p4, "rNt", mMT, 2)
        MrkT = prod_ev(p5, "rN", mMT, 2)
        psR = K.psum(hold=True, nbanks=8)
        for h in range(4):
            hs = slice(h * 64, (h + 1) * 64)
            if not zero_state:
                K.mm(psR[:, hs], AtT[:, h, :], Sb[:, h, :], True, False, [AtT, Sb], [psR])
            K.mm(psR[:, hs], NakT[:, h, :], vb[:, hs], zero_state, True, [NakT, vb], [psR])
        yield
        Rsb = K.rot("rRsb" + sx, [128, 256], BF16, 1)
        K.cp("act", Rsb[:], psR[:, 0:256], [psR], [Rsb])
        K.psfree(psR)
        psU = K.psum(hold=True, nbanks=8)
        for h in range(4):
            hs = slice(h * 64, (h + 1) * 64)
            K.mm(psU[:, hs], Xtb[:, h, :], Rsb[:, hs], True, True, [Xtb, Rsb], [psU])
        yield
        Usb = K.rot("rUsb" + sx, [128, 256], BF16, 1)
        K.cp("act", Usb[:], psU[:, 0:256], [psU], [Usb])
        K.psfree(psU)
        psY = K.psum(hold=True, nbanks=8)
        psS = K.psum(hold=True, nbanks=8)
        for h in range(4):
            hs = slice(h * 64, (h + 1) * 64)
            if not zero_state:
                K.mm(psY[:, hs], RtT[:, h, :], Sb[:, h, :], True, False, [RtT, Sb], [psY])
            K.mm(psY[:, hs], MrbT[:, h, :], Usb[:, hs], zero_state, False, [MrbT, Usb], [psY])
            K.mm(psY[:, hs], MrkT[:, h, :], vb[:, hs], False, True, [MrkT, vb], [psY])
            K.mm(psS[0:64, hs], Bhb[:, hs], Usb[:, hs], True, False, [Bhb, Usb], [psS])
            K.mm(psS[0:64, hs], Khb[:, hs], vb[:, hs], False, True, [Khb, vb], [psS])
        yield
        K.tt("dve", ysum[:, n, :], ysum[:, n, :], psY[:, 0:256], ALU.add, [ysum, psY], [ysum])
        K.psfree(psY)
        K.tt("dve", S4, S4, bc(pcol[:].unsqueeze(2), [64, 4, 64]), ALU.mult, Sj + [pcol], Sj)
        K.tt("dve", S4, S4, psS[0:64, 0:256].rearrange("p (h v) -> p h v", h=4), ALU.add, Sj + [psS], Sj)
        K.psfree(psS)

    def _rwkv_main(self, l, tok0, nt, latent, pb, rkv, logw, aa, kvec):
        K = self.K
        I = self.I
        if True:
            kap = K.sb([128, nt, 256], F32, "rkap")
            kk_ = rkv[:, :, 256:512]
            K.tt("dve", kap[:], kk_, bc(kvec[:, 0, :].unsqueeze(1), [128, nt, 256]), ALU.mult, [rkv, kvec], [kap])
            ss = K.sb([128, nt * 4], F32, "rss")
            with K.scope():
                sq = K.rot("ropeA", [128, nt, 256], F32, 1)
                K.tt("pool", sq[:], kap[:], kap[:], ALU.mult, [kap], [sq])
                K.red(ss[:], sq[:].rearrange("p n (h d) -> p (n h) d", h=4), ALU.add, [sq], [ss])
            K.ts("dve", ss[:], ss[:], LN_EPS, None, ALU.add, None, [ss], [ss])
            K.act(ss[:], ss[:], AF.Sqrt, [ss], [ss])
            K.S.op("dve", lambda e: e.reciprocal(ss[:], ss[:]), [ss.b], [ss.b])
            kap3 = kap[:].rearrange("p n (h d) -> p (n h) d", h=4)
            K.tt("dve", kap3, kap3, bc(ss[:].unsqueeze(2), [128, nt * 4, 64]), ALU.mult, [kap, ss], [kap])
            Sst = K.sb([64, 8, 64], F32, "rS")
            Sv = [T(Sst.h, f"rS{j}") for j in range(8)]
            if latent:
                R0 = K.sb([64, 8, 64], F32, "rR0")
                K.dma("sp", R0[:], I["sr"][l].rearrange("d h v k -> v (d h) k"), [I["sr"]], [R0])
                for jg in range(2):
                    ps = K.psum()
                    for jj in range(4):
                        K.tr(ps[0:64, jj * 64:(jj + 1) * 64], R0[:, jg * 4 + jj, :], self.ident[0:64, 0:64],
                             [R0, self.ident], [ps])
                    K.cp("dve", Sst[:, jg * 4:(jg + 1) * 4, :], ps[0:64, 0:256].rearrange("p (j v) -> p j v", j=4),
                         [ps], Sv[jg * 4:(jg + 1) * 4])
            else:
                K.memset("dve", Sst[:], 0.0, Sv)
            ysum = K.sb([128, nt, 256], F32, "rys")
            K.memset("pool", ysum[:], 0.0, [ysum])
            ctx = dict(nt=nt, latent=latent, rkv=rkv, logw=logw, aa=aa, kvec=kvec, kap=kap, Sst=Sst, Sv=Sv, ysum=ysum)
            with K.scope():
                for idx in range(nt):
                    K.interleave([self._rwkv_unit(0, idx, idx, ctx), self._rwkv_unit(1, nt - 1 - idx, idx, ctx)])
            self.head_norm(ysum, nt, True)
            bt = K.rot("ropeA", [128, nt, 256], F32, 1)
            K.tt("dve", bt[:], rkv[:, :, 0:256], rkv[:, :, 256:512], ALU.mult, [rkv], [bt])
            K.tt("dve", bt[:], bt[:], bc(kvec[:, 2, :].unsqueeze(1), [128, nt, 256]), ALU.mult, [bt, kvec], [bt])
            bs = K.sb([128, nt * 4], F32, "rbs")
            K.red(bs[:], bt[:].rearrange("p n (h d) -> p (n h) d", h=4), ALU.add, [bt], [bs])
            bt4 = bt[:].rearrange("p n (h d) -> p n h d", h=4)
            K.tt("dve", bt4, rkv[:, :, 512:768].rearrange("p n (h d) -> p n h d", h=4),
                 bc(bs[:].rearrange("p (n h) -> p n h", h=4).unsqueeze(3), [128, nt, 4, 64]), ALU.mult, [rkv, bs], [bt])
            K.tt("dve", ysum[:], ysum[:], bt[:], ALU.add, [ysum, bt], [ysum])
            Wg = K.sb([64, 256], F32, "rWg")
            K.dma("sp", Wg[:], I["w_g2"][l], [I["w_g2"]], [Wg])
            rg = K.sb([128, nt, 64], F32, "rrg")
            self.load_tok(rg, tok0, nt, COLS['r_g'], 64)
            K.act(rg[:], rg[:], AF.Sigmoid, [rg], [rg])
            for n in range(nt):
                rgT = K.rot("rrgT", [64, 128], F32, 2)
                self.transpose_block(rgT[:], rgT, rg[:, n, :], rg, 128, 64)
                ps = K.psum()
                K.mm(ps[:, 0:256], rgT[:], Wg[:], True, True, [rgT, Wg], [ps])
                K.tt("dve", ysum[:, n, :], ysum[:, n, :], ps[:, 0:256], ALU.mult, [ysum, ps], [ysum])
            self.store_branch(2, tok0, nt, ysum)
            if not latent:
                So = K.sb([64, 8, 64], F32, "rSo")
                for jg in range(2):
                    ps = K.psum()
                    for jj in range(4):
                        K.tr(ps[0:64, jj * 64:(jj + 1) * 64], Sst[:, jg * 4 + jj, :], self.ident[0:64, 0:64],
                             [Sv[jg * 4 + jj], self.ident], [ps])
                    K.cp("dve", So[:, jg * 4:(jg + 1) * 4, :], ps[0:64, 0:256].rearrange("p (j k) -> p j k", j=4),
                         [ps], [So])
                K.dma("sp", self.O["nr"][pb, l].rearrange("j v k -> v j k"), So[:], [So], [self.O["nr"]])

    def na_tables(self, l, Toe):
        K = self.K
        I = self.I
        with K.scope():
            z = K.sb([60, 128], F32, "naz")
            K.memset("dve", z[:], 0.0, [z])
            K.dma("sp", self.rpbpad[l], z[:], [z], [self.rpbpad])
            K.dma("sp", self.rpbpad[l, :, 48:79], I["rpb"][l], [I["rpb"]], [self.rpbpad])
            Tp = K.sb([64, 60, 64], F32, "naTp")
            src_ap = bass.AP(self.rpbpad.h, l * 60 * 128, [[1, 64], [128, 60], [1, 64]])
            K.dma("sp", Tp[:], src_ap, [self.rpbpad], [Tp])
            val = K.sb([64, 128], F32, "naval")
            K.S.op("pool", lambda e: e.iota(val[:], [[1, 128]], base=0, channel_multiplier=1,
                                             allow_small_or_imprecise_dtypes=True), [], [val.b])
            J2 = K.sb([64, 128], F32, "naJ2")
            J2b = K.sb([64, 128], F32, "naJ2b")
            K.ts("dve", J2[:], val[:], 63.0, None, ALU.is_equal, None, [val], [J2])
            K.ts("dve", J2b[:], val[:], 127.0, None, ALU.is_equal, None, [val], [J2b])
            K.tt("dve", J2[:], J2[:], J2b[:], ALU.add, [J2, J2b], [J2])
            ge64 = K.sb([128, 1], F32, "nage")
            K.ts("dve", ge64[:], self.pidx[:], 64.0, None, ALU.is_ge, None, [self.pidx], [ge64])
            cs = K.sb([128, 1], F32, "nacs")
            K.stt("dve", cs[:], ge64[:], -64.0, self.pidx[:], ALU.mult, ALU.add, [ge64, self.pidx], [cs])
            K.ts("dve", cs[:], cs[:], -8.0, 0.0, ALU.add, ALU.max, [cs], [cs])
            K.ts("dve", cs[:], cs[:], 48.0, None, ALU.min, None, [cs], [cs])
            dl = K.sb([128, 64], F32, "nadl")
            ok2 = K.sb([128, 64], F32, "naok2")
            K.ts("dve", dl[:], self.fidx[:, 0:64], cs[:, 0:1], None, ALU.subtract, None, [self.fidx, cs], [dl])
            K.ts("dve", ok2[:], dl[:], 16.0, None, ALU.is_lt, None, [dl], [ok2])
            K.ts("dve", dl[:], dl[:], 0.0, None, ALU.is_ge, None, [dl], [dl])
            K.tt("dve", dl[:], dl[:], ok2[:], ALU.mult, [dl, ok2], [dl])
            K.ts("dve", dl[:], dl[:], -NEG, NEG, ALU.mult, ALU.add, [dl], [dl])
            Tpf = Tp[:].rearrange("p j c -> p (j c)")
            Tf = Toe[:].rearrange("p h x -> p (h x)")
            for cb in range(8):
                c0 = cb * 512
                cw = min(512, 3840 - c0)
                ps = K.psum()
                K.mm(ps[:, 0:cw], J2[:], Tpf[:, c0:c0 + cw], True, True, [J2, Tp], [ps])
                K.tt("dve", Tf[:, c0:c0 + cw].rearrange("p (j c) -> p j c", c=64),
                     ps[:, 0:cw].rearrange("p (j c) -> p j c", c=64),
                     bc(dl[:].unsqueeze(1), [128, cw // 64, 64]), ALU.add, [ps, dl], [Toe])

    def na(self, l, tok0, T_, latent, pb):
        K = self.K
        I = self.I
        nt = T_ // 128
        C0 = COLS['n_qkv']
        with K.scope():
            q = K.sb([128, nt, 256], F32, "nq")
            k = K.sb([128, nt, 256], F32, "nk")
            v = K.sb([128, nt, 256], BF16, "nv")
            self.load_tok(q, tok0, nt, C0, 256)
            self.load_tok(k, tok0, nt, C0 + 256, 256)
            self.load_tok(v, tok0, nt, C0 + 512, 256, eng="pool")
            qT = K.sb([64, 4, T_], BF16, "nqT")
            kT = K.sb([64, 4, T_], BF16, "nkT")
            self.tok2feat(qT, q, nt)
            self.tok2feat(kT, k, nt)
            nout = K.sb([128, nt, 256], F32, "nout")
            if not latent:
                K.dma("sp", self.O["nak"][pb, l].rearrange("h s d -> s h d"),
                      self.proj[tok0:tok0 + 256, C0 + 256:C0 + 512].rearrange("s (h d) -> s h d", h=4),
                      [self.proj], [self.O["nak"]])
                K.dma("act", self.O["nav"][pb, l].rearrange("h s d -> s h d"),
                      self.proj[tok0:tok0 + 256, C0 + 512:C0 + 768].rearrange("s (h d) -> s h d", h=4),
                      [self.proj], [self.O["nav"]])
                for h in range(4):
                    hs = slice(h * 64, (h + 1) * 64)
                    for n in range(nt):
                        ps = K.psum()
                        K.mm(ps[:, 0:T_], qT[:, h, n * 128:(n + 1) * 128], kT[:, h, :], True, True, [qT, kT], [ps])
                        st = K.rot("nst", [128, 2], F32, 3)
                        K.red(st[:, 0:1], ps[:, 0:T_], ALU.max, [ps], [st])
                        K.ts("dve", st[:, 0:1], st[:, 0:1], -0.125, None, ALU.mult, None, [st], [st])
                        p = K.rot("np", [128, T_], F32, 2)
                        K.act(p[:], ps[:, 0:T_], AF.Exp, [ps, st], [p, st], bias=st[:, 0:1], scale=0.125,
                              accum=st[:, 1:2])
                        ps2 = K.psum()
                        for m in range(nt):
                            K.tr(ps2[:, m * 128:(m + 1) * 128], p[:, m * 128:(m + 1) * 128], self.ident[:],
                                 [p, self.ident], [ps2])
                        pT = K.rot("npT", [128, nt, 128], BF16, 2)
                        K.cp(K.evq(), pT[:], ps2[:, 0:T_].rearrange("p (m t) -> p m t", m=nt), [ps2], [pT])
                        acc = K.psum()
                        for m in range(nt):
                            K.mm(acc[:, 0:64], pT[:, m, :], v[:, m, hs], m == 0, m == nt - 1, [pT, v], [acc])
                        K.S.op("dve", lambda e, st=st: e.reciprocal(st[:, 1:2], st[:, 1:2]), [st.b], [st.b])
                        K.ts("dve", nout[:, n, hs], acc[:, 0:64], st[:, 1:2], None, ALU.mult, None, [acc, st], [nout])
            else:
                Toe = K.sb([128, 4, 960], F32, "naToe")
                self.na_tables(l, Toe)
                kc = K.sb([128, 2, 4, 64], F32, "nakc")
                vc = K.sb([128, 2, 4, 64], BF16, "navc")
                for c in range(2):
                    K.dma("sp", kc[:, c], I["cak"][l, :, c * 128:(c + 1) * 128, :].rearrange("h p d -> p h d"),
                          [I["cak"]], [kc])
                    K.dma("pool", vc[:, c], I["cav"][l, :, c * 128:(c + 1) * 128, :].rearrange("h p d -> p h d"),
                          [I["cav"]], [vc])
                kcT = K.sb([64, 4, 256], BF16, "nakcT")
                for c in range(2):
                    ps = K.psum()
                    for h in range(4):
                        K.tr(ps[0:64, h * 128:(h + 1) * 128], kc[:, c, h, :], self.ident[:], [kc, self.ident], [ps])
                    K.cp(K.evq(), kcT[:, :, c * 128:(c + 1) * 128], ps[0:64, :].rearrange("p (h t) -> p h t", h=4),
                         [ps], [kcT])
                for h in range(4):
                    hs = slice(h * 64, (h + 1) * 64)
                    for n in range(nt):
                        rs_ = [min(max(2 * n + hf - 4, 0), 8) for hf in range(2)]
                        kt0 = rs_[0] // 2
                        ntl = (rs_[1] + 7) // 2 - kt0 + 1
                        LW = ntl * 128
                        psA = K.psum()
                        psB = K.psum()
                        wa = min(LW, 512)
                        K.mm(psA[:, 0:wa], qT[:, h, n * 128:(n + 1) * 128], kT[:, h, kt0 * 128:kt0 * 128 + wa], True, True,
                             [qT, kT], [psA])
                        if LW > 512:
                            K.mm(psB[:, 0:LW - 512], qT[:, h, n * 128:(n + 1) * 128],
                                 kT[:, h, kt0 * 128 + 512:kt0 * 128 + LW], True, True, [qT, kT], [psB])
                        K.mm(psB[:, 128:384], qT[:, h, n * 128:(n + 1) * 128], kcT[:, h, :], True, True, [qT, kcT], [psB])
                        scb = K.rot("nscb", [128, 640 + 256], F32, 2)
                        K.memset("pool", scb[:, 0:LW], NEG, [scb])
                        for hf in range(2):
                            r = 2 * n + hf
                            a = (rs_[hf] - kt0 * 2) * 64
                            b = a + 512
                            tof = (rs_[hf] - r + 7) * 64
                            pp = slice(hf * 64, (hf + 1) * 64)
                            if a < 512:
                                e_ = min(b, 512)
                                K.stt("dve", scb[pp, a:e_], psA[pp, a:e_], 0.125, Toe[pp, h, tof:tof + (e_ - a)],
                                      ALU.mult, ALU.add, [psA, Toe], [scb])
                            if b > 512:
                                s_ = max(a, 512)
                                K.stt("dve", scb[pp, s_:b], psB[pp, s_ - 512:b - 512], 0.125,
                                      Toe[pp, h, tof + (s_ - a):tof + 512], ALU.mult, ALU.add, [psB, Toe], [scb])
                        K.ts("dve", scb[:, LW:LW + 256], psB[:, 128:384], 0.125, None, ALU.mult, None, [psB], [scb])
                        st = K.rot("nst", [128, 2], F32, 3)
                        K.red(st[:, 0:1], scb[:, 0:LW + 256], ALU.max, [scb], [st])
                        K.ts("dve", st[:, 0:1], st[:, 0:1], -1.0, None, ALU.mult, None, [st], [st])
                        K.act(scb[:, 0:LW + 256], scb[:, 0:LW + 256], AF.Exp, [scb, st], [scb, st], bias=st[:, 0:1],
                              accum=st[:, 1:2])
                        nb = ntl + 2
                        pT = K.rot("npT", [128, 7, 128], BF16, 2)
                        for g0 in range(0, nb, 4):
                            g1 = min(nb, g0 + 4)
                            ps2 = K.psum()
                            for c in range(g0, g1):
                                K.tr(ps2[:, (c - g0) * 128:(c - g0 + 1) * 128], scb[:, c * 128:(c + 1) * 128],
                                     self.ident[:], [scb, self.ident], [ps2])
                            K.cp(K.evq(), pT[:, g0:g1, :],
                                 ps2[:, 0:(g1 - g0) * 128].rearrange("p (m t) -> p m t", t=128), [ps2], [pT])
                        acc = K.psum()
                        for c in range(nb):
                            rhs = v[:, kt0 + c, hs] if c < ntl else vc[:, c - ntl, h, :]
                            K.mm(acc[:, 0:64], pT[:, c, :], rhs, c == 0, c == nb - 1, [pT, v, vc], [acc])
                        K.S.op("dve", lambda e, st=st: e.reciprocal(st[:, 1:2], st[:, 1:2]), [st.b], [st.b])
                        K.ts("dve", nout[:, n, hs], acc[:, 0:64], st[:, 1:2], None, ALU.mult, None, [acc, st], [nout])
            self.store_branch(3, tok0, nt, nout)

    def phaseC(self, l):
        for (tok0, T_, latent, pb) in self.seqs():
            if self.debug and "only_s" in self.debug and not latent:
                continue
            if self.debug and "only_p" in self.debug and (latent or pb != 0):
                continue
            for br in ("mlstm", "gla", "rwkv", "na"):
                if self.debug and ("br_" + br) not in self.debug and any(d.startswith("br_") for d in self.debug):
                    continue
                if hasattr(self, br):
                    getattr(self, br)(l, tok0, T_, latent, pb)

    def layer_norm(self, y, which_ln, lnb_t, out):
        K = self.K
        st = K.rot("ln_st", [128, 2, 6], F32, 2)
        for hh in range(2):
            K.S.op("dve", lambda e, hh=hh, st=st: e.bn_stats(st[:, hh, :], y[:, hh * 512:(hh + 1) * 512]), [y.b], [st.b])
        mv = K.rot("ln_mv", [128, 2], F32, 2)
        K.S.op("dve", lambda e, st=st, mv=mv: e.bn_aggr(mv[:], st[:].rearrange("p a b -> p (a b)")), [st.b], [mv.b])
        K.ts("dve", mv[:, 1:2], mv[:, 1:2], LN_EPS, None, ALU.add, None, [mv], [mv])
        K.act(mv[:, 1:2], mv[:, 1:2], AF.Sqrt, [mv], [mv])
        K.S.op("dve", lambda e, mv=mv: e.reciprocal(mv[:, 1:2], mv[:, 1:2]), [mv.b], [mv.b])
        nb = K.rot("ln_nb", [128, 1], F32, 2)
        K.stt("dve", nb[:], mv[:, 0:1], -1.0, mv[:, 1:2], ALU.mult, ALU.mult, [mv], [nb])
        K.act(y[:], y[:], AF.Identity, [y, mv, nb], [y], bias=nb[:, 0:1], scale=mv[:, 1:2])
        K.tt("dve", y[:], y[:], lnb_t[:, 0, :], ALU.mult, [y, lnb_t], [y])
        K.tt("dve", out[:], y[:], lnb_t[:, 1, :], ALU.add, [y, lnb_t], [out])

    def bcast_rows(self, dst_ap, dst_t, src_row_ap, src_t, width):
        self.K.dma("sp", dst_ap, bc(src_row_ap, [128, width]), [src_t], [dst_t])

    def phaseD(self, l, xsrc):
        K = self.K
        I = self.I
        with K.scope():
            wbr = K.sb([128, 4, 2, D], BF16, "wbr")
            for z in range(4):
                K.dma("pool", wbr[:, z], I["w_br"][l, z].rearrange("(k p) d -> p k d", p=128), [I["w_br"]], [wbr])
            wout = K.sb([128, 8, D], BF16, "wout")
            K.dma("pool", wout[:], I["w_out"][l].rearrange("(k p) d -> p k d", p=128), [I["w_out"]], [wout])
            wr = K.sb([128, 8, NEXP], F32, "wr")
            K.dma("sp", wr[:], I["w_router"][l].rearrange("(k p) e -> p k e", p=128), [I["w_router"]], [wr])
            g1b = K.sb([128, 2, D], F32, "g1b")
            sc2b = K.sb([128, 2, D], F32, "sc2b")
            sh2b = K.sb([128, 2, D], F32, "sh2b")
            for w_ in range(2):
                self.bcast_rows(g1b[:, w_, :], g1b, self.modrow[l, w_:w_ + 1, 2 * D:3 * D], self.modrow, D)
                self.bcast_rows(sh2b[:, w_, :], sh2b, self.modrow[l, w_:w_ + 1, 3 * D:4 * D], self.modrow, D)
                self.bcast_rows(sc2b[:, w_, :], sc2b, self.modrow[l, w_:w_ + 1, 4 * D:5 * D], self.modrow, D)
            K.ts("dve", sc2b[:], sc2b[:], 1.0, None, ALU.add, None, [sc2b], [sc2b])
            ln1 = K.sb([128, 2, D], F32, "ln1")
            self.bcast_rows(ln1[:, 0, :], ln1, I["ln_g"][l, 0:1, :], I["ln_g"], D)
            self.bcast_rows(ln1[:, 1, :], ln1, I["ln_b"][l, 0:1, :], I["ln_b"], D)
            W = (wbr, wout, wr, g1b, sc2b, sh2b, ln1)
            for i0_ in range(0, NT, 2):
                K.interleave([self._phaseD_tile(l, i0_, xsrc, W), self._phaseD_tile(l, i0_ + 1, xsrc, W)])

    def _phaseD_tile(self, l, i, xsrc, W):
        K = self.K
        wbr, wout, wr, g1b, sc2b, sh2b, ln1 = W
        MC = COLS['merge']
        px = f"_{i % 2}"
        wch = 0 if i < 8 else 1
        ts_ = slice(i * 128, (i + 1) * 128)
        gates = K.rot("dgates" + px, [128, 4 * D], BF16, 1)
        K.dma("pool", gates[:], self.proj[ts_, MC:MC + 4 * D], [self.proj], [gates])
        K.act(gates[:], gates[:], AF.Sigmoid, [gates], [gates])
        xt = K.rot("dxt" + px, [128, D], F32, 1)
        K.dma(K.dq(), xt[:], xsrc[ts_, :], [xsrc], [xt])
        msum = K.rot("dmsum" + px, [128, D], F32, 1)
        tmp = K.rot("dtmp" + px, [128, D], F32, 1)
        for z in range(4):
            for hh in range(2):
                cs_ = slice(hh * 512, (hh + 1) * 512)
                ps = K.psum()
                for kk in range(2):
                    K.mm(ps[:], self.brT[:, 2 * z + kk, ts_], wbr[:, z, kk, cs_], kk == 0, kk == 1,
                         [self.brT, wbr], [ps])
                gsl = gates[:, z * D + hh * 512:z * D + (hh + 1) * 512]
                if z == 0:
                    K.tt("dve", msum[:, cs_], ps[:], gsl, ALU.mult, [ps, gates], [msum])
                else:
                    K.tt("dve", tmp[:, cs_], ps[:], gsl, ALU.mult, [ps, gates], [tmp])
                    K.tt("dve", msum[:, cs_], msum[:, cs_], tmp[:, cs_], ALU.add, [msum, tmp], [msum])
        yield
        msT = K.rot("dmsT" + px, [128, 8, 128], BF16, 1)
        for kg in range(2):
            ps = K.psum()
            for kk in range(4):
                k = kg * 4 + kk
                K.tr(ps[:, kk * 128:(kk + 1) * 128], msum[:, k * 128:(k + 1) * 128], self.ident[:],
                     [msum, self.ident], [ps])
            K.cp(K.evq(), msT[:, kg * 4:(kg + 1) * 4, :], ps[:].rearrange("p (k t) -> p k t", k=4), [ps], [msT])
        yield
        y = K.rot("dy" + px, [128, D], F32, 1)
        for hh in range(2):
            cs_ = slice(hh * 512, (hh + 1) * 512)
            ps = K.psum()
            for k in range(8):
                K.mm(ps[:], msT[:, k, :], wout[:, k, cs_], k == 0, k == 7, [msT, wout], [ps])
            K.tt("dve", y[:, cs_], ps[:], g1b[:, wch, cs_], ALU.mult, [ps, g1b], [y])
        K.stt("dve", y[:], xt[:], ALPHA, y[:], ALU.mult, ALU.add, [xt, y], [y])
        yield
        x1 = K.rot("dx1" + px, [128, D], F32, 1)
        self.layer_norm(y, 0, ln1, x1)
        K.dma("sp", self.x1res[ts_, :], x1[:], [x1], [self.x1res])
        h2 = K.rot("dh2" + px, [128, D], F32, 1)
        K.tt("dve", h2[:], x1[:], sc2b[:, wch, :], ALU.mult, [x1, sc2b], [h2])
        K.tt("dve", h2[:], h2[:], sh2b[:, wch, :], ALU.add, [h2, sh2b], [h2])
        h2b = K.rot("dh2b" + px, [128, D], BF16, 1)
        K.cp("act", h2b[:], h2[:], [h2], [h2b])
        K.dma("act", self.h2res[ts_, :], h2b[:], [h2b], [self.h2res])
        yield
        h2T = K.rot("dh2T" + px, [128, 8, 128], F32, 1)
        for kg in range(2):
            ps = K.psum()
            for kk in range(4):
                k = kg * 4 + kk
                K.tr(ps[:, kk * 128:(kk + 1) * 128], h2[:, k * 128:(k + 1) * 128], self.ident[:],
                     [h2, self.ident], [ps])
            K.cp(K.evq(), h2T[:, kg * 4:(kg + 1) * 4, :], ps[:].rearrange("p (k t) -> p k t", k=4), [ps], [h2T])
        yield
        ps = K.psum()
        for k in range(8):
            K.mm(ps[:, 0:NEXP], h2T[:, k, :], wr[:, k, :], k == 0, k == 7, [h2T, wr], [ps])
        st = K.rot("dst" + px, [128, 2], F32, 1)
        K.red(st[:, 0:1], ps[:, 0:NEXP], ALU.max, [ps], [st])
        K.ts("dve", st[:, 0:1], st[:, 0:1], -1.0, None, ALU.mult, None, [st], [st])
        ex = K.rot("dex" + px, [128, NEXP], F32, 1)
        K.act(ex[:], ps[:, 0:NEXP], AF.Exp, [ps, st], [ex, st], bias=st[:, 0:1], accum=st[:, 1:2])
        K.S.op("dve", lambda e, st=st: e.reciprocal(st[:, 1:2], st[:, 1:2]), [st.b], [st.b])
        K.ts("dve", self.affall[:, i, :], ex[:], st[:, 1:2], None, ALU.mult, None, [ex, st], [self.affall])

    def phaseE(self, l, xdst):
        K = self.K
        I = self.I
        sets = [(s * 256, 256, 32) for s in range(4)] + [(1024, 1024, 128)]
        with K.scope():
            slot = K.sb([16, NTOK], F32, "eslot")
            gatev = K.sb([16, NTOK], F32, "egate")
            slotT = K.sb([128, NT, NEXP], F32, "eslotT")
            with K.scope():
                affT = K.sb([16, NTOK], F32, "eaffT")
                work = K.sb([16, NTOK], F32, "ework")
                cum = K.sb([16, NTOK], F32, "ecum")
                one16 = K.sb([16, 1024], F32, "eone")
                K.memset("dve", one16[:], 1.0, [one16])
                for ig in range(4):
                    ps = K.psum()
                    for ii in range(4):
                        i = ig * 4 + ii
                        K.tr(ps[0:16, ii * 128:(ii + 1) * 128], self.affall[:, i, :], self.ident[:],
                             [self.affall, self.ident], [ps])
                    K.cp("dve", affT[:, ig * 512:(ig + 1) * 512], ps[0:16, :], [ps], [affT])
                K.cp("act", work[:], affT[:], [affT], [work])
                for (c0, T_, cap) in sets:
                    for it in range(cap // 8):
                        m8 = K.rot("em8", [16, 8], F32, 2)
                        K.S.op("dve", lambda e, m8=m8, c0=c0, T_=T_: e.max(m8[:], work[:, c0:c0 + T_]), [work.b], [m8.b])
                        K.S.op("dve", lambda e, m8=m8, c0=c0, T_=T_: e.match_replace(
                            work[:, c0:c0 + T_], m8[:], work[:, c0:c0 + T_], -1.0), [work.b, m8.b], [work.b])
                K.ts("dve", work[:], work[:], 0.0, None, ALU.is_lt, None, [work], [work])
                K.tt("dve", gatev[:], affT[:], work[:], ALU.mult, [affT, work], [gatev])
                for (c0, T_, cap) in sets:
                    K.S.op("dve", lambda e, c0=c0, T_=T_: e.tensor_tensor_scan(
                        cum[:, c0:c0 + T_], one16[:, 0:T_], work[:, c0:c0 + T_], 0.0, ALU.mult, ALU.add),
                        [one16.b, work.b], [cum.b])
                K.ts("dve", cum[:], cum[:], 999.0, None, ALU.add, None, [cum], [cum])
                K.tt("dve", cum[:], cum[:], work[:], ALU.mult, [cum, work], [cum])
                K.ts("dve", slot[:], cum[:], -1000.0, None, ALU.add, None, [cum], [slot])
                ps = K.psum()
                for i in range(NT):
                    K.tr(ps[:, i * 16:(i + 1) * 16], slot[:, i * 128:(i + 1) * 128], self.ident[0:16, 0:16],
                         [slot, self.ident], [ps])
                K.cp("dve", slotT[:], ps[:, 0:256].rearrange("p (i e) -> p i e", e=16), [ps], [slotT])
            with K.scope():
                h2tok = K.sb([128, NT, D], BF16, "eh2")
                for ig in range(4):
                    K.dma(K.dq(), h2tok[:, ig * 4:(ig + 1) * 4, :],
                          self.h2res[ig * 512:(ig + 1) * 512, :].rearrange("(i p) d -> p i d", p=128), [self.h2res], [h2tok])
                for e_ in range(NEXP):
                    PTs = K.rot("ePTs", [128, 8, 128], BF16, 2)
                    PTp = K.rot("ePTp", [128, 8, 32], BF16, 2)
                    K.tt("dve", PTs[:], bc(self.fidx[:].unsqueeze(1), [128, 8, 128]),
                         bc(slotT[:, 8:16, e_].unsqueeze(2), [128, 8, 128]), ALU.is_equal, [self.fidx, slotT], [PTs])
                    K.tt("dve", PTp[:], bc(self.fidx[:, 0:32].unsqueeze(1), [128, 8, 32]),
                         bc(slotT[:, 0:8, e_].unsqueeze(2), [128, 8, 32]), ALU.is_equal, [self.fidx, slotT], [PTp])
                    xeT = K.rot("exeT", [128, 8, 256], BF16, 2)
                    for kg in range(2):
                        ps = K.psum()
                        ps2 = K.psum()
                        for kk in range(4):
                            k = kg * 4 + kk
                            for i in range(8, 16):
                                K.mm(ps[:, kk * 128:(kk + 1) * 128], h2tok[:, i, k * 128:(k + 1) * 128], PTs[:, i - 8, :],
                                     i == 8, i == 15, [h2tok, PTs], [ps])
                            for s in range(4):
                                for ii in range(2):
                                    i = 2 * s + ii
                                    K.mm(ps2[:, kk * 128 + s * 32:kk * 128 + (s + 1) * 32],
                                         h2tok[:, i, k * 128:(k + 1) * 128], PTp[:, i, :], ii == 0, ii == 1,
                                         [h2tok, PTp], [ps2])
                        K.cp("act", xeT[:, kg * 4:(kg + 1) * 4, 0:128], ps[:].rearrange("p (k c) -> p k c", k=4),
                             [ps], [xeT])
                        K.cp("dve", xeT[:, kg * 4:(kg + 1) * 4, 128:256], ps2[:].rearrange("p (k c) -> p k c", k=4),
                             [ps2], [xeT])
                    hmT = K.rot("ehmT", [128, 16, 256], BF16, 2)
                    for fg in range(4):
                        wu = K.rot("ewu", [128, 8, 2, 512], BF16, 2)
                        for ab in range(2):
                            c0 = ab * FF + fg * 512
                            K.dma("pool", wu[:, :, ab, :],
                                  I["w_up"][l, e_, :, c0:c0 + 512].rearrange("(k p) f -> p k f", p=128), [I["w_up"]], [wu])
                        for f4 in range(4):
                            fc = fg * 4 + f4
                            psa = K.psum()
                            psb = K.psum()
                            for k in range(8):
                                K.mm(psa[:, 0:256], wu[:, k, 0, f4 * 128:(f4 + 1) * 128], xeT[:, k, :], k == 0, k == 7,
                                     [wu, xeT], [psa])
                            for k in range(8):
                                K.mm(psb[:, 0:256], wu[:, k, 1, f4 * 128:(f4 + 1) * 128], xeT[:, k, :], k == 0, k == 7,
                                     [wu, xeT], [psb])
                            sa = K.rot("esa", [128, 256], F32, 2)
                            K.act(sa[:], psa[:, 0:256], AF.Silu, [psa], [sa])
                            K.tt("dve", hmT[:, fc, :], sa[:], psb[:, 0:256], ALU.mult, [sa, psb], [hmT])
                    yeb = K.rot("eyeb", [128, 2, D], BF16, 2)
                    for hh in range(2):
                        wd = K.rot("ewd", [128, 16, 512], BF16, 2)
                        K.dma("pool", wd[:], I["w_down"][l, e_, :, hh * 512:(hh + 1) * 512].rearrange("(k p) d -> p k d", p=128),
                              [I["w_down"]], [wd])
                        for grp in range(2):
                            ps = K.psum()
                            for fc in range(16):
                                K.mm(ps[:], hmT[:, fc, grp * 128:(grp + 1) * 128], wd[:, fc, :], fc == 0, fc == 15,
                                     [hmT, wd], [ps])
                            K.cp(K.evq(), yeb[:, grp, hh * 512:(hh + 1) * 512], ps[:], [ps], [yeb])
                    for grp in range(2):
                        K.dma(K.dq(), self.ye[grp, e_], yeb[:, grp, :], [yeb], [self.ye])
            with K.scope():
                g2b = K.sb([128, 2, D], F32, "g2b")
                for w_ in range(2):
                    self.bcast_rows(g2b[:, w_, :], g2b, self.modrow[l, w_:w_ + 1, 5 * D:6 * D], self.modrow, D)
                ln2 = K.sb([128, 2, D], F32, "ln2")
                self.bcast_rows(ln2[:, 0, :], ln2, I["ln_g"][l, 1:2, :], I["ln_g"], D)
                self.bcast_rows(ln2[:, 1, :], ln2, I["ln_b"][l, 1:2, :], I["ln_b"], D)
                cidx = K.sb([128, 5], F32, "ecidx")
                for s in range(4):
                    K.ts("dve", cidx[:, s:s + 1], self.pidx[:], -32.0 * s, None, ALU.add, None, [self.pidx], [cidx])
                K.cp("dve", cidx[:, 4:5], self.pidx[:], [self.pidx], [cidx])
                id16 = self.ident[0:16, 0:16]
                for grp in (1, 0):
                    yeg = K.rot("eyeg", [128, NEXP, D], BF16, 1)
                    for eg in range(4):
                        K.dma(K.dq(), yeg[:, eg * 4:(eg + 1) * 4, :],
                              self.ye[grp, eg * 4:(eg + 1) * 4].rearrange("e c d -> c e d"), [self.ye], [yeg])
                    tiles = range(8, 16) if grp == 0 else range(0, 8)
                    for i in tiles:
                        wch = 0 if i < 8 else 1
                        ts_ = slice(i * 128, (i + 1) * 128)
                        ccol = 4 if i >= 8 else i // 2
                        Rs = K.rot("eRs", [16, NEXP, 128], F32, 2)
                        Rg = K.rot("eRg", [16, NEXP, 128], F32, 2)
                        K.tt("dve", Rs[:], bc(slot[:, ts_].unsqueeze(1), [16, NEXP, 128]),
                             bc(id16.unsqueeze(2), [16, NEXP, 128]), ALU.mult, [slot, self.ident], [Rs])
                        K.tt("dve", Rg[:], bc(gatev[:, ts_].unsqueeze(1), [16, NEXP, 128]),
                             bc(id16.unsqueeze(2), [16, NEXP, 128]), ALU.mult, [gatev, self.ident], [Rg])
                        PTg = K.rot("ePTg", [128, NEXP, 128], BF16, 2)
                        for eg in range(4):
                            pss = K.psum()
                            psg = K.psum()
                            K.mm(pss[:], self.ones[0:16, :], Rs[:, eg * 4:(eg + 1) * 4, :].rearrange("p e t -> p (e t)"),
                                 True, True, [self.ones, Rs], [pss])
                            K.mm(psg[:], self.ones[0:16, :], Rg[:, eg * 4:(eg + 1) * 4, :].rearrange("p e t -> p (e t)"),
                                 True, True, [self.ones, Rg], [psg])
                            gsb = K.rot("egsb", [128, 512], F32, 2)
                            K.cp("act", gsb[:], psg[:], [psg], [gsb])
                            K.stt("dve", PTg[:, eg * 4:(eg + 1) * 4, :].rearrange("p e t -> p (e t)"), pss[:],
                                  cidx[:, ccol:ccol + 1], gsb[:], ALU.is_equal, ALU.mult, [pss, cidx, gsb], [PTg])
                        x1 = K.rot("ex1", [128, D], F32, 2)
                        K.dma(K.dq(), x1[:], self.x1res[ts_, :], [self.x1res], [x1])
                        y = K.rot("ey", [128, D], F32, 2)
                        for hh in range(2):
                            cs_ = slice(hh * 512, (hh + 1) * 512)
                            ps = K.psum()
                            for e_ in range(NEXP):
                                K.mm(ps[:], PTg[:, e_, :], yeg[:, e_, cs_], e_ == 0, e_ == NEXP - 1, [PTg, yeg], [ps])
                            K.tt("dve", y[:, cs_], ps[:], g2b[:, wch, cs_], ALU.mult, [ps, g2b], [y])
                        K.stt("dve", y[:], x1[:], ALPHA, y[:], ALU.mult, ALU.add, [x1, y], [y])
                        xo = K.rot("exo", [128, D], F32, 2)
                        self.layer_norm(y, 1, ln2, xo)
                        K.dma(K.dq(), xdst[ts_, :], xo[:], [xo], [xdst])

    def build(self):
        self.phase0_mods()
        if self.debug and "stop0" in self.debug:
            return self.finish()
        for l in range(DEPTH):
            xsrc = self.I["xin"] if l == 0 else self.xres[(l - 1) % 2]
            if not (self.debug and "projin" in self.debug):
                with self.K.scope():
                    self.hT = self.K.sb([128, 8, NTOK], BF16, "hT")
                    self.phaseA(l, xsrc)
                    self.phaseB(l)
            if self.debug and "stopB" in self.debug:
                return self.finish()
            if l == 0:
                self.affall = self.K.sb([128, NT, NEXP], F32, "affall")
            with self.K.scope():
                self.brT = self.K.sb([128, 8, NTOK], BF16, "brT")
                self.phaseC(l)
                if self.debug and "stopC" in self.debug:
                    return self.finish()
                self.phaseD(l, xsrc)
                if self.debug and "stopD" in self.debug:
                    return self.finish()
            xdst = self.O["y"] if l == DEPTH - 1 else self.xres[l % 2]
            self.phaseE(l, xdst)
            if self.debug and "stopE" in self.debug:
                return self.finish()
        return self.finish()

    def finish(self):
        self.K.S.emit()
        return self.nc


IN_SHAPES = {
    "xin": [NTOK, D], "cvec": [2, D],
    "sC": [DEPTH, 2, H, HD, HD], "sn": [DEPTH, 2, H, HD], "sm": [DEPTH, 8],
    "sg": [DEPTH, 2, H, HD, HD], "sr": [DEPTH, 2, H, HD, HD],
    "cak": [DEPTH, H, 256, HD], "cav": [DEPTH, H, 256, HD],
    "w_ada": [DEPTH, D, 6 * D], "b_ada": [DEPTH, 6 * D], "w_in": [DEPTH, D, NIN],
    "b_ig": [DEPTH, 8], "b_fg": [DEPTH, 8], "w_gla_a2": [DEPTH, 2, 16, MIXW], "b_gla_a": [DEPTH, 2, MIXW],
    "shift_rwkv": [DEPTH, 3, 768], "w0_rwkv": [DEPTH, 2, MIXW], "w_w2": [DEPTH, 2, 32, MIXW],
    "a0_rwkv": [DEPTH, 2, MIXW], "w_a2": [DEPTH, 2, 32, MIXW], "w_g2": [DEPTH, 64, MIXW],
    "k_k": [DEPTH, MIXW], "k_a": [DEPTH, MIXW], "r_k": [DEPTH, MIXW], "rpb": [DEPTH, 60, 31],
    "w_br": [DEPTH, 4, MIXW, D], "w_out": [DEPTH, D, D], "ln_g": [DEPTH, 2, D], "ln_b": [DEPTH, 2, D],
    "w_router": [DEPTH, D, NEXP], "w_up": [DEPTH, NEXP, D, 2 * FF], "w_down": [DEPTH, NEXP, FF, D],
}
OUT_SHAPES = {
    "y": [NTOK, D], "nC": [4, DEPTH, 8, HD, HD], "nn": [4, DEPTH, 8, HD], "nm": [4, DEPTH, 8],
    "ng": [4, DEPTH, 8, HD, HD], "nr": [4, DEPTH, 8, HD, HD],
    "nak": [4, DEPTH, H, 256, HD], "nav": [4, DEPTH, H, 256, HD],
}


def make_in_maps(inputs):
    f = lambda a: np.ascontiguousarray(np.asarray(a, dtype=np.float32))
    shared = {}
    for name in ("w_ada", "b_ada", "w_in", "w_gla_a2", "b_gla_a", "shift_rwkv", "w0_rwkv", "w_w2", "a0_rwkv",
                 "w_a2", "w_g2", "k_k", "k_a", "r_k", "w_br", "w_out", "ln_g", "ln_b", "w_router", "w_up", "w_down"):
        shared[name] = f(inputs[name])
    shared["b_ig"] = f(inputs["b_ig"]).reshape(DEPTH, 8)
    shared["b_fg"] = f(inputs["b_fg"]).reshape(DEPTH, 8)
    shared["rpb"] = f(inputs["rpb"]).reshape(DEPTH, 60, 31)
    xp = f(inputs["x_prompt"])
    xs = f(inputs["x_sample"])
    maps = []
    for i in range(NCORES):
        m = dict(shared)
        m["xin"] = np.ascontiguousarray(np.concatenate([xp[4 * i:4 * i + 4].reshape(1024, D), xs[i]], axis=0))
        m["cvec"] = np.ascontiguousarray(np.stack([f(inputs["c_ctx"]), f(inputs["c"])[i]], axis=0))
        m["sC"] = f(inputs["state_mlstm_C"][i])
        m["sn"] = f(inputs["state_mlstm_n"][i])
        m["sm"] = f(inputs["state_mlstm_m"][i]).reshape(DEPTH, 8)
        m["sg"] = f(inputs["state_gla"][i])
        m["sr"] = f(inputs["state_rwkv"][i])
        m["cak"] = f(inputs["cache_na_k"][i])
        m["cav"] = f(inputs["cache_na_v"][i])
        maps.append(m)
    return maps


def kernel(**inputs):
    prog = Prog()
    nc = prog.build()
    maps = make_in_maps(inputs)
    res = run_bass_kernel_spmd(nc, maps, core_ids=list(range(NCORES)))
    R = res.results
    y = np.stack([r["y"] for r in R], 0)
    y_prompt = y[:, :1024].reshape(32, 256, D)
    y_sample = y[:, 1024:].reshape(8, 1024, D)
    cat = lambda k: np.concatenate([r[k] for r in R], 0)
    nC = cat("nC").reshape(32, DEPTH, 2, H, HD, HD)
    nn = cat("nn").reshape(32, DEPTH, 2, H, HD)
    nm = cat("nm").reshape(32, DEPTH, 2, H)
    ng = cat("ng").reshape(32, DEPTH, 2, H, HD, HD)
    nr = cat("nr").reshape(32, DEPTH, 2, H, HD, HD)
    nak = cat("nak")
    nav = cat("nav")
    return tuple(np.ascontiguousarray(a, dtype=np.float32) for a in (y_prompt, y_sample, nC, nn, nm, ng, nr, nak, nav))
```

```python
import math
from contextlib import ExitStack, contextmanager
import numpy as np
import concourse.bass as bass
import concourse.mybir as mybir
from concourse.bass_utils import run_bass_kernel_spmd

F32 = mybir.dt.float32
BF16 = mybir.dt.bfloat16
AF = mybir.ActivationFunctionType
ALU = mybir.AluOpType
AX = mybir.AxisListType

ENGS = ("pe", "act", "dve", "pool", "sp")
N_DMA_SLOTS = 12

NCORES = 8
DEPTH = 2
D = 1024
NT = 16
NTOK = 2048
NIN = 7920
H = 4
HD = 64
MIXW = 256
NEXP = 16
FF = 2048
LN_EPS = 1e-5
ALPHA = (2 * DEPTH) ** 0.25
NEG = -30000.0

COLS = {}
_off = 0
for _n, _w in (('m_q', 256), ('m_k', 256), ('m_v', 256), ('m_o', 256), ('m_i', 8), ('m_f', 8),
               ('g_q', 256), ('g_k', 256), ('g_v', 256), ('g_g', 256), ('g_a', 32),
               ('r_rkv', 768), ('r_w', 64), ('r_a', 64), ('r_g', 64), ('n_qkv', 768), ('merge', 4096)):
    COLS[_n] = _off
    _off += _w
assert _off == NIN


class Buf:
    __slots__ = ("name", "last_w", "readers")

    def __init__(self, name=""):
        self.name = name
        self.last_w = None
        self.readers = []


class Sched:
    def __init__(self, nc):
        self.nc = nc
        self.q = {e: [] for e in ENGS}
        self.cnt = {e: 0 for e in ENGS}
        self.sems = {}
        self.seen = {e: {} for e in ENGS}
        self.pending = {e: [] for e in ENGS}
        self.dma_slot = {e: 0 for e in ENGS}
        self.dma_cnt = {}
        for e in ENGS:
            self.sems[("c", e)] = nc.alloc_semaphore(name=f"c_{e}")
        for e in ("sp", "act", "pool"):
            for s in range(N_DMA_SLOTS):
                self.sems[("d", e, s)] = nc.alloc_semaphore(name=f"d_{e}_{s}")
                self.dma_cnt[(e, s)] = 0
        self.n_ops = 0

    def _collect(self, eng, reads, writes, extra):
        need = {}

        def add(tok):
            if tok is None:
                return
            sid, val = tok
            if need.get(sid, 0) < val:
                need[sid] = val
        for b in reads:
            add(b.last_w)
        for b in writes:
            add(b.last_w)
            for t in b.readers:
                add(t)
        for t in extra:
            add(t)
        for t in self.pending[eng]:
            add(t)
        self.pending[eng] = []
        waits = []
        seen = self.seen[eng]
        for sid, val in need.items():
            if sid == ("c", "pe") and eng == "pe":
                continue
            if seen.get(sid, 0) >= val:
                continue
            seen[sid] = val
            waits.append((sid, val))
        return waits

    def _commit(self, tok, reads, writes):
        for b in reads:
            b.readers.append(tok)
            if len(b.readers) > 64:
                mx = {}
                for sid, val in b.readers:
                    if mx.get(sid, 0) < val:
                        mx[sid] = val
                b.readers = list(mx.items())
        for b in writes:
            b.last_w = tok
            b.readers = []

    def op(self, eng, fn, reads=(), writes=()):
        waits = self._collect(eng, reads, writes, ())
        self.cnt[eng] += 1
        tok = (("c", eng), self.cnt[eng])
        self.q[eng].append((waits, fn, ("c", eng), 1))
        self._commit(tok, reads, writes)
        self.n_ops += 1
        return tok

    def dma(self, eng, out_ap, in_ap, reads=(), writes=(), **kw):
        slot = self.dma_slot[eng]
        self.dma_slot[eng] = (slot + 1) % N_DMA_SLOTS
        sid = ("d", eng, slot)
        prev = self.dma_cnt[(eng, slot)]
        extra = [(sid, 16 * prev)] if prev > 0 else []
        waits = self._collect(eng, reads, writes, extra)
        self.dma_cnt[(eng, slot)] = prev + 1
        tok = (sid, 16 * (prev + 1))

        def fn(e, out_ap=out_ap, in_ap=in_ap, kw=kw):
            return e.dma_start(out=out_ap, in_=in_ap, **kw)
        self.q[eng].append((waits, fn, sid, 16))
        self._commit(tok, reads, writes)
        self.n_ops += 1
        return tok

    def all_tokens(self):
        toks = []
        for e in ENGS:
            if self.cnt[e] > 0:
                toks.append((("c", e), self.cnt[e]))
        for (e, s), c in self.dma_cnt.items():
            if c > 0:
                toks.append((("d", e, s), 16 * c))
        return toks

    def barrier(self):
        toks = self.all_tokens()
        for e in ENGS:
            self.pending[e] = list(toks)

    def emit(self):
        nc = self.nc
        final_waits = self.all_tokens()
        engmap = {"pe": "tensor", "act": "scalar", "dve": "vector", "pool": "gpsimd", "sp": "sync"}
        with nc.Block() as block:
            for e in ENGS:
                items = self.q[e]
                fw = final_waits if e == "sp" else []
                sems = self.sems

                def body(engobj, items=items, fw=fw, sems=sems):
                    for waits, fn, sid, inc in items:
                        for (wsid, val) in waits:
                            engobj.wait_ge(sems[wsid], val)
                        ins = fn(engobj)
                        ins.then_inc(sems[sid], inc)
                    for (wsid, val) in fw:
                        engobj.wait_ge(sems[wsid], val)
                getattr(block, engmap[e])(body)


class T:
    __slots__ = ("h", "b", "psum")

    def __init__(self, h, name="", psum=False):
        self.h = h
        self.b = Buf(name)
        self.psum = psum


    def __getitem__(self, k):
        return self.h[k]


def _rw(r, w):
    rb = [t.b for t in r if not t.psum]
    wb = [t.b for t in w] + [t.b for t in r if t.psum]
    return rb, wb


class KB:
    def __init__(self, nc):
        self.nc = nc
        self.S = Sched(nc)
        self.stacks = []
        self.rots = []
        self.uid = 0
        self.ps = [T(nc.alloc_psum_tensor(f"psb{i}", [128, 512], F32), f"ps{i}", psum=True) for i in range(8)]
        self.ps_i = 0
        self.ev_i = 0
        self.dq_i = 0

    @contextmanager
    def scope(self, active=True):
        if not active:
            yield
            return
        st = ExitStack()
        self.stacks.append(st)
        self.rots.append({})
        try:
            yield
        finally:
            self.S.barrier()
            self.stacks.pop()
            self.rots.pop()
            st.close()

    def sb(self, shape, dtype=F32, name=None):
        self.uid += 1
        nm = f"{name or 't'}_{self.uid}"
        if self.stacks:
            h = self.stacks[-1].enter_context(self.nc.sbuf_tensor(nm, list(shape), dtype))
        else:
            h = self.nc.alloc_sbuf_tensor(nm, list(shape), dtype)
        return T(h, nm)

    def rot(self, key, shape, dtype=F32, n=2):
        d = self.rots[-1]
        if key not in d:
            d[key] = [[self.sb(shape, dtype, key) for _ in range(n)], 0]
        lst, i = d[key]
        d[key][1] = (i + 1) % n
        return lst[i]

    def psum(self, hold=False, nbanks=6):
        held = getattr(self, "held", None)
        if held is None:
            held = self.held = set()
        for _ in range(8):
            i = self.ps_i
            self.ps_i = (self.ps_i + 1) % nbanks
            if i not in held:
                if hold:
                    held.add(i)
                return self.ps[i]
        raise RuntimeError("no free PSUM bank")

    def psfree(self, t):
        self.held.discard(self.ps.index(t))

    @staticmethod
    def interleave(gens):
        gens = list(gens)
        while gens:
            for g in list(gens):
                try:
                    next(g)
                except StopIteration:
                    gens.remove(g)

    def psacc(self):
        self.acc_i = 1 - getattr(self, "acc_i", 0)
        return self.ps[6 + self.acc_i]

    def evq(self):
        self.ev_i += 1
        return "act" if self.ev_i % 2 else "dve"

    def dq(self):
        self.dq_i += 1
        return "sp" if self.dq_i % 2 else "act"

    def mm(self, out, lhsT, rhs, start, stop, r, w):
        self.S.op("pe", lambda e: e.matmul(out, lhsT=lhsT, rhs=rhs, start=start, stop=stop),
                  *_rw(r, w))

    def tr(self, out, in_, ident, r, w):
        self.S.op("pe", lambda e: e.transpose(out, in_, ident), *_rw(r, w))

    def act(self, out, in_, func, r, w, bias=None, scale=None, accum=None):
        kw = {}
        if bias is not None:
            kw["bias"] = bias
        if scale is not None:
            kw["scale"] = scale
        if accum is not None:
            kw["accum_out"] = accum
        self.S.op("act", lambda e: e.activation(out, in_, func, **kw), *_rw(r, w))

    def tt(self, eng, out, a, b, op, r, w):
        self.S.op(eng, lambda e: e.tensor_tensor(out, a, b, op), *_rw(r, w))

    def ts(self, eng, out, a, s1, s2, op0, op1, r, w):
        if op1 is None:
            self.S.op(eng, lambda e: e.tensor_scalar(out, a, s1, None, op0), *_rw(r, w))
        else:
            self.S.op(eng, lambda e: e.tensor_scalar(out, a, s1, s2, op0, op1), *_rw(r, w))

    def stt(self, eng, out, in0, scalar, in1, op0, op1, r, w):
        self.S.op(eng, lambda e: e.scalar_tensor_tensor(out, in0, scalar, in1, op0, op1),
                  *_rw(r, w))

    def cp(self, eng, out, in_, r, w):
        if eng == "act":
            self.S.op("act", lambda e: e.copy(out, in_), *_rw(r, w))
        else:
            self.S.op(eng, lambda e: e.tensor_copy(out, in_), *_rw(r, w))

    def red(self, out, in_, op, r, w, axis=AX.X):
        self.S.op("dve", lambda e: e.tensor_reduce(out, in_, axis, op), *_rw(r, w))

    def memset(self, eng, ap, val, w):
        self.S.op(eng, lambda e: e.memset(ap, val), [], [t.b for t in w])

    def dma(self, eng, out, in_, r, w, **kw):
        self.S.dma(eng, out, in_, [t.b for t in r], [t.b for t in w], **kw)


def bc(ap, shape):
    return ap.to_broadcast(list(shape))


class Prog:
    def __init__(self, debug=None):
        self.debug = debug
        nc = bass.Bass("TRN2", target_bir_lowering=False)
        self.nc = nc
        self.K = KB(nc)

        def inp(name, shape):
            return T(nc.dram_tensor(name, list(shape), F32, kind="ExternalInput"), name)

        def outp(name, shape):
            return T(nc.dram_tensor(name, list(shape), F32, kind="ExternalOutput"), name)

        def scr(name, shape, dtype=F32):
            kind = "ExternalOutput" if (debug and name in debug) else "Internal"
            return T(nc.dram_tensor(name, list(shape), dtype, kind=kind), name)
        self.I = {}
        for name, shape in IN_SHAPES.items():
            self.I[name] = inp(name, shape)
        self.O = {}
        for name, shape in OUT_SHAPES.items():
            self.O[name] = outp(name, shape)
        self.modrow = scr("modrow", [DEPTH, 2, 6 * D])
        if debug and "projin" in debug:
            self.proj = inp("proj", [NTOK, NIN])
        else:
            self.proj = scr("proj", [NTOK, NIN])
        self.brdbg = scr("brdbg", [4, NTOK, MIXW]) if (debug and "brdbg" in debug) else None
        self.xres = [scr("xres0", [NTOK, D]), scr("xres1", [NTOK, D])]
        self.x1res = scr("x1res", [NTOK, D])
        self.rpbpad = scr("rpbpad", [DEPTH, 60, 128])
        self.ye = scr("ye", [2, NEXP, 128, D], BF16)
        self.h2res = scr("h2res", [NTOK, D], BF16)
        self.consts()

    def consts(self):
        K = self.K
        io = K.sb([128, 128], F32, "io")
        K.S.op("pool", lambda e: e.iota(io[:], [[1, 128]], base=0, channel_multiplier=-1,
                                         allow_small_or_imprecise_dtypes=True), [], [io.b])
        self.ident = K.sb([128, 128], F32, "ident")
        self.U = K.sb([128, 128], F32, "U")
        self.Lo = K.sb([128, 128], F32, "Lo")
        self.Us = K.sb([128, 128], F32, "Us")
        self.Ls = K.sb([128, 128], F32, "Ls")
        self.ones = K.sb([128, 128], F32, "ones")
        for t, op in ((self.ident, ALU.is_equal), (self.U, ALU.is_ge), (self.Lo, ALU.is_le),
                      (self.Us, ALU.is_gt), (self.Ls, ALU.is_lt)):
            K.S.op("dve", lambda e, t=t, op=op: e.tensor_single_scalar(t[:], io[:], 0.0, op), [io.b], [t.b])
        K.memset("dve", self.ones[:], 1.0, [self.ones])
        self.pidx = K.sb([128, 1], F32, "pidx")
        K.S.op("pool", lambda e: e.iota(self.pidx[:], [[0, 1]], base=0, channel_multiplier=1,
                                         allow_small_or_imprecise_dtypes=True), [], [self.pidx.b])
        self.fidx = K.sb([128, 128], F32, "fidx")
        K.S.op("pool", lambda e: e.iota(self.fidx[:], [[1, 128]], base=0, channel_multiplier=0,
                                         allow_small_or_imprecise_dtypes=True), [], [self.fidx.b])
        self.identb = K.sb([128, 128], BF16, "identb")
        K.cp("dve", self.identb[:], self.ident[:], [self.ident], [self.identb])
        self.sel8 = K.sb([8, 8, 128], F32, "sel8")
        K.cp("dve", self.sel8[:], bc(self.ident[0:8, 0:8].unsqueeze(2), [8, 8, 128]), [self.ident], [self.sel8])
        self.modT = K.sb([128, DEPTH, 2, 48], F32, "modT")
        self.rope_tables()

    def transpose_block(self, dst_ap, dst_t, src_ap, src_t, pin, fin, eng=None):
        K = self.K
        ps = K.psum()
        K.tr(ps[0:fin, 0:pin], src_ap, self.ident[0:pin, 0:pin], [src_t, self.ident], [ps])
        K.cp(eng or K.evq(), dst_ap, ps[0:fin, 0:pin], [ps], [dst_t])

    def phase0_mods(self):
        K = self.K
        I = self.I
        with K.scope():
            cv = K.sb([16, 128], F32, "cv")
            for wch in range(2):
                K.dma("sp", cv[wch:16:2, :], I["cvec"][wch].rearrange("(k p) -> k p", p=128), [I["cvec"]], [cv])
            sg = K.sb([16, 128], F32, "sg")
            K.act(sg[:], cv[:], AF.Sigmoid, [cv], [sg])
            K.tt("dve", cv[:], cv[:], sg[:], ALU.mult, [cv, sg], [cv])
            condT = K.sb([128, 16], F32, "condT")
            self.transpose_block(condT[:], condT, cv[:], cv, 16, 128)
            mrow = K.sb([2, 6 * D], F32, "mrow")
            bada = K.sb([2, 6 * D], F32, "bada")
            for l in range(DEPTH):
                K.dma("sp", bada[:], bc(I["b_ada"][l:l + 1, :], [2, 6 * D]), [I["b_ada"]], [bada])
                for cb in range(12):
                    wt = K.rot("wada", [128, 8, 512], F32, 2)
                    K.dma(K.dq(), wt[:], I["w_ada"][l, :, cb * 512:(cb + 1) * 512].rearrange("(k p) f -> p k f", p=128),
                          [I["w_ada"]], [wt])
                    ps = K.psum()
                    for k in range(8):
                        K.mm(ps[0:2, :], condT[:, 2 * k:2 * k + 2], wt[:, k, :], k == 0, k == 7, [condT, wt], [ps])
                    K.tt("dve", mrow[:, cb * 512:(cb + 1) * 512], ps[0:2, :], bada[:, cb * 512:(cb + 1) * 512],
                         ALU.add, [ps, bada], [mrow])
                K.dma("sp", self.modrow[l], mrow[:], [mrow], [self.modrow])
                ps = K.psum()
                for c in range(48):
                    K.tr(ps[:, 2 * c:2 * c + 2], mrow[0:2, c * 128:(c + 1) * 128], self.ident[0:2, 0:2],
                         [mrow, self.ident], [ps])
                K.cp("dve", self.modT[:, l, :, :], ps[:, 0:96].rearrange("p (c w) -> p w c", w=2), [ps], [self.modT])

    def phaseA(self, l, xsrc):
        K = self.K
        with K.scope():
            sc1p = K.sb([128, 2, 8], F32, "sc1p")
            K.ts("dve", sc1p[:], self.modT[:, l, :, 8:16], 1.0, None, ALU.add, None, [self.modT], [sc1p])
            for i in range(NT):
                wch = 0 if i < 8 else 1
                xt = K.rot("xt", [128, D], F32, 3)
                K.dma(K.dq(), xt[:], xsrc[i * 128:(i + 1) * 128, :], [xsrc], [xt])
                for kg in range(2):
                    ps = K.psum()
                    for kk in range(4):
                        k = kg * 4 + kk
                        K.tr(ps[:, kk * 128:(kk + 1) * 128], xt[:, k * 128:(k + 1) * 128], self.ident[:],
                             [xt, self.ident], [ps])
                    for kk in range(4):
                        k = kg * 4 + kk
                        if kk % 2 == 0:
                            K.act(self.hT[:, k, i * 128:(i + 1) * 128], ps[:, kk * 128:(kk + 1) * 128], AF.Identity,
                                  [ps, sc1p, self.modT], [self.hT],
                                  bias=self.modT[:, l, wch, k:k + 1], scale=sc1p[:, wch, k:k + 1])
                        else:
                            K.ts("dve", self.hT[:, k, i * 128:(i + 1) * 128], ps[:, kk * 128:(kk + 1) * 128],
                                 sc1p[:, wch, k:k + 1], self.modT[:, l, wch, k:k + 1], ALU.mult, ALU.add,
                                 [ps, sc1p, self.modT], [self.hT])

    def phaseB(self, l):
        K = self.K
        I = self.I
        with K.scope():
            ncb = (NIN + 511) // 512
            for cb in range(ncb):
                c0 = cb * 512
                cw = min(512, NIN - c0)
                wt = K.rot("win", [128, 8, 512], BF16, 3)
                K.dma("pool", wt[:, :, 0:cw], I["w_in"][l, :, c0:c0 + cw].rearrange("(k p) f -> p k f", p=128),
                      [I["w_in"]], [wt])
                for i in range(NT):
                    ps = K.psum()
                    for k in range(8):
                        K.mm(ps[:, 0:cw], self.hT[:, k, i * 128:(i + 1) * 128], wt[:, k, 0:cw], k == 0, k == 7,
                             [self.hT, wt], [ps])
                    st = K.rot("pst", [128, 512], F32, 4)
                    K.cp(K.evq(), st[:, 0:cw], ps[:, 0:cw], [ps], [st])
                    K.dma(K.dq(), self.proj[i * 128:(i + 1) * 128, c0:c0 + cw], st[:, 0:cw], [st], [self.proj])

    def seqs(self):
        return [(s * 256, 256, False, s) for s in range(4)] + [(1024, 1024, True, None)]

    def rope_tables(self):
        K = self.K
        self.cosF = K.sb([128, 8, 256], F32, "cosF")
        self.sinF = K.sb([128, 8, 256], F32, "sinF")
        with K.scope():
            invf = K.sb([128, 16], F32, "invf")
            K.act(invf[:], self.fidx[:, 0:16], AF.Exp, [self.fidx], [invf], scale=-math.log(10000.0) / 16.0)
            ge64 = K.sb([128, 1], F32, "ge64")
            K.ts("dve", ge64[:], self.pidx[:], 64.0, None, ALU.is_ge, None, [self.pidx], [ge64])
            pcol = K.sb([128, 1], F32, "pcol")
            K.stt("dve", pcol[:], ge64[:], -64.0, self.pidx[:], ALU.mult, ALU.add, [ge64, self.pidx], [pcol])
            rown = K.sb([128, 8], F32, "rown")
            K.S.op("pool", lambda e: e.iota(rown[:], [[2, 8]], base=0, channel_multiplier=0,
                                             allow_small_or_imprecise_dtypes=True), [], [rown.b])
            K.ts("dve", rown[:], rown[:], ge64[:, 0:1], None, ALU.add, None, [rown, ge64], [rown])
            ang = K.sb([128, 8, 2, 16], F32, "ang")
            for n in range(8):
                K.ts("dve", ang[:, n, 0, :], invf[:], rown[:, n:n + 1], None, ALU.mult, None, [invf, rown], [ang])
                K.ts("dve", ang[:, n, 1, :], invf[:], pcol[:, 0:1], None, ALU.mult, None, [invf, pcol], [ang])
            us = K.sb([128, 8, 2, 16], F32, "us")
            uc = K.sb([128, 8, 2, 16], F32, "uc")
            K.cp("dve", us[:], ang[:], [ang], [us])
            K.ts("dve", uc[:], ang[:], 0.5 * math.pi, None, ALU.add, None, [ang], [uc])
            ki = K.sb([128, 8, 2, 16], mybir.dt.int32, "ki")
            kf = K.sb([128, 8, 2, 16], F32, "kf")
            for u in (us, uc):
                K.ts("dve", kf[:], u[:], 1.0 / (2 * math.pi), None, ALU.mult, None, [u], [kf])
                K.cp("dve", ki[:], kf[:], [kf], [ki])
                K.cp("dve", kf[:], ki[:], [ki], [kf])
                K.stt("dve", u[:], kf[:], -2 * math.pi, u[:], ALU.mult, ALU.add, [kf, u], [u])
                K.ts("dve", kf[:], u[:], math.pi, -2 * math.pi, ALU.is_gt, ALU.mult, [u], [kf])
                K.tt("dve", u[:], u[:], kf[:], ALU.add, [u, kf], [u])
                K.ts("dve", kf[:], u[:], -math.pi, 2 * math.pi, ALU.is_lt, ALU.mult, [u], [kf])
                K.tt("dve", u[:], u[:], kf[:], ALU.add, [u, kf], [u])
                K.ts("dve", u[:], u[:], math.pi, -math.pi, ALU.min, ALU.max, [u], [u])
                K.act(u[:], u[:], AF.Sin, [u], [u])
            cF = self.cosF[:].rearrange("p n (h a b f) -> p n h a b f", h=4, a=2, b=2)
            sF = self.sinF[:].rearrange("p n (h a b f) -> p n h a b f", h=4, a=2, b=2)
            for h in range(4):
                for b in range(2):
                    K.cp("dve", cF[:, :, h, :, b, :], uc[:], [uc], [self.cosF])
                    K.ts("dve", sF[:, :, h, :, b, :], us[:], (-1.0 if b == 0 else 1.0), None, ALU.mult, None,
                         [us], [self.sinF])

    def rope(self, X, nt):
        K = self.K
        tmp = K.rot("ropeA", [128, nt, 256], F32, 1)
        t2 = K.rot("ropeB", [128, nt, 256], F32, 1)
        K.tt("dve", tmp[:], X[:], self.cosF[:, 0:nt, :], ALU.mult, [X, self.cosF], [tmp])
        X5 = X[:].rearrange("p n (g b f) -> p (n g) b f", b=2, f=16)
        S5 = self.sinF[:, 0:nt, :].rearrange("p n (g b f) -> p (n g) b f", b=2, f=16)
        T5 = t2[:].rearrange("p n (g b f) -> p (n g) b f", b=2, f=16)
        K.tt("pool", T5[:, :, 0, :], X5[:, :, 1, :], S5[:, :, 0, :], ALU.mult, [X, self.sinF], [t2])
        K.tt("pool", T5[:, :, 1, :], X5[:, :, 0, :], S5[:, :, 1, :], ALU.mult, [X, self.sinF], [t2])
        K.tt("dve", X[:], tmp[:], t2[:], ALU.add, [tmp, t2], [X])

    def load_tok(self, dst, tok0, nt, col0, width, eng=None):
        K = self.K
        K.dma(eng or K.dq(), dst[:, 0:nt, 0:width],
              self.proj[tok0:tok0 + nt * 128, col0:col0 + width].rearrange("(n p) c -> p n c", p=128),
              [self.proj], [dst])

    def tok2feat(self, dstT, src, nt, col0=0):
        K = self.K
        for n in range(nt):
            ps = K.psum()
            for h in range(4):
                K.tr(ps[0:64, h * 128:(h + 1) * 128], src[:, n, col0 + h * 64:col0 + (h + 1) * 64], self.ident[:],
                     [src, self.ident], [ps])
            K.cp(K.evq(), dstT[:, :, n * 128:(n + 1) * 128], ps[0:64, :].rearrange("p (h t) -> p h t", h=4),
                 [ps], [dstT])

    def head_norm(self, X, nt, centre):
        K = self.K
        G = nt * 4
        X3 = X[:].rearrange("p n (h d) -> p (n h) d", h=4)
        st = K.rot("hn_st", [128, G], F32, 2)
        if centre:
            K.red(st[:], X3, ALU.add, [X], [st])
            K.ts("dve", st[:], st[:], -1.0 / 64.0, None, ALU.mult, None, [st], [st])
            K.tt("dve", X3, X3, bc(st[:].unsqueeze(2), [128, G, 64]), ALU.add, [X, st], [X])
        sq = K.rot("ropeA", [128, nt, 256], F32, 1)
        K.tt("pool", sq[:], X[:], X[:], ALU.mult, [X], [sq])
        ms = K.rot("hn_ms", [128, G], F32, 2)
        K.red(ms[:], sq[:].rearrange("p n (h d) -> p (n h) d", h=4), ALU.add, [sq], [ms])
        K.ts("dve", ms[:], ms[:], 1.0 / 64.0, LN_EPS, ALU.mult, ALU.add, [ms], [ms])
        K.act(ms[:], ms[:], AF.Sqrt, [ms], [ms])
        K.S.op("dve", lambda e, ms=ms: e.reciprocal(ms[:], ms[:]), [ms.b], [ms.b])
        K.tt("dve", X3, X3, bc(ms[:].unsqueeze(2), [128, G, 64]), ALU.mult, [X, ms], [X])

    def store_branch(self, z, tok0, nt, src):
        K = self.K
        for n in range(nt):
            ps = K.psum()
            for kk in range(2):
                K.tr(ps[:, kk * 128:(kk + 1) * 128], src[:, n, kk * 128:(kk + 1) * 128], self.ident[:],
                     [src, self.ident], [ps])
            t0 = tok0 + n * 128
            K.cp(K.evq(), self.brT[:, 2 * z:2 * z + 2, t0:t0 + 128], ps[:, 0:256].rearrange("p (k t) -> p k t", k=2),
                 [ps], [self.brT])
        if self.brdbg is not None:
            K.dma("sp", self.brdbg[z, tok0:tok0 + nt * 128, :].rearrange("(n p) c -> p n c", p=128), src[:, 0:nt, :],
                  [src], [self.brdbg])

    def mlstm(self, l, tok0, T_, latent, pb):
        K = self.K
        I = self.I
        nt = T_ // 128
        LN8 = math.log(0.125)
        with K.scope():
            q = K.sb([128, nt, 256], F32, "mq")
            k = K.sb([128, nt, 256], F32, "mk")
            og = K.sb([128, nt, 256], F32, "mo")
            vaug = K.sb([128, nt, 4, 65], F32, "mv")
            ifg = K.sb([128, nt, 16], F32, "mifg")
            self.load_tok(q, tok0, nt, COLS['m_q'], 256)
            self.load_tok(k, tok0, nt, COLS['m_k'], 256)
            self.load_tok(og, tok0, nt, COLS['m_o'], 256)
            self.load_tok(ifg, tok0, nt, COLS['m_i'], 16)
            K.memset("pool", vaug[:], 1.0, [vaug])
            for n in range(nt):
                K.dma(K.dq(), vaug[:, n, :, 0:64],
                      self.proj[tok0 + n * 128:tok0 + (n + 1) * 128, COLS['m_v']:COLS['m_v'] + 256].rearrange(
                          "p (h d) -> p h d", h=4), [self.proj], [vaug])
            bigf = K.sb([128, 16], F32, "bigf")
            K.dma("sp", bigf[:, 0:8], bc(I["b_ig"][l:l + 1, :], [128, 8]), [I["b_ig"]], [bigf])
            K.dma("sp", bigf[:, 8:16], bc(I["b_fg"][l:l + 1, :], [128, 8]), [I["b_fg"]], [bigf])
            if latent:
                with K.scope():
                    self.rope(q, nt)
                    self.rope(k, nt)
            qT = K.sb([64, 4, T_], F32, "mqT")
            qTb = K.sb([64, 4, T_], BF16, "mqTb")
            kT = K.sb([64, 4, T_], BF16, "mkT")
            self.tok2feat(qT, q, nt)
            K.cp("pool", qTb[:], qT[:], [qT], [qTb])
            self.tok2feat(kT, k, nt)
            K.tt("dve", ifg[:], ifg[:], bc(bigf[:].unsqueeze(1), [128, nt, 16]), ALU.add, [ifg, bigf], [ifg])
            lf = K.sb([128, nt, 8], F32, "mlf")
            K.act(lf[:], ifg[:, :, 8:16], AF.Exp, [ifg], [lf], scale=-1.0)
            K.act(lf[:], lf[:], AF.Ln, [lf], [lf], bias=1.0)
            K.ts("dve", lf[:], lf[:], -1.0, None, ALU.mult, None, [lf], [lf])
            Bcol = K.sb([128, nt, 8], F32, "mB")
            Bmat = K.sb([8, T_], F32, "mBmat")
            with K.scope(nt > 2):
                lfT = K.sb([8, T_], F32, "mlfT")
                for cg in range((nt + 3) // 4):
                    ps = K.psum()
                    nn_ = min(4, nt - cg * 4)
                    for i_ in range(nn_):
                        K.tr(ps[0:8, i_ * 128:(i_ + 1) * 128], lf[:, cg * 4 + i_, :], self.ident[:], [lf, self.ident], [ps])
                    K.cp("dve", lfT[:, cg * 512:cg * 512 + nn_ * 128], ps[0:8, 0:nn_ * 128], [ps], [lfT])
                one8 = K.sb([8, T_], F32, "mone8")
                K.memset("dve", one8[:], 1.0, [one8])
                Bp = K.sb([8, T_], F32, "mBp")
                K.S.op("dve", lambda e: e.tensor_tensor_scan(Bp[:], one8[:], lfT[:], 0.0, ALU.mult, ALU.add),
                       [one8.b, lfT.b], [Bp.b])
                isf = K.sb([8, 1], F32, "misf")
                K.ts("dve", isf[:], self.pidx[0:8, :], 4.0, None, ALU.is_lt, None, [self.pidx], [isf])
                K.tt("dve", lfT[:], lfT[:], Bp[:], ALU.subtract, [lfT, Bp], [lfT])
                K.ts("dve", lfT[:], lfT[:], Bp[:, T_ - 1:T_], None, ALU.add, None, [lfT, Bp], [lfT])
                K.tt("dve", Bp[:], Bp[:], lfT[:], ALU.subtract, [Bp, lfT], [Bp])
                K.stt("dve", Bmat[:], Bp[:], isf[:, 0:1], lfT[:], ALU.mult, ALU.add, [Bp, isf, lfT], [Bmat])
            ps = K.psum()
            for n in range(nt):
                K.tr(ps[:, n * 8:(n + 1) * 8], Bmat[:, n * 128:(n + 1) * 128], self.ident[0:8, 0:8], [Bmat, self.ident], [ps])
            K.cp("act", Bcol[:], ps[:, 0:nt * 8].rearrange("p (n j) -> p n j", j=8), [ps], [Bcol])
            cb = K.sb([128, nt, 8], F32, "mcb")
            K.tt("dve", cb[:], ifg[:, :, 0:8], Bcol[:], ALU.subtract, [ifg, Bcol], [cb])
            K.ts("dve", cb[:], cb[:], LN8, None, ALU.add, None, [cb], [cb])
            if latent:
                m0b = K.sb([128, 8], F32, "m0b")
                K.dma("sp", m0b[:], bc(I["sm"][l:l + 1, :], [128, 8]), [I["sm"]], [m0b])
                c0a = K.sb([64, 8, 65], F32, "c0a")
                K.dma("sp", c0a[:, :, 0:64], I["sC"][l].rearrange("d h k v -> k (d h) v"), [I["sC"]], [c0a])
                K.dma("sp", c0a[:, :, 64], I["sn"][l].rearrange("d h k -> k (d h)"), [I["sn"]], [c0a],
                      allow_slow_non_contiguous=True)
            hsum = K.sb([128, nt, 256], F32, "mhs")
            K.memset("pool", hsum[:], 0.0, [hsum])
            C = dict(nt=nt, T=T_, latent=latent, qT=qT, qTb=qTb, kT=kT, vaug=vaug, Bcol=Bcol, Bmat=Bmat, cb=cb, hsum=hsum,
                     m0b=(m0b if latent else None), c0a=(c0a if latent else None))
            with K.scope(nt > 2):
                nch = (T_ + 511) // 512
                for dr in range(2):
                    K.interleave([self._mlstm_pre(dr * 4 + h, C) for h in range(4)])
                    for c in range(nch):
                        K.interleave([self._mlstm_unit(dr * 4 + h, c, C) for h in range(4)])
            self.head_norm(hsum, nt, True)
            K.act(og[:], og[:], AF.Sigmoid, [og], [og])
            K.tt("dve", hsum[:], hsum[:], og[:], ALU.mult, [hsum, og], [hsum])
            self.store_branch(0, tok0, nt, hsum)
            if not latent:
                self.mlstm_state(l, pb, nt, T_, k, vaug, ifg, lf)

    def _mlstm_pre(self, j, C):
        K = self.K
        nt, T_, latent = C["nt"], C["T"], C["latent"]
        qT, Bcol, m0b = C["qT"], C["Bcol"], C["m0b"]
        h = j % 4
        sx = f"_{h}"
        Brow = K.rot("mBrow" + sx, [128, T_], F32, 1)
        C["Brow", j] = Brow
        Bmat = C["Bmat"]
        for c0 in range(0, T_, 512):
            w = min(512, T_ - c0)
            ps = K.psum(hold=True, nbanks=8)
            K.mm(ps[:, 0:w], self.sel8[:, j, :], Bmat[:, c0:c0 + w], True, True, [self.sel8, Bmat], [ps])
            yield
            K.cp("act", Brow[:, c0:c0 + w], ps[:, 0:w], [ps], [Brow])
            K.psfree(ps)
        if latent:
            qTw = K.rot("mqTw" + sx, [64, T_], F32, 1)
            C["qTw", j] = qTw
            K.act(qTw[:], Brow[0:64, :], AF.Exp, [Brow, m0b], [qTw], bias=m0b[0:64, j:j + 1])
            K.tt("dve", qTw[:], qT[:, h, :], qTw[:], ALU.mult, [qT, qTw], [qTw])

    def _mlstm_unit(self, j, c, C):
        K = self.K
        nt, T_, latent = C["nt"], C["T"], C["latent"]
        qT, kT, vaug, cb, hsum, c0a = (C[k_] for k_ in ("qTb", "kT", "vaug", "cb", "hsum", "c0a"))
        Brow = C["Brow", j]
        dr, h = j // 4, j % 4
        fwd = dr == 0
        sx = f"_{h}"
        c0 = c * 512
        c1 = min(T_, c0 + 512)
        order = list(range(nt)) if fwd else list(range(nt - 1, -1, -1))
        acc = K.psum(hold=True, nbanks=8)
        started = False
        steps = []
        for m in order:
            ta, tb = (m * 128, T_) if fwd else (0, (m + 1) * 128)
            ca, cb_ = max(ta, c0), min(tb, c1)
            if ca < cb_:
                steps.append((m, ca, cb_))
        for si, (m, ca, cb_) in enumerate(steps):
            w = cb_ - ca
            rc = m * 128 + 127 if fwd else m * 128
            refc = Brow[:, rc:rc + 1]
            st = K.rot("mst" + sx, [128, 2], F32, 3)
            K.ts("dve", st[:, 0:1], refc, -1.0, None, ALU.mult, None, [Brow], [st])
            K.act(st[:, 1:2], cb[:, m, j:j + 1], AF.Exp, [cb, Brow], [st], bias=refc)
            vs = K.rot("mvs" + sx, [128, 65], BF16, 3)
            K.ts("dve", vs[:], vaug[:, m, h, :], st[:, 1:2], None, ALU.mult, None, [vaug, st], [vs])
            ps = K.psum(hold=True, nbanks=8)
            K.mm(ps[:, 0:w], kT[:, h, m * 128:(m + 1) * 128], qT[:, h, ca:cb_], True, True, [kT, qT], [ps])
            E1 = K.rot("mE1" + sx, [128, 512], F32, 1)
            K.act(E1[:, 0:w], Brow[:, ca:cb_], AF.Exp, [Brow, st], [E1], bias=st[:, 0:1])
            yield
            Pm = K.rot("mPm" + sx, [128, 512], BF16, 2)
            K.tt("dve", Pm[:, 0:w], ps[:, 0:w], E1[:, 0:w], ALU.mult, [ps, E1], [Pm])
            K.psfree(ps)
            if ca <= m * 128 < cb_:
                off = m * 128 - ca
                if fwd:
                    K.S.op("pool", lambda e, Pm=Pm, off=off: e.affine_select(
                        Pm[:, off:off + 128], Pm[:, off:off + 128], [[1, 128]], ALU.is_ge, 0.0, base=0,
                        channel_multiplier=-1), [Pm.b], [Pm.b])
                else:
                    K.S.op("pool", lambda e, Pm=Pm, off=off: e.affine_select(
                        Pm[:, off:off + 128], Pm[:, off:off + 128], [[-1, 128]], ALU.is_ge, 0.0, base=0,
                        channel_multiplier=1), [Pm.b], [Pm.b])
            last = (si == len(steps) - 1) and not latent
            K.mm(acc[0:65, ca - c0:cb_ - c0], vs[:], Pm[:, 0:w], si == 0, last, [vs, Pm], [acc])
        if latent:
            qTw = C["qTw", j]
            K.mm(acc[0:65, 0:c1 - c0], c0a[:, j, :], qTw[:, c0:c1], False, True, [c0a, qTw], [acc])
        yield
        hTj = K.rot("mhTj" + sx, [65, 512], F32, 1)
        K.cp("act", hTj[0:65, 0:c1 - c0], acc[0:65, 0:c1 - c0], [acc], [hTj])
        K.psfree(acc)
        ntl = (c1 - c0) // 128
        n0 = c0 // 128
        ps = K.psum(hold=True, nbanks=8)
        for i_ in range(ntl):
            K.tr(ps[:, i_ * 65:(i_ + 1) * 65], hTj[0:65, i_ * 128:(i_ + 1) * 128], self.ident[0:65, 0:65],
                 [hTj, self.ident], [ps])
        yield
        X3 = ps[:, 0:ntl * 65].rearrange("p (n c) -> p n c", c=65)
        den = K.rot("mden" + sx, [128, 4], F32, 2)
        K.ts("dve", den[:, 0:ntl], X3[:, :, 64], -1.0, None, ALU.mult, None, [ps], [den])
        K.tt("dve", den[:, 0:ntl], den[:, 0:ntl], X3[:, :, 64], ALU.max, [ps, den], [den])
        K.ts("dve", den[:, 0:ntl], den[:, 0:ntl], 1.0, None, ALU.max, None, [den], [den])
        K.S.op("dve", lambda e, den=den: e.reciprocal(den[:, 0:ntl], den[:, 0:ntl]), [den.b], [den.b])
        tmp = K.rot("mtmp" + sx, [128, 4, 64], F32, 2)
        K.tt("dve", tmp[:, 0:ntl, :], X3[:, :, 0:64], bc(den[:, 0:ntl].unsqueeze(2), [128, ntl, 64]), ALU.mult,
             [ps, den], [tmp])
        K.psfree(ps)
        hs_ = hsum[:, n0:n0 + ntl, h * 64:(h + 1) * 64]
        K.tt("pool", hs_, hs_, tmp[:, 0:ntl, :], ALU.add, [hsum, tmp], [hsum])

    def mlstm_state(self, l, pb, nt, T_, k, vaug, ifg, lf):
        K = self.K
        LN8 = math.log(0.125)
        rows = K.sb([8, 2, T_], F32, "msrow")
        for which, c0 in ((0, None), (1, 0)):
            ps = K.psum()
            for n in range(nt):
                src = lf[:, n, :] if which == 0 else ifg[:, n, 0:8]
                K.tr(ps[0:8, n * 128:(n + 1) * 128], src, self.ident[:], [lf, ifg, self.ident], [ps])
            K.cp("dve", rows[:, which, :], ps[0:8, 0:T_], [ps], [rows])
        one8 = K.sb([8, T_], F32, "msone")
        K.memset("dve", one8[:], 1.0, [one8])
        Bp = K.sb([8, T_], F32, "msB")
        K.S.op("dve", lambda e: e.tensor_tensor_scan(Bp[:], one8[:], rows[:, 0, :], 0.0, ALU.mult, ALU.add),
               [one8.b, rows.b], [Bp.b])
        isf = K.sb([8, 1], F32, "msisf")
        K.ts("dve", isf[:], self.pidx[0:8, :], 4.0, None, ALU.is_lt, None, [self.pidx], [isf])
        a1 = K.sb([8, T_], F32, "msa1")
        a2 = K.sb([8, T_], F32, "msa2")
        K.ts("dve", a1[:], Bp[:], -1.0, Bp[:, T_ - 1:T_], ALU.mult, ALU.add, [Bp], [a1])
        K.tt("dve", a2[:], Bp[:], rows[:, 0, :], ALU.subtract, [Bp, rows], [a2])
        K.tt("dve", a1[:], a1[:], a2[:], ALU.subtract, [a1, a2], [a1])
        K.tt("dve", a2[:], a2[:], rows[:, 1, :], ALU.add, [a2, rows], [a2])
        lw = K.sb([8, T_], F32, "mslw")
        K.stt("dve", lw[:], a1[:], isf[:, 0:1], a2[:], ALU.mult, ALU.add, [a1, isf, a2], [lw])
        mnew = K.sb([8, 2], F32, "msm")
        K.red(mnew[:, 0:1], lw[:], ALU.max, [lw], [mnew])
        K.tt("dve", mnew[:, 0:1], mnew[:, 0:1], Bp[:, T_ - 1:T_], ALU.max, [mnew, Bp], [mnew])
        K.ts("dve", mnew[:, 1:2], mnew[:, 0:1], -1.0, LN8, ALU.mult, ALU.add, [mnew], [mnew])
        K.dma("sp", self.O["nm"][pb, l, :].rearrange("(j o) -> j o", o=1), mnew[:, 0:1], [mnew], [self.O["nm"]])
        wrow = K.sb([8, T_], F32, "mswr")
        K.act(wrow[:], lw[:], AF.Exp, [lw, mnew], [wrow], bias=mnew[:, 1:2])
        wcol = K.sb([128, nt, 8], F32, "mswc")
        ps = K.psum()
        for n in range(nt):
            K.tr(ps[:, n * 8:(n + 1) * 8], wrow[:, n * 128:(n + 1) * 128], self.ident[0:8, 0:8], [wrow, self.ident], [ps])
        K.cp("dve", wcol[:], ps[:, 0:nt * 8].rearrange("p (n j) -> p n j", j=8), [ps], [wcol])
        kw = K.sb([128, nt, 8, 64], F32, "mskw")
        for n in range(nt):
            for dr in range(2):
                K.tt("dve", kw[:, n, dr * 4:(dr + 1) * 4, :], k[:, n, :].rearrange("p (h d) -> p h d", h=4),
                     bc(wcol[:, n, dr * 4:(dr + 1) * 4].unsqueeze(2), [128, 4, 64]), ALU.mult, [k, wcol], [kw])
        cst = K.sb([64, 8, 65], F32, "mscst")
        for jg in range(2):
            ps = K.psum()
            for jj in range(4):
                j = jg * 4 + jj
                for n in range(nt):
                    K.mm(ps[0:64, jj * 65:(jj + 1) * 65], kw[:, n, j, :], vaug[:, n, j % 4, :], n == 0, n == nt - 1,
                         [kw, vaug], [ps])
            K.cp("act", cst[:, jg * 4:(jg + 1) * 4, :], ps[0:64, 0:260].rearrange("p (j c) -> p j c", j=4), [ps], [cst])
        K.dma("sp", self.O["nC"][pb, l].rearrange("j k v -> k j v"), cst[:, :, 0:64], [cst], [self.O["nC"]])
        K.dma("sp", self.O["nn"][pb, l].rearrange("j k -> k j"), cst[:, :, 64], [cst], [self.O["nn"]],
              allow_slow_non_contiguous=True)

    def lowrank(self, dst, srcT_tile, nt, W2pad, rows):
        pass

    def _gla_unit(self, dr, n, idx, C):
        K = self.K
        q, k, v, g, Sst, Sv, osum = (C[k_] for k_ in ("q", "k", "v", "g", "Sst", "Sv", "osum"))
        sx = f"_{dr}"
        fwd = dr == 0
        zero_state = (not C["latent"]) and idx == 0
        LT = self.U if fwd else self.Lo
        gs = g[:, n, dr, :]
        Sj = Sv[dr * 4:(dr + 1) * 4]
        S4 = Sst[:, dr * 4:(dr + 1) * 4, :]
        psG = K.psum(hold=True, nbanks=8)
        psT = K.psum(hold=True, nbanks=8)
        K.mm(psG[:, 0:256], LT[:], gs, True, True, [LT, g], [psG])
        K.mm(psT[:, 0:256], self.ones[:], gs, True, True, [self.ones, g], [psT])
        for h in range(4):
            K.mm(psT[0:64, 256 + h:257 + h], g[:, n, dr, h * 64:(h + 1) * 64], self.ones[:, 0:1], True, True,
                 [g, self.ones], [psT])
        yield
        eG = K.rot("geG" + sx, [128, 256], F32, 1)
        enG = K.rot("genG" + sx, [128, 256], F32, 1)
        eGt = K.rot("geGt" + sx, [128, 256], F32, 1)
        dcol = K.rot("gdcol" + sx, [64, 4], F32, 2)
        K.act(eG[:], psG[:, 0:256], AF.Exp, [psG], [eG])
        K.act(enG[:], psG[:, 0:256], AF.Exp, [psG], [enG], scale=-1.0)
        K.act(eGt[:], psT[:, 0:256], AF.Exp, [psT], [eGt])
        K.act(dcol[:], psT[0:64, 256:260], AF.Exp, [psT], [dcol])
        K.psfree(psG)
        K.psfree(psT)
        qt, kt, kh = eG, enG, eGt
        K.tt("dve", qt[:], q[:, n, :], eG[:], ALU.mult, [q, eG], [qt])
        K.tt("dve", kt[:], k[:, n, :], enG[:], ALU.mult, [k, enG], [kt])
        K.tt("pool", kh[:], kt[:], eGt[:], ALU.mult, [kt, eGt], [kh])
        vb = K.rot("gvb" + sx, [128, 256], BF16, 1)
        K.cp("pool", vb[:], v[:, n, :], [v], [vb])
        psq = K.psum(hold=True, nbanks=8)
        psk = K.psum(hold=True, nbanks=8)
        for (ps_, src_) in ((psq, qt), (psk, kt)):
            for h in range(4):
                K.tr(ps_[0:64, h * 128:(h + 1) * 128], src_[:, h * 64:(h + 1) * 64], self.ident[:], [src_, self.ident], [ps_])
        yield
        v4 = lambda ps_: ps_[0:64, :].rearrange("p (h t) -> p h t", h=4)
        qtT = K.rot("gqtT" + sx, [64, 4, 128], F32, 1)
        qtTb = K.rot("gqtTb" + sx, [64, 4, 128], BF16, 1)
        ktTb = K.rot("gktTb" + sx, [64, 4, 128], BF16, 1)
        K.cp("act", qtT[:], v4(psq), [psq], [qtT])
        K.cp("dve", qtTb[:], v4(psq), [psq], [qtTb])
        K.cp("act", ktTb[:], v4(psk), [psk], [ktTb])
        K.psfree(psq)
        K.psfree(psk)
        psA = K.psum(hold=True, nbanks=8)
        for h in range(4):
            K.mm(psA[:, h * 128:(h + 1) * 128], ktTb[:, h, :], qtTb[:, h, :], True, True, [ktTb, qtTb], [psA])
        yield
        attm = K.rot("gatt" + sx, [128, 4, 128], BF16, 1)
        K.tt("dve", attm[:], psA[:].rearrange("p (h t) -> p h t", h=4), bc(LT[:].unsqueeze(1), [128, 4, 128]), ALU.mult,
             [psA, LT], [attm])
        K.psfree(psA)
        acc = K.psum(hold=True, nbanks=8)
        psU = K.psum(hold=True, nbanks=8)
        for h in range(4):
            hs = slice(h * 64, (h + 1) * 64)
            K.mm(acc[:, hs], attm[:, h, :], vb[:, hs], True, zero_state, [attm, vb], [acc])
            if not zero_state:
                K.mm(acc[:, hs], qtT[:, h, :], Sst[:, dr * 4 + h, :], False, True, [qtT, Sj[h]], [acc])
            K.mm(psU[0:64, hs], kh[:, hs], v[:, n, hs], True, True, [kh, v], [psU])
        yield
        K.tt("dve", osum[:, n, :], osum[:, n, :], acc[:, 0:256], ALU.add, [osum, acc], [osum])
        K.psfree(acc)
        K.tt("dve", S4, S4, bc(dcol[:].unsqueeze(2), [64, 4, 64]), ALU.mult, Sj + [dcol], Sj)
        K.tt("dve", S4, S4, psU[0:64, 0:256].rearrange("p (h v) -> p h v", h=4), ALU.add, Sj + [psU], Sj)
        K.psfree(psU)

    def gla(self, l, tok0, T_, latent, pb):
        K = self.K
        I = self.I
        nt = T_ // 128
        with K.scope():
            q = K.sb([128, nt, 256], F32, "gq")
            k = K.sb([128, nt, 256], F32, "gk")
            v = K.sb([128, nt, 256], F32, "gv")
            gg = K.sb([128, nt, 256], F32, "ggg")
            ga = K.sb([128, nt, 32], F32, "gga")
            self.load_tok(q, tok0, nt, COLS['g_q'], 256)
            self.load_tok(k, tok0, nt, COLS['g_k'], 256)
            self.load_tok(v, tok0, nt, COLS['g_v'], 256)
            self.load_tok(gg, tok0, nt, COLS['g_g'], 256)
            self.load_tok(ga, tok0, nt, COLS['g_a'], 32)
            if latent:
                self.rope(q, nt)
                self.rope(k, nt)
            K.ts("dve", q[:], q[:], 0.125, None, ALU.mult, None, [q], [q])
            W2 = K.sb([32, 2, 256], F32, "gW2")
            K.memset("dve", W2[:], 0.0, [W2])
            K.dma("sp", W2[0:16, 0, :], I["w_gla_a2"][l, 0], [I["w_gla_a2"]], [W2])
            K.dma("sp", W2[16:32, 1, :], I["w_gla_a2"][l, 1], [I["w_gla_a2"]], [W2])
            bgl = K.sb([128, 2, 256], F32, "gbgl")
            K.dma("sp", bgl[:], bc(I["b_gla_a"][l:l + 1].rearrange("o d c -> o (d c)"), [128, 512]).rearrange(
                "p (d c) -> p d c", d=2), [I["b_gla_a"]], [bgl])
            g = K.sb([128, nt, 2, 256], F32, "gg_")
            for n in range(nt):
                gaT = K.rot("gaT", [32, 128], F32, 2)
                self.transpose_block(gaT[:], gaT, ga[:, n, :], ga, 128, 32)
                for dr in range(2):
                    ps = K.psum()
                    K.mm(ps[:, 0:256], gaT[:], W2[:, dr, :], True, True, [gaT, W2], [ps])
                    K.tt("dve", g[:, n, dr, :], ps[:, 0:256], bgl[:, dr, :], ALU.add, [ps, bgl], [g])
            K.act(g[:], g[:], AF.Exp, [g], [g], scale=-1.0)
            K.act(g[:], g[:], AF.Ln, [g], [g], bias=1.0)
            K.ts("dve", g[:], g[:], -1.0 / 16.0, None, ALU.mult, None, [g], [g])
            Sst = K.sb([64, 8, 64], F32, "gS")
            Sv = [T(Sst.h, f"gS{j}") for j in range(8)]
            if latent:
                K.dma("sp", Sst[:], I["sg"][l].rearrange("d h k v -> k (d h) v"), [I["sg"]], Sv)
            else:
                K.memset("dve", Sst[:], 0.0, Sv)
            osum = K.sb([128, nt, 256], F32, "gos")
            K.memset("pool", osum[:], 0.0, [osum])
            C = dict(nt=nt, latent=latent, q=q, k=k, v=v, g=g, Sst=Sst, Sv=Sv, osum=osum)
            with K.scope(nt > 2):
                for idx in range(nt):
                    K.interleave([self._gla_unit(0, idx, idx, C), self._gla_unit(1, nt - 1 - idx, idx, C)])
            self.head_norm(osum, nt, False)
            K.act(gg[:], gg[:], AF.Silu, [gg], [gg])
            K.tt("dve", osum[:], osum[:], gg[:], ALU.mult, [osum, gg], [osum])
            self.store_branch(1, tok0, nt, osum)
            if not latent:
                K.dma("sp", self.O["ng"][pb, l].rearrange("j k v -> k j v"), Sst[:], Sv, [self.O["ng"]])

    def rwkv(self, l, tok0, T_, latent, pb):
        K = self.K
        I = self.I
        nt = T_ // 128
        C0 = COLS['r_rkv']
        with K.scope():
            rkv = K.sb([128, nt, 768], F32, "rrkv")
            logw = K.sb([128, nt, 2, 256], F32, "rlogw")
            aa = K.sb([128, nt, 2, 256], F32, "raa")
            kvec = K.sb([128, 3, 256], F32, "rkvec")
            self._rwkv_pre(l, tok0, nt, rkv, logw, aa, kvec)
            self._rwkv_main(l, tok0, nt, latent, pb, rkv, logw, aa, kvec)

    def _rwkv_pre(self, l, tok0, nt, rkv, logw, aa, kvec):
        K = self.K
        I = self.I
        C0 = COLS['r_rkv']
        with K.scope(nt > 2):
            taps = K.sb([128, 3, 768], F32, "rtaps")
            K.dma("sp", taps[:], bc(I["shift_rwkv"][l:l + 1].rearrange("o s c -> o (s c)"), [128, 2304]).rearrange(
                "p (s c) -> p s c", s=3), [I["shift_rwkv"]], [taps])
            for n in range(nt):
                r0 = tok0 + n * 128
                xm = K.rot("rxm", [128, 768], F32, 2)
                xp = K.rot("rxp", [128, 768], F32, 2)
                K.dma(K.dq(), rkv[:, n, :], self.proj[r0:r0 + 128, C0:C0 + 768], [self.proj], [rkv])
                if n == 0:
                    K.memset("pool", xm[:], 0.0, [xm])
                    K.dma(K.dq(), xm[1:128, :], self.proj[r0:r0 + 127, C0:C0 + 768], [self.proj], [xm])
                else:
                    K.dma(K.dq(), xm[:], self.proj[r0 - 1:r0 + 127, C0:C0 + 768], [self.proj], [xm])
                if n == nt - 1:
                    K.memset("pool", xp[:], 0.0, [xp])
                    K.dma(K.dq(), xp[0:127, :], self.proj[r0 + 1:r0 + 128, C0:C0 + 768], [self.proj], [xp])
                else:
                    K.dma(K.dq(), xp[:], self.proj[r0 + 1:r0 + 129, C0:C0 + 768], [self.proj], [xp])
                K.tt("dve", rkv[:, n, :], rkv[:, n, :], taps[:, 1, :], ALU.mult, [rkv, taps], [rkv])
                K.tt("pool", xm[:], xm[:], taps[:, 0, :], ALU.mult, [xm, taps], [xm])
                K.tt("pool", xp[:], xp[:], taps[:, 2, :], ALU.mult, [xp, taps], [xp])
                K.tt("dve", rkv[:, n, :], rkv[:, n, :], xm[:], ALU.add, [rkv, xm], [rkv])
                K.tt("dve", rkv[:, n, :], rkv[:, n, :], xp[:], ALU.add, [rkv, xp], [rkv])
        with K.scope(nt > 2):
            lwag = K.sb([128, nt, 128], F32, "rlwag")
            self.load_tok(lwag, tok0, nt, COLS['r_w'], 128)
            K.act(lwag[:, :, 0:64], lwag[:, :, 0:64], AF.Tanh, [lwag], [lwag])
            Ww = K.sb([64, 2, 256], F32, "rWw")
            Wa = K.sb([64, 2, 256], F32, "rWa")
            K.memset("dve", Ww[:], 0.0, [Ww])
            K.memset("dve", Wa[:], 0.0, [Wa])
            for dr in range(2):
                K.dma("sp", Ww[dr * 32:(dr + 1) * 32, dr, :], I["w_w2"][l, dr], [I["w_w2"]], [Ww])
                K.dma("sp", Wa[dr * 32:(dr + 1) * 32, dr, :], I["w_a2"][l, dr], [I["w_a2"]], [Wa])
            w0b = K.sb([128, 2, 256], F32, "rw0b")
            a0b = K.sb([128, 2, 256], F32, "ra0b")
            K.dma("sp", w0b[:], bc(I["w0_rwkv"][l:l + 1].rearrange("o d c -> o (d c)"), [128, 512]).rearrange(
                "p (d c) -> p d c", d=2), [I["w0_rwkv"]], [w0b])
            K.dma("sp", a0b[:], bc(I["a0_rwkv"][l:l + 1].rearrange("o d c -> o (d c)"), [128, 512]).rearrange(
                "p (d c) -> p d c", d=2), [I["a0_rwkv"]], [a0b])
            for i_, nm in enumerate(("k_k", "k_a", "r_k")):
                K.dma("sp", kvec[:, i_, :], bc(I[nm][l:l + 1, :], [128, 256]), [I[nm]], [kvec])
            for n in range(nt):
                ps = K.psum()
                for i_ in range(2):
                    K.tr(ps[0:64, i_ * 128:(i_ + 1) * 128], lwag[:, n, i_ * 64:(i_ + 1) * 64], self.ident[:],
                         [lwag, self.ident], [ps])
                tT = K.rot("rtT", [64, 2, 128], F32, 2)
                K.cp(K.evq(), tT[:], ps[0:64, 0:256].rearrange("p (i t) -> p i t", i=2), [ps], [tT])
                for dr in range(2):
                    ps = K.psum()
                    K.mm(ps[:, 0:256], tT[:, 0, :], Ww[:, dr, :], True, True, [tT, Ww], [ps])
                    K.tt("dve", logw[:, n, dr, :], ps[:, 0:256], w0b[:, dr, :], ALU.add, [ps, w0b], [logw])
                    ps = K.psum()
                    K.mm(ps[:, 0:256], tT[:, 1, :], Wa[:, dr, :], True, True, [tT, Wa], [ps])
                    K.tt("dve", aa[:, n, dr, :], ps[:, 0:256], a0b[:, dr, :], ALU.add, [ps, a0b], [aa])
            K.act(logw[:], logw[:], AF.Sigmoid, [logw], [logw])
            K.ts("dve", logw[:], logw[:], -math.exp(-0.5), None, ALU.mult, None, [logw], [logw])
            K.act(aa[:], aa[:], AF.Sigmoid, [aa], [aa])

    def _rwkv_unit(self, dr, n, idx, C):
        K = self.K
        sx = f"_{dr}"
        fwd = dr == 0
        rkv, logw, aa, kvec, kap, Sst, Sv, ysum = (C[k_] for k_ in ("rkv", "logw", "aa", "kvec", "kap", "Sst", "Sv", "ysum"))
        zero_state = (not C["latent"]) and idx == 0
        LT = self.U if fwd else self.Lo
        mN = self.Ls if fwd else self.Us
        mNT = self.Us if fwd else self.Ls
        mMT = self.U if fwd else self.Lo
        lw_ = logw[:, n, dr, :]
        a_ = aa[:, n, dr, :]
        r_ = rkv[:, n, 0:256]
        k_ = rkv[:, n, 256:512]
        v_ = rkv[:, n, 512:768]
        Sj = Sv[dr * 4:(dr + 1) * 4]
        S4 = Sst[:, dr * 4:(dr + 1) * 4, :]

        def m4(mask):
            return bc(mask[:].unsqueeze(1), [128, 4, 128])
        psL = K.psum(hold=True, nbanks=8)
        psT = K.psum(hold=True, nbanks=8)
        K.mm(psL[:, 0:256], LT[:], lw_, True, True, [LT, logw], [psL])
        K.mm(psT[:, 0:256], self.ones[:], lw_, True, True, [self.ones, logw], [psT])
        for h in range(4):
            K.mm(psT[0:64, 256 + h:257 + h], logw[:, n, dr, h * 64:(h + 1) * 64], self.ones[:, 0:1], True, True,
                 [logw, self.ones], [psT])
        yield
        eL = K.rot("reL" + sx, [128, 256], F32, 1)
        enL = K.rot("renL" + sx, [128, 256], F32, 1)
        eLex = K.rot("reLex" + sx, [128, 256], F32, 1)
        eLt = K.rot("reLt" + sx, [128, 256], F32, 1)
        pcol = K.rot("rpcol" + sx, [64, 4], F32, 2)
        K.act(eL[:], psL[:, 0:256], AF.Exp, [psL], [eL])
        K.act(enL[:], psL[:, 0:256], AF.Exp, [psL], [enL], scale=-1.0)
        K.tt("dve", eLex[:], psL[:, 0:256], lw_, ALU.subtract, [psL, logw], [eLex])
        K.act(eLex[:], eLex[:], AF.Exp, [eLex], [eLex])
        K.act(eLt[:], psT[:, 0:256], AF.Exp, [psT], [eLt])
        K.act(pcol[:], psT[0:64, 256:260], AF.Exp, [psT], [pcol])
        K.psfree(psL)
        K.psfree(psT)
        Bt = K.rot("rBt" + sx, [128, 256], F32, 1)
        Kt = K.rot("rKt" + sx, [128, 256], F32, 1)
        At, Rt, Bh, Kh = eLex, eL, enL, eLt
        K.stt("dve", At[:], kap[:, n, :], -1.0, eLex[:], ALU.mult, ALU.mult, [kap, eLex], [At])
        K.tt("pool", Bt[:], a_, kap[:, n, :], ALU.mult, [aa, kap], [Bt])
        K.tt("pool", Bt[:], Bt[:], enL[:], ALU.mult, [Bt, enL], [Bt])
        K.stt("dve", Kt[:], a_, -1.0, kvec[:, 1, :], ALU.add, ALU.mult, [aa, kvec], [Kt])
        K.stt("dve", Kt[:], Kt[:], 1.0, k_, ALU.add, ALU.mult, [Kt, rkv], [Kt])
        K.tt("pool", Kt[:], Kt[:], enL[:], ALU.mult, [Kt, enL], [Kt])
        K.tt("pool", Rt[:], r_, eL[:], ALU.mult, [rkv, eL], [Rt])
        K.tt("pool", Bh[:], Bt[:], eLt[:], ALU.mult, [Bt, eLt, enL], [Bh])
        K.tt("pool", Kh[:], Kt[:], eLt[:], ALU.mult, [Kt, eLt], [Kh])
        TT = {}
        for pair in ((("A", At), ("B", Bt)), (("K", Kt), ("R", Rt))):
            pss = []
            for nm_, src_ in pair:
                ps = K.psum(hold=True, nbanks=8)
                for h in range(4):
                    K.tr(ps[0:64, h * 128:(h + 1) * 128], src_[:, h * 64:(h + 1) * 64], self.ident[:],
                         [src_, self.ident], [ps])
                pss.append(ps)
            yield
            for (nm_, src_), ps in zip(pair, pss):
                dstT = K.rot("rT" + nm_ + sx, [64, 4, 128], BF16, 1)
                K.cp(K.evq(), dstT[:], ps[0:64, :].rearrange("p (h t) -> p h t", h=4), [ps], [dstT])
                K.psfree(ps)
                TT[nm_] = dstT
        AtT, BtT, KtT, RtT = TT["A"], TT["B"], TT["K"], TT["R"]
        Bhb = K.rot("rBhb" + sx, [128, 256], BF16, 1)
        Khb = K.rot("rKhb" + sx, [128, 256], BF16, 1)
        vb = K.rot("rvb" + sx, [128, 256], BF16, 1)
        K.cp("pool", Bhb[:], Bh[:], [Bh], [Bhb])
        K.cp("pool", Khb[:], Kh[:], [Kh], [Khb])
        K.cp("pool", vb[:], v_, [rkv], [vb])
        Sb = K.rot("rSb" + sx, [64, 4, 64], BF16, 1)
        if not zero_state:
            K.cp("act", Sb[:], S4, Sj, [Sb])
        v4 = lambda ps_: ps_[:].rearrange("p (h t) -> p h t", h=4)

        def prod_mm(lt, rt):
            ps_ = K.psum(hold=True, nbanks=8)
            for h in range(4):
                K.mm(ps_[:, h * 128:(h + 1) * 128], lt[:, h, :], rt[:, h, :], True, True, [lt, rt], [ps_])
            return ps_

        def prod_ev(ps_, nm_, mask, n_=1):
            o_ = K.rot(nm_ + sx, [128, 4, 128], BF16, n_)
            K.tt("dve", o_[:], v4(ps_), m4(mask), ALU.mult, [ps_, mask], [o_])
            K.psfree(ps_)
            return o_
        p1 = prod_mm(AtT, BtT)
        p2 = prod_mm(BtT, AtT)
        yield
        Nk = prod_ev(p1, "rN", mN, 2)
        Ntk = prod_ev(p2, "rNt", mNT, 2)
        Xtb = K.rot("rXtb" + sx, [128, 4, 128], BF16, 1)
        K.tt("pool", Xtb[:], Ntk[:], m4(self.ident), ALU.add, [Ntk, self.ident], [Xtb])
        psX = K.psum(hold=True, nbanks=8)
        for h in range(4):
            K.mm(psX[:, h * 128:(h + 1) * 128], self.identb[:], Xtb[:, h, :], h == 0, True, [self.identb, Xtb], [psX])
        for it in range(6):
            ps1 = prod_mm(Ntk, Nk)
            ps2 = prod_mm(Nk, Ntk) if it < 5 else None
            yield
            Nk2 = K.rot("rN" + sx, [128, 4, 128], BF16, 2)
            K.cp("act", Nk2[:], v4(ps1), [ps1], [Nk2])
            K.psfree(ps1)
            if it < 5:
                Ntk2 = K.rot("rNt" + sx, [128, 4, 128], BF16, 2)
                K.cp("dve", Ntk2[:], v4(ps2), [ps2], [Ntk2])
                K.psfree(ps2)
            for h in range(4):
                K.mm(psX[:, h * 128:(h + 1) * 128], Nk2[:, h, :], Xtb[:, h, :], False, True, [Nk2, Xtb], [psX])
            yield
            Xtb = K.rot("rXtb" + sx, [128, 4, 128], BF16, 1)
            K.cp("act", Xtb[:], v4(psX), [psX], [Xtb])
            Nk = Nk2
            if it < 5:
                Ntk = Ntk2
        K.psfree(psX)
        p3 = prod_mm(KtT, AtT)
        p4 = prod_mm(BtT, RtT)
        p5 = prod_mm(KtT, RtT)
        yield
        NakT = prod_ev(p3, "rN", mNT, 2)
        MrbT = prod_ev(p4, "rNt", mMT, 2)
        MrkT = prod_ev(p5, "rN", mMT, 2)
        psR = K.psum(hold=True, nbanks=8)
        for h in range(4):
            hs = slice(h * 64, (h + 1) * 64)
            if not zero_state:
                K.mm(psR[:, hs], AtT[:, h, :], Sb[:, h, :], True, False, [AtT, Sb], [psR])
            K.mm(psR[:, hs], NakT[:, h, :], vb[:, hs], zero_state, True, [NakT, vb], [psR])
        yield
        Rsb = K.rot("rRsb" + sx, [128, 256], BF16, 1)
        K.cp("act", Rsb[:], psR[:, 0:256], [psR], [Rsb])
        K.psfree(psR)
        psU = K.psum(hold=True, nbanks=8)
        for h in range(4):
            hs = slice(h * 64, (h + 1) * 64)
            K.mm(psU[:, hs], Xtb[:, h, :], Rsb[:, hs], True, True, [Xtb, Rsb], [psU])
        yield
        Usb = K.rot("rUsb" + sx, [128, 256], BF16, 1)
        K.cp("act", Usb[:], psU[:, 0:256], [psU], [Usb])
        K.psfree(psU)
        psY = K.psum(hold=True, nbanks=8)
        psS = K.psum(hold=True, nbanks=8)
        for h in range(4):
            hs = slice(h * 64, (h + 1) * 64)
            if not zero_state:
                K.mm(psY[:, hs], RtT[:, h, :], Sb[:, h, :], True, False, [RtT, Sb], [psY])
            K.mm(psY[:, hs], MrbT[:, h, :], Usb[:, hs], zero_state, False, [MrbT, Usb], [psY])
            K.mm(psY[:, hs], MrkT[:, h, :], vb[:, hs], False, True, [MrkT, vb], [psY])
            K.mm(psS[0:64, hs], Bhb[:, hs], Usb[:, hs], True, False, [Bhb, Usb], [psS])
            K.mm(psS[0:64, hs], Khb[:, hs], vb[:, hs], False, True, [Khb, vb], [psS])
        yield
        K.tt("dve", ysum[:, n, :], ysum[:, n, :], psY[:, 0:256], ALU.add, [ysum, psY], [ysum])
        K.psfree(psY)
        K.tt("dve", S4, S4, bc(pcol[:].unsqueeze(2), [64, 4, 64]), ALU.mult, Sj + [pcol], Sj)
        K.tt("dve", S4, S4, psS[0:64, 0:256].rearrange("p (h v) -> p h v", h=4), ALU.add, Sj + [psS], Sj)
        K.psfree(psS)

    def _rwkv_main(self, l, tok0, nt, latent, pb, rkv, logw, aa, kvec):
        K = self.K
        I = self.I
        if True:
            kap = K.sb([128, nt, 256], F32, "rkap")
            kk_ = rkv[:, :, 256:512]
            K.tt("dve", kap[:], kk_, bc(kvec[:, 0, :].unsqueeze(1), [128, nt, 256]), ALU.mult, [rkv, kvec], [kap])
            ss = K.sb([128, nt * 4], F32, "rss")
            with K.scope(nt > 2):
                sq = K.rot("ropeA", [128, nt, 256], F32, 1)
                K.tt("pool", sq[:], kap[:], kap[:], ALU.mult, [kap], [sq])
                K.red(ss[:], sq[:].rearrange("p n (h d) -> p (n h) d", h=4), ALU.add, [sq], [ss])
            K.ts("dve", ss[:], ss[:], LN_EPS, None, ALU.add, None, [ss], [ss])
            K.act(ss[:], ss[:], AF.Sqrt, [ss], [ss])
            K.S.op("dve", lambda e: e.reciprocal(ss[:], ss[:]), [ss.b], [ss.b])
            kap3 = kap[:].rearrange("p n (h d) -> p (n h) d", h=4)
            K.tt("dve", kap3, kap3, bc(ss[:].unsqueeze(2), [128, nt * 4, 64]), ALU.mult, [kap, ss], [kap])
            Sst = K.sb([64, 8, 64], F32, "rS")
            Sv = [T(Sst.h, f"rS{j}") for j in range(8)]
            if latent:
                R0 = K.sb([64, 8, 64], F32, "rR0")
                K.dma("sp", R0[:], I["sr"][l].rearrange("d h v k -> v (d h) k"), [I["sr"]], [R0])
                for jg in range(2):
                    ps = K.psum()
                    for jj in range(4):
                        K.tr(ps[0:64, jj * 64:(jj + 1) * 64], R0[:, jg * 4 + jj, :], self.ident[0:64, 0:64],
                             [R0, self.ident], [ps])
                    K.cp("dve", Sst[:, jg * 4:(jg + 1) * 4, :], ps[0:64, 0:256].rearrange("p (j v) -> p j v", j=4),
                         [ps], Sv[jg * 4:(jg + 1) * 4])
            else:
                K.memset("dve", Sst[:], 0.0, Sv)
            ysum = K.sb([128, nt, 256], F32, "rys")
            K.memset("pool", ysum[:], 0.0, [ysum])
            ctx = dict(nt=nt, latent=latent, rkv=rkv, logw=logw, aa=aa, kvec=kvec, kap=kap, Sst=Sst, Sv=Sv, ysum=ysum)
            with K.scope(nt > 2):
                for idx in range(nt):
                    K.interleave([self._rwkv_unit(0, idx, idx, ctx), self._rwkv_unit(1, nt - 1 - idx, idx, ctx)])
            self.head_norm(ysum, nt, True)
            bt = K.rot("ropeA", [128, nt, 256], F32, 1)
            K.tt("dve", bt[:], rkv[:, :, 0:256], rkv[:, :, 256:512], ALU.mult, [rkv], [bt])
            K.tt("dve", bt[:], bt[:], bc(kvec[:, 2, :].unsqueeze(1), [128, nt, 256]), ALU.mult, [bt, kvec], [bt])
            bs = K.sb([128, nt * 4], F32, "rbs")
            K.red(bs[:], bt[:].rearrange("p n (h d) -> p (n h) d", h=4), ALU.add, [bt], [bs])
            bt4 = bt[:].rearrange("p n (h d) -> p n h d", h=4)
            K.tt("dve", bt4, rkv[:, :, 512:768].rearrange("p n (h d) -> p n h d", h=4),
                 bc(bs[:].rearrange("p (n h) -> p n h", h=4).unsqueeze(3), [128, nt, 4, 64]), ALU.mult, [rkv, bs], [bt])
            K.tt("dve", ysum[:], ysum[:], bt[:], ALU.add, [ysum, bt], [ysum])
            Wg = K.sb([64, 256], F32, "rWg")
            K.dma("sp", Wg[:], I["w_g2"][l], [I["w_g2"]], [Wg])
            rg = K.sb([128, nt, 64], F32, "rrg")
            self.load_tok(rg, tok0, nt, COLS['r_g'], 64)
            K.act(rg[:], rg[:], AF.Sigmoid, [rg], [rg])
            for n in range(nt):
                rgT = K.rot("rrgT", [64, 128], F32, 2)
                self.transpose_block(rgT[:], rgT, rg[:, n, :], rg, 128, 64)
                ps = K.psum()
                K.mm(ps[:, 0:256], rgT[:], Wg[:], True, True, [rgT, Wg], [ps])
                K.tt("dve", ysum[:, n, :], ysum[:, n, :], ps[:, 0:256], ALU.mult, [ysum, ps], [ysum])
            self.store_branch(2, tok0, nt, ysum)
            if not latent:
                So = K.sb([64, 8, 64], F32, "rSo")
                for jg in range(2):
                    ps = K.psum()
                    for jj in range(4):
                        K.tr(ps[0:64, jj * 64:(jj + 1) * 64], Sst[:, jg * 4 + jj, :], self.ident[0:64, 0:64],
                             [Sv[jg * 4 + jj], self.ident], [ps])
                    K.cp("dve", So[:, jg * 4:(jg + 1) * 4, :], ps[0:64, 0:256].rearrange("p (j k) -> p j k", j=4),
                         [ps], [So])
                K.dma("sp", self.O["nr"][pb, l].rearrange("j v k -> v j k"), So[:], [So], [self.O["nr"]])

    def na_tables(self, l, Toe):
        K = self.K
        I = self.I
        with K.scope():
            z = K.sb([60, 128], F32, "naz")
            K.memset("dve", z[:], 0.0, [z])
            K.dma("sp", self.rpbpad[l], z[:], [z], [self.rpbpad])
            K.dma("sp", self.rpbpad[l, :, 48:79], I["rpb"][l], [I["rpb"]], [self.rpbpad])
            Tp = K.sb([64, 60, 64], F32, "naTp")
            src_ap = bass.AP(self.rpbpad.h, l * 60 * 128, [[1, 64], [128, 60], [1, 64]])
            K.dma("sp", Tp[:], src_ap, [self.rpbpad], [Tp])
            val = K.sb([64, 128], F32, "naval")
            K.S.op("pool", lambda e: e.iota(val[:], [[1, 128]], base=0, channel_multiplier=1,
                                             allow_small_or_imprecise_dtypes=True), [], [val.b])
            J2 = K.sb([64, 128], F32, "naJ2")
            J2b = K.sb([64, 128], F32, "naJ2b")
            K.ts("dve", J2[:], val[:], 63.0, None, ALU.is_equal, None, [val], [J2])
            K.ts("dve", J2b[:], val[:], 127.0, None, ALU.is_equal, None, [val], [J2b])
            K.tt("dve", J2[:], J2[:], J2b[:], ALU.add, [J2, J2b], [J2])
            ge64 = K.sb([128, 1], F32, "nage")
            K.ts("dve", ge64[:], self.pidx[:], 64.0, None, ALU.is_ge, None, [self.pidx], [ge64])
            cs = K.sb([128, 1], F32, "nacs")
            K.stt("dve", cs[:], ge64[:], -64.0, self.pidx[:], ALU.mult, ALU.add, [ge64, self.pidx], [cs])
            K.ts("dve", cs[:], cs[:], -8.0, 0.0, ALU.add, ALU.max, [cs], [cs])
            K.ts("dve", cs[:], cs[:], 48.0, None, ALU.min, None, [cs], [cs])
            dl = K.sb([128, 64], F32, "nadl")
            ok2 = K.sb([128, 64], F32, "naok2")
            K.ts("dve", dl[:], self.fidx[:, 0:64], cs[:, 0:1], None, ALU.subtract, None, [self.fidx, cs], [dl])
            K.ts("dve", ok2[:], dl[:], 16.0, None, ALU.is_lt, None, [dl], [ok2])
            K.ts("dve", dl[:], dl[:], 0.0, None, ALU.is_ge, None, [dl], [dl])
            K.tt("dve", dl[:], dl[:], ok2[:], ALU.mult, [dl, ok2], [dl])
            K.ts("dve", dl[:], dl[:], -NEG, NEG, ALU.mult, ALU.add, [dl], [dl])
            Tpf = Tp[:].rearrange("p j c -> p (j c)")
            Tf = Toe[:].rearrange("p h x -> p (h x)")
            for cb in range(8):
                c0 = cb * 512
                cw = min(512, 3840 - c0)
                ps = K.psum()
                K.mm(ps[:, 0:cw], J2[:], Tpf[:, c0:c0 + cw], True, True, [J2, Tp], [ps])
                K.tt("dve", Tf[:, c0:c0 + cw].rearrange("p (j c) -> p j c", c=64),
                     ps[:, 0:cw].rearrange("p (j c) -> p j c", c=64),
                     bc(dl[:].unsqueeze(1), [128, cw // 64, 64]), ALU.add, [ps, dl], [Toe])

    def na(self, l, tok0, T_, latent, pb):
        K = self.K
        I = self.I
        nt = T_ // 128
        C0 = COLS['n_qkv']
        with K.scope():
            q = K.sb([128, nt, 256], F32, "nq")
            k = K.sb([128, nt, 256], F32, "nk")
            v = K.sb([128, nt, 256], BF16, "nv")
            self.load_tok(q, tok0, nt, C0, 256)
            self.load_tok(k, tok0, nt, C0 + 256, 256)
            self.load_tok(v, tok0, nt, C0 + 512, 256, eng="pool")
            qT = K.sb([64, 4, T_], BF16, "nqT")
            kT = K.sb([64, 4, T_], BF16, "nkT")
            self.tok2feat(qT, q, nt)
            self.tok2feat(kT, k, nt)
            nout = K.sb([128, nt, 256], F32, "nout")
            if not latent:
                K.dma("sp", self.O["nak"][pb, l].rearrange("h s d -> s h d"),
                      self.proj[tok0:tok0 + 256, C0 + 256:C0 + 512].rearrange("s (h d) -> s h d", h=4),
                      [self.proj], [self.O["nak"]])
                K.dma("act", self.O["nav"][pb, l].rearrange("h s d -> s h d"),
                      self.proj[tok0:tok0 + 256, C0 + 512:C0 + 768].rearrange("s (h d) -> s h d", h=4),
                      [self.proj], [self.O["nav"]])
                for h in range(4):
                    hs = slice(h * 64, (h + 1) * 64)
                    for n in range(nt):
                        ps = K.psum()
                        K.mm(ps[:, 0:T_], qT[:, h, n * 128:(n + 1) * 128], kT[:, h, :], True, True, [qT, kT], [ps])
                        st = K.rot("nst", [128, 2], F32, 3)
                        K.red(st[:, 0:1], ps[:, 0:T_], ALU.max, [ps], [st])
                        K.ts("dve", st[:, 0:1], st[:, 0:1], -0.125, None, ALU.mult, None, [st], [st])
                        p = K.rot("np", [128, T_], F32, 2)
                        K.act(p[:], ps[:, 0:T_], AF.Exp, [ps, st], [p, st], bias=st[:, 0:1], scale=0.125,
                              accum=st[:, 1:2])
                        ps2 = K.psum()
                        for m in range(nt):
                            K.tr(ps2[:, m * 128:(m + 1) * 128], p[:, m * 128:(m + 1) * 128], self.ident[:],
                                 [p, self.ident], [ps2])
                        pT = K.rot("npT", [128, nt, 128], BF16, 2)
                        K.cp(K.evq(), pT[:], ps2[:, 0:T_].rearrange("p (m t) -> p m t", m=nt), [ps2], [pT])
                        acc = K.psum()
                        for m in range(nt):
                            K.mm(acc[:, 0:64], pT[:, m, :], v[:, m, hs], m == 0, m == nt - 1, [pT, v], [acc])
                        K.S.op("dve", lambda e, st=st: e.reciprocal(st[:, 1:2], st[:, 1:2]), [st.b], [st.b])
                        K.ts("dve", nout[:, n, hs], acc[:, 0:64], st[:, 1:2], None, ALU.mult, None, [acc, st], [nout])
            else:
                Toe = K.sb([128, 4, 960], F32, "naToe")
                self.na_tables(l, Toe)
                kc = K.sb([128, 2, 4, 64], F32, "nakc")
                vc = K.sb([128, 2, 4, 64], BF16, "navc")
                for c in range(2):
                    K.dma("sp", kc[:, c], I["cak"][l, :, c * 128:(c + 1) * 128, :].rearrange("h p d -> p h d"),
                          [I["cak"]], [kc])
                    K.dma("pool", vc[:, c], I["cav"][l, :, c * 128:(c + 1) * 128, :].rearrange("h p d -> p h d"),
                          [I["cav"]], [vc])
                kcT = K.sb([64, 4, 256], BF16, "nakcT")
                for c in range(2):
                    ps = K.psum()
                    for h in range(4):
                        K.tr(ps[0:64, h * 128:(h + 1) * 128], kc[:, c, h, :], self.ident[:], [kc, self.ident], [ps])
                    K.cp(K.evq(), kcT[:, :, c * 128:(c + 1) * 128], ps[0:64, :].rearrange("p (h t) -> p h t", h=4),
                         [ps], [kcT])
                for h in range(4):
                    hs = slice(h * 64, (h + 1) * 64)
                    for n in range(nt):
                        rs_ = [min(max(2 * n + hf - 4, 0), 8) for hf in range(2)]
                        kt0 = rs_[0] // 2
                        ntl = (rs_[1] + 7) // 2 - kt0 + 1
                        LW = ntl * 128
                        psA = K.psum()
                        psB = K.psum()
                        wa = min(LW, 512)
                        K.mm(psA[:, 0:wa], qT[:, h, n * 128:(n + 1) * 128], kT[:, h, kt0 * 128:kt0 * 128 + wa], True, True,
                             [qT, kT], [psA])
                        if LW > 512:
                            K.mm(psB[:, 0:LW - 512], qT[:, h, n * 128:(n + 1) * 128],
                                 kT[:, h, kt0 * 128 + 512:kt0 * 128 + LW], True, True, [qT, kT], [psB])
                        K.mm(psB[:, 128:384], qT[:, h, n * 128:(n + 1) * 128], kcT[:, h, :], True, True, [qT, kcT], [psB])
                        scb = K.rot("nscb", [128, 640 + 256], F32, 2)
                        K.memset("pool", scb[:, 0:LW], NEG, [scb])
                        for hf in range(2):
                            r = 2 * n + hf
                            a = (rs_[hf] - kt0 * 2) * 64
                            b = a + 512
                            tof = (rs_[hf] - r + 7) * 64
                            pp = slice(hf * 64, (hf + 1) * 64)
                            if a < 512:
                                e_ = min(b, 512)
                                K.stt("dve", scb[pp, a:e_], psA[pp, a:e_], 0.125, Toe[pp, h, tof:tof + (e_ - a)],
                                      ALU.mult, ALU.add, [psA, Toe], [scb])
                            if b > 512:
                                s_ = max(a, 512)
                                K.stt("dve", scb[pp, s_:b], psB[pp, s_ - 512:b - 512], 0.125,
                                      Toe[pp, h, tof + (s_ - a):tof + 512], ALU.mult, ALU.add, [psB, Toe], [scb])
                        K.ts("dve", scb[:, LW:LW + 256], psB[:, 128:384], 0.125, None, ALU.mult, None, [psB], [scb])
                        st = K.rot("nst", [128, 2], F32, 3)
                        K.red(st[:, 0:1], scb[:, 0:LW + 256], ALU.max, [scb], [st])
                        K.ts("dve", st[:, 0:1], st[:, 0:1], -1.0, None, ALU.mult, None, [st], [st])
                        K.act(scb[:, 0:LW + 256], scb[:, 0:LW + 256], AF.Exp, [scb, st], [scb, st], bias=st[:, 0:1],
                              accum=st[:, 1:2])
                        nb = ntl + 2
                        pT = K.rot("npT", [128, 7, 128], BF16, 2)
                        for g0 in range(0, nb, 4):
                            g1 = min(nb, g0 + 4)
                            ps2 = K.psum()
                            for c in range(g0, g1):
                                K.tr(ps2[:, (c - g0) * 128:(c - g0 + 1) * 128], scb[:, c * 128:(c + 1) * 128],
                                     self.ident[:], [scb, self.ident], [ps2])
                            K.cp(K.evq(), pT[:, g0:g1, :],
                                 ps2[:, 0:(g1 - g0) * 128].rearrange("p (m t) -> p m t", t=128), [ps2], [pT])
                        acc = K.psum()
                        for c in range(nb):
                            rhs = v[:, kt0 + c, hs] if c < ntl else vc[:, c - ntl, h, :]
                            K.mm(acc[:, 0:64], pT[:, c, :], rhs, c == 0, c == nb - 1, [pT, v, vc], [acc])
                        K.S.op("dve", lambda e, st=st: e.reciprocal(st[:, 1:2], st[:, 1:2]), [st.b], [st.b])
                        K.ts("dve", nout[:, n, hs], acc[:, 0:64], st[:, 1:2], None, ALU.mult, None, [acc, st], [nout])
            self.store_branch(3, tok0, nt, nout)

    def phaseC(self, l):
        for (tok0, T_, latent, pb) in self.seqs():
            if self.debug and "only_s" in self.debug and not latent:
                continue
            if self.debug and "only_p" in self.debug and (latent or pb != 0):
                continue
            for br in ("mlstm", "gla", "rwkv", "na"):
                if self.debug and ("br_" + br) not in self.debug and any(d.startswith("br_") for d in self.debug):
                    continue
                if hasattr(self, br):
                    getattr(self, br)(l, tok0, T_, latent, pb)

    def layer_norm(self, y, which_ln, lnb_t, out):
        K = self.K
        st = K.rot("ln_st", [128, 2, 6], F32, 2)
        for hh in range(2):
            K.S.op("dve", lambda e, hh=hh, st=st: e.bn_stats(st[:, hh, :], y[:, hh * 512:(hh + 1) * 512]), [y.b], [st.b])
        mv = K.rot("ln_mv", [128, 2], F32, 2)
        K.S.op("dve", lambda e, st=st, mv=mv: e.bn_aggr(mv[:], st[:].rearrange("p a b -> p (a b)")), [st.b], [mv.b])
        K.ts("dve", mv[:, 1:2], mv[:, 1:2], LN_EPS, None, ALU.add, None, [mv], [mv])
        K.act(mv[:, 1:2], mv[:, 1:2], AF.Sqrt, [mv], [mv])
        K.S.op("dve", lambda e, mv=mv: e.reciprocal(mv[:, 1:2], mv[:, 1:2]), [mv.b], [mv.b])
        nb = K.rot("ln_nb", [128, 1], F32, 2)
        K.stt("dve", nb[:], mv[:, 0:1], -1.0, mv[:, 1:2], ALU.mult, ALU.mult, [mv], [nb])
        K.act(y[:], y[:], AF.Identity, [y, mv, nb], [y], bias=nb[:, 0:1], scale=mv[:, 1:2])
        K.tt("dve", y[:], y[:], lnb_t[:, 0, :], ALU.mult, [y, lnb_t], [y])
        K.tt("dve", out[:], y[:], lnb_t[:, 1, :], ALU.add, [y, lnb_t], [out])

    def bcast_rows(self, dst_ap, dst_t, src_row_ap, src_t, width):
        self.K.dma("sp", dst_ap, bc(src_row_ap, [128, width]), [src_t], [dst_t])

    def phaseD(self, l, xsrc):
        K = self.K
        I = self.I
        with K.scope():
            wbr = K.sb([128, 4, 2, D], BF16, "wbr")
            for z in range(4):
                K.dma("pool", wbr[:, z], I["w_br"][l, z].rearrange("(k p) d -> p k d", p=128), [I["w_br"]], [wbr])
            wout = K.sb([128, 8, D], BF16, "wout")
            K.dma("pool", wout[:], I["w_out"][l].rearrange("(k p) d -> p k d", p=128), [I["w_out"]], [wout])
            wr = K.sb([128, 8, NEXP], F32, "wr")
            K.dma("sp", wr[:], I["w_router"][l].rearrange("(k p) e -> p k e", p=128), [I["w_router"]], [wr])
            g1b = K.sb([128, 2, D], F32, "g1b")
            sc2b = K.sb([128, 2, D], F32, "sc2b")
            sh2b = K.sb([128, 2, D], F32, "sh2b")
            for w_ in range(2):
                self.bcast_rows(g1b[:, w_, :], g1b, self.modrow[l, w_:w_ + 1, 2 * D:3 * D], self.modrow, D)
                self.bcast_rows(sh2b[:, w_, :], sh2b, self.modrow[l, w_:w_ + 1, 3 * D:4 * D], self.modrow, D)
                self.bcast_rows(sc2b[:, w_, :], sc2b, self.modrow[l, w_:w_ + 1, 4 * D:5 * D], self.modrow, D)
            K.ts("dve", sc2b[:], sc2b[:], 1.0, None, ALU.add, None, [sc2b], [sc2b])
            ln1 = K.sb([128, 2, D], F32, "ln1")
            self.bcast_rows(ln1[:, 0, :], ln1, I["ln_g"][l, 0:1, :], I["ln_g"], D)
            self.bcast_rows(ln1[:, 1, :], ln1, I["ln_b"][l, 0:1, :], I["ln_b"], D)
            W = (wbr, wout, wr, g1b, sc2b, sh2b, ln1)
            for i0_ in range(0, NT, 2):
                K.interleave([self._phaseD_tile(l, i0_, xsrc, W), self._phaseD_tile(l, i0_ + 1, xsrc, W)])

    def _phaseD_tile(self, l, i, xsrc, W):
        K = self.K
        wbr, wout, wr, g1b, sc2b, sh2b, ln1 = W
        MC = COLS['merge']
        px = f"_{i % 2}"
        wch = 0 if i < 8 else 1
        ts_ = slice(i * 128, (i + 1) * 128)
        gates = K.rot("dgates" + px, [128, 4 * D], BF16, 1)
        K.dma("pool", gates[:], self.proj[ts_, MC:MC + 4 * D], [self.proj], [gates])
        K.act(gates[:], gates[:], AF.Sigmoid, [gates], [gates])
        xt = K.rot("dxt" + px, [128, D], F32, 1)
        K.dma(K.dq(), xt[:], xsrc[ts_, :], [xsrc], [xt])
        msum = K.rot("dmsum" + px, [128, D], F32, 1)
        tmp = K.rot("dtmp" + px, [128, D], F32, 1)
        for z in range(4):
            for hh in range(2):
                cs_ = slice(hh * 512, (hh + 1) * 512)
                ps = K.psum()
                for kk in range(2):
                    K.mm(ps[:], self.brT[:, 2 * z + kk, ts_], wbr[:, z, kk, cs_], kk == 0, kk == 1,
                         [self.brT, wbr], [ps])
                gsl = gates[:, z * D + hh * 512:z * D + (hh + 1) * 512]
                if z == 0:
                    K.tt("dve", msum[:, cs_], ps[:], gsl, ALU.mult, [ps, gates], [msum])
                else:
                    K.tt("dve", tmp[:, cs_], ps[:], gsl, ALU.mult, [ps, gates], [tmp])
                    K.tt("dve", msum[:, cs_], msum[:, cs_], tmp[:, cs_], ALU.add, [msum, tmp], [msum])
        yield
        msT = K.rot("dmsT" + px, [128, 8, 128], BF16, 1)
        for kg in range(2):
            ps = K.psum()
            for kk in range(4):
                k = kg * 4 + kk
                K.tr(ps[:, kk * 128:(kk + 1) * 128], msum[:, k * 128:(k + 1) * 128], self.ident[:],
                     [msum, self.ident], [ps])
            K.cp(K.evq(), msT[:, kg * 4:(kg + 1) * 4, :], ps[:].rearrange("p (k t) -> p k t", k=4), [ps], [msT])
        yield
        y = K.rot("dy" + px, [128, D], F32, 1)
        for hh in range(2):
            cs_ = slice(hh * 512, (hh + 1) * 512)
            ps = K.psum()
            for k in range(8):
                K.mm(ps[:], msT[:, k, :], wout[:, k, cs_], k == 0, k == 7, [msT, wout], [ps])
            K.tt("dve", y[:, cs_], ps[:], g1b[:, wch, cs_], ALU.mult, [ps, g1b], [y])
        K.stt("dve", y[:], xt[:], ALPHA, y[:], ALU.mult, ALU.add, [xt, y], [y])
        yield
        x1 = K.rot("dx1" + px, [128, D], F32, 1)
        self.layer_norm(y, 0, ln1, x1)
        K.dma("sp", self.x1res[ts_, :], x1[:], [x1], [self.x1res])
        h2 = K.rot("dh2" + px, [128, D], F32, 1)
        K.tt("dve", h2[:], x1[:], sc2b[:, wch, :], ALU.mult, [x1, sc2b], [h2])
        K.tt("dve", h2[:], h2[:], sh2b[:, wch, :], ALU.add, [h2, sh2b], [h2])
        h2b = K.rot("dh2b" + px, [128, D], BF16, 1)
        K.cp("act", h2b[:], h2[:], [h2], [h2b])
        K.dma("act", self.h2res[ts_, :], h2b[:], [h2b], [self.h2res])
        yield
        h2T = K.rot("dh2T" + px, [128, 8, 128], F32, 1)
        for kg in range(2):
            ps = K.psum()
            for kk in range(4):
                k = kg * 4 + kk
                K.tr(ps[:, kk * 128:(kk + 1) * 128], h2[:, k * 128:(k + 1) * 128], self.ident[:],
                     [h2, self.ident], [ps])
            K.cp(K.evq(), h2T[:, kg * 4:(kg + 1) * 4, :], ps[:].rearrange("p (k t) -> p k t", k=4), [ps], [h2T])
        yield
        ps = K.psum()
        for k in range(8):
            K.mm(ps[:, 0:NEXP], h2T[:, k, :], wr[:, k, :], k == 0, k == 7, [h2T, wr], [ps])
        st = K.rot("dst" + px, [128, 2], F32, 1)
        K.red(st[:, 0:1], ps[:, 0:NEXP], ALU.max, [ps], [st])
        K.ts("dve", st[:, 0:1], st[:, 0:1], -1.0, None, ALU.mult, None, [st], [st])
        ex = K.rot("dex" + px, [128, NEXP], F32, 1)
        K.act(ex[:], ps[:, 0:NEXP], AF.Exp, [ps, st], [ex, st], bias=st[:, 0:1], accum=st[:, 1:2])
        K.S.op("dve", lambda e, st=st: e.reciprocal(st[:, 1:2], st[:, 1:2]), [st.b], [st.b])
        K.ts("dve", self.affall[:, i, :], ex[:], st[:, 1:2], None, ALU.mult, None, [ex, st], [self.affall])

    def phaseE(self, l, xdst):
        K = self.K
        I = self.I
        sets = [(s * 256, 256, 32) for s in range(4)] + [(1024, 1024, 128)]
        with K.scope():
            slot = K.sb([16, NTOK], F32, "eslot")
            gatev = K.sb([16, NTOK], F32, "egate")
            slotT = K.sb([128, NT, NEXP], F32, "eslotT")
            with K.scope():
                affT = K.sb([16, NTOK], F32, "eaffT")
                work = K.sb([16, NTOK], F32, "ework")
                cum = K.sb([16, NTOK], F32, "ecum")
                one16 = K.sb([16, 1024], F32, "eone")
                K.memset("dve", one16[:], 1.0, [one16])
                for ig in range(4):
                    ps = K.psum()
                    for ii in range(4):
                        i = ig * 4 + ii
                        K.tr(ps[0:16, ii * 128:(ii + 1) * 128], self.affall[:, i, :], self.ident[:],
                             [self.affall, self.ident], [ps])
                    K.cp("dve", affT[:, ig * 512:(ig + 1) * 512], ps[0:16, :], [ps], [affT])
                K.cp("act", work[:], affT[:], [affT], [work])
                for (c0, T_, cap) in sets:
                    for it in range(cap // 8):
                        m8 = K.rot("em8", [16, 8], F32, 2)
                        K.S.op("dve", lambda e, m8=m8, c0=c0, T_=T_: e.max(m8[:], work[:, c0:c0 + T_]), [work.b], [m8.b])
                        K.S.op("dve", lambda e, m8=m8, c0=c0, T_=T_: e.match_replace(
                            work[:, c0:c0 + T_], m8[:], work[:, c0:c0 + T_], -1.0), [work.b, m8.b], [work.b])
                K.ts("dve", work[:], work[:], 0.0, None, ALU.is_lt, None, [work], [work])
                K.tt("dve", gatev[:], affT[:], work[:], ALU.mult, [affT, work], [gatev])
                for (c0, T_, cap) in sets:
                    K.S.op("dve", lambda e, c0=c0, T_=T_: e.tensor_tensor_scan(
                        cum[:, c0:c0 + T_], one16[:, 0:T_], work[:, c0:c0 + T_], 0.0, ALU.mult, ALU.add),
                        [one16.b, work.b], [cum.b])
                K.ts("dve", cum[:], cum[:], 999.0, None, ALU.add, None, [cum], [cum])
                K.tt("dve", cum[:], cum[:], work[:], ALU.mult, [cum, work], [cum])
                K.ts("dve", slot[:], cum[:], -1000.0, None, ALU.add, None, [cum], [slot])
                ps = K.psum()
                for i in range(NT):
                    K.tr(ps[:, i * 16:(i + 1) * 16], slot[:, i * 128:(i + 1) * 128], self.ident[0:16, 0:16],
                         [slot, self.ident], [ps])
                K.cp("dve", slotT[:], ps[:, 0:256].rearrange("p (i e) -> p i e", e=16), [ps], [slotT])
            with K.scope():
                h2tok = K.sb([128, NT, D], BF16, "eh2")
                for ig in range(4):
                    K.dma(K.dq(), h2tok[:, ig * 4:(ig + 1) * 4, :],
                          self.h2res[ig * 512:(ig + 1) * 512, :].rearrange("(i p) d -> p i d", p=128), [self.h2res], [h2tok])
                for e_ in range(NEXP):
                    PTs = K.rot("ePTs", [128, 8, 128], BF16, 2)
                    PTp = K.rot("ePTp", [128, 8, 32], BF16, 2)
                    K.tt("dve", PTs[:], bc(self.fidx[:].unsqueeze(1), [128, 8, 128]),
                         bc(slotT[:, 8:16, e_].unsqueeze(2), [128, 8, 128]), ALU.is_equal, [self.fidx, slotT], [PTs])
                    K.tt("dve", PTp[:], bc(self.fidx[:, 0:32].unsqueeze(1), [128, 8, 32]),
                         bc(slotT[:, 0:8, e_].unsqueeze(2), [128, 8, 32]), ALU.is_equal, [self.fidx, slotT], [PTp])
                    xeT = K.rot("exeT", [128, 8, 256], BF16, 2)
                    for kg in range(2):
                        ps = K.psum()
                        ps2 = K.psum()
                        for kk in range(4):
                            k = kg * 4 + kk
                            for i in range(8, 16):
                                K.mm(ps[:, kk * 128:(kk + 1) * 128], h2tok[:, i, k * 128:(k + 1) * 128], PTs[:, i - 8, :],
                                     i == 8, i == 15, [h2tok, PTs], [ps])
                            for s in range(4):
                                for ii in range(2):
                                    i = 2 * s + ii
                                    K.mm(ps2[:, kk * 128 + s * 32:kk * 128 + (s + 1) * 32],
                                         h2tok[:, i, k * 128:(k + 1) * 128], PTp[:, i, :], ii == 0, ii == 1,
                                         [h2tok, PTp], [ps2])
                        K.cp("act", xeT[:, kg * 4:(kg + 1) * 4, 0:128], ps[:].rearrange("p (k c) -> p k c", k=4),
                             [ps], [xeT])
                        K.cp("dve", xeT[:, kg * 4:(kg + 1) * 4, 128:256], ps2[:].rearrange("p (k c) -> p k c", k=4),
                             [ps2], [xeT])
                    hmT = K.rot("ehmT", [128, 16, 256], BF16, 2)
                    for fg in range(4):
                        wu = K.rot("ewu", [128, 8, 2, 512], BF16, 2)
                        for ab in range(2):
                            c0 = ab * FF + fg * 512
                            K.dma("pool", wu[:, :, ab, :],
                                  I["w_up"][l, e_, :, c0:c0 + 512].rearrange("(k p) f -> p k f", p=128), [I["w_up"]], [wu])
                        for f4 in range(4):
                            fc = fg * 4 + f4
                            psa = K.psum()
                            psb = K.psum()
                            for k in range(8):
                                K.mm(psa[:, 0:256], wu[:, k, 0, f4 * 128:(f4 + 1) * 128], xeT[:, k, :], k == 0, k == 7,
                                     [wu, xeT], [psa])
                            for k in range(8):
                                K.mm(psb[:, 0:256], wu[:, k, 1, f4 * 128:(f4 + 1) * 128], xeT[:, k, :], k == 0, k == 7,
                                     [wu, xeT], [psb])
                            sa = K.rot("esa", [128, 256], F32, 2)
                            K.act(sa[:], psa[:, 0:256], AF.Silu, [psa], [sa])
                            K.tt("dve", hmT[:, fc, :], sa[:], psb[:, 0:256], ALU.mult, [sa, psb], [hmT])
                    yeb = K.rot("eyeb", [128, 2, D], BF16, 2)
                    for hh in range(2):
                        wd = K.rot("ewd", [128, 16, 512], BF16, 2)
                        K.dma("pool", wd[:], I["w_down"][l, e_, :, hh * 512:(hh + 1) * 512].rearrange("(k p) d -> p k d", p=128),
                              [I["w_down"]], [wd])
                        for grp in range(2):
                            ps = K.psum()
                            for fc in range(16):
                                K.mm(ps[:], hmT[:, fc, grp * 128:(grp + 1) * 128], wd[:, fc, :], fc == 0, fc == 15,
                                     [hmT, wd], [ps])
                            K.cp(K.evq(), yeb[:, grp, hh * 512:(hh + 1) * 512], ps[:], [ps], [yeb])
                    for grp in range(2):
                        K.dma(K.dq(), self.ye[grp, e_], yeb[:, grp, :], [yeb], [self.ye])
            with K.scope():
                g2b = K.sb([128, 2, D], F32, "g2b")
                for w_ in range(2):
                    self.bcast_rows(g2b[:, w_, :], g2b, self.modrow[l, w_:w_ + 1, 5 * D:6 * D], self.modrow, D)
                ln2 = K.sb([128, 2, D], F32, "ln2")
                self.bcast_rows(ln2[:, 0, :], ln2, I["ln_g"][l, 1:2, :], I["ln_g"], D)
                self.bcast_rows(ln2[:, 1, :], ln2, I["ln_b"][l, 1:2, :], I["ln_b"], D)
                cidx = K.sb([128, 5], F32, "ecidx")
                for s in range(4):
                    K.ts("dve", cidx[:, s:s + 1], self.pidx[:], -32.0 * s, None, ALU.add, None, [self.pidx], [cidx])
                K.cp("dve", cidx[:, 4:5], self.pidx[:], [self.pidx], [cidx])
                id16 = self.ident[0:16, 0:16]
                for grp in (1, 0):
                    yeg = K.rot("eyeg", [128, NEXP, D], BF16, 1)
                    for eg in range(4):
                        K.dma(K.dq(), yeg[:, eg * 4:(eg + 1) * 4, :],
                              self.ye[grp, eg * 4:(eg + 1) * 4].rearrange("e c d -> c e d"), [self.ye], [yeg])
                    tiles = range(8, 16) if grp == 0 else range(0, 8)
                    for i in tiles:
                        wch = 0 if i < 8 else 1
                        ts_ = slice(i * 128, (i + 1) * 128)
                        ccol = 4 if i >= 8 else i // 2
                        Rs = K.rot("eRs", [16, NEXP, 128], F32, 2)
                        Rg = K.rot("eRg", [16, NEXP, 128], F32, 2)
                        K.tt("dve", Rs[:], bc(slot[:, ts_].unsqueeze(1), [16, NEXP, 128]),
                             bc(id16.unsqueeze(2), [16, NEXP, 128]), ALU.mult, [slot, self.ident], [Rs])
                        K.tt("dve", Rg[:], bc(gatev[:, ts_].unsqueeze(1), [16, NEXP, 128]),
                             bc(id16.unsqueeze(2), [16, NEXP, 128]), ALU.mult, [gatev, self.ident], [Rg])
                        PTg = K.rot("ePTg", [128, NEXP, 128], BF16, 2)
                        for eg in range(4):
                            pss = K.psum()
                            psg = K.psum()
                            K.mm(pss[:], self.ones[0:16, :], Rs[:, eg * 4:(eg + 1) * 4, :].rearrange("p e t -> p (e t)"),
                                 True, True, [self.ones, Rs], [pss])
                            K.mm(psg[:], self.ones[0:16, :], Rg[:, eg * 4:(eg + 1) * 4, :].rearrange("p e t -> p (e t)"),
                                 True, True, [self.ones, Rg], [psg])
                            gsb = K.rot("egsb", [128, 512], F32, 2)
                            K.cp("act", gsb[:], psg[:], [psg], [gsb])
                            K.stt("dve", PTg[:, eg * 4:(eg + 1) * 4, :].rearrange("p e t -> p (e t)"), pss[:],
                                  cidx[:, ccol:ccol + 1], gsb[:], ALU.is_equal, ALU.mult, [pss, cidx, gsb], [PTg])
                        x1 = K.rot("ex1", [128, D], F32, 2)
                        K.dma(K.dq(), x1[:], self.x1res[ts_, :], [self.x1res], [x1])
                        y = K.rot("ey", [128, D], F32, 2)
                        for hh in range(2):
                            cs_ = slice(hh * 512, (hh + 1) * 512)
                            ps = K.psum()
                            for e_ in range(NEXP):
                                K.mm(ps[:], PTg[:, e_, :], yeg[:, e_, cs_], e_ == 0, e_ == NEXP - 1, [PTg, yeg], [ps])
                            K.tt("dve", y[:, cs_], ps[:], g2b[:, wch, cs_], ALU.mult, [ps, g2b], [y])
                        K.stt("dve", y[:], x1[:], ALPHA, y[:], ALU.mult, ALU.add, [x1, y], [y])
                        xo = K.rot("exo", [128, D], F32, 2)
                        self.layer_norm(y, 1, ln2, xo)
                        K.dma(K.dq(), xdst[ts_, :], xo[:], [xo], [xdst])

    def build(self):
        self.phase0_mods()
        if self.debug and "stop0" in self.debug:
            return self.finish()
        for l in range(DEPTH):
            xsrc = self.I["xin"] if l == 0 else self.xres[(l - 1) % 2]
            if not (self.debug and "projin" in self.debug):
                with self.K.scope():
                    self.hT = self.K.sb([128, 8, NTOK], BF16, "hT")
                    self.phaseA(l, xsrc)
                    self.phaseB(l)
            if self.debug and "stopB" in self.debug:
                return self.finish()
            if l == 0:
                self.affall = self.K.sb([128, NT, NEXP], F32, "affall")
            with self.K.scope():
                self.brT = self.K.sb([128, 8, NTOK], BF16, "brT")
                self.phaseC(l)
                if self.debug and "stopC" in self.debug:
                    return self.finish()
                self.phaseD(l, xsrc)
                if self.debug and "stopD" in self.debug:
                    return self.finish()
            xdst = self.O["y"] if l == DEPTH - 1 else self.xres[l % 2]
            self.phaseE(l, xdst)
            if self.debug and "stopE" in self.debug:
                return self.finish()
        return self.finish()

    def finish(self):
        self.K.S.emit()
        return self.nc


IN_SHAPES = {
    "xin": [NTOK, D], "cvec": [2, D],
    "sC": [DEPTH, 2, H, HD, HD], "sn": [DEPTH, 2, H, HD], "sm": [DEPTH, 8],
    "sg": [DEPTH, 2, H, HD, HD], "sr": [DEPTH, 2, H, HD, HD],
    "cak": [DEPTH, H, 256, HD], "cav": [DEPTH, H, 256, HD],
    "w_ada": [DEPTH, D, 6 * D], "b_ada": [DEPTH, 6 * D], "w_in": [DEPTH, D, NIN],
    "b_ig": [DEPTH, 8], "b_fg": [DEPTH, 8], "w_gla_a2": [DEPTH, 2, 16, MIXW], "b_gla_a": [DEPTH, 2, MIXW],
    "shift_rwkv": [DEPTH, 3, 768], "w0_rwkv": [DEPTH, 2, MIXW], "w_w2": [DEPTH, 2, 32, MIXW],
    "a0_rwkv": [DEPTH, 2, MIXW], "w_a2": [DEPTH, 2, 32, MIXW], "w_g2": [DEPTH, 64, MIXW],
    "k_k": [DEPTH, MIXW], "k_a": [DEPTH, MIXW], "r_k": [DEPTH, MIXW], "rpb": [DEPTH, 60, 31],
    "w_br": [DEPTH, 4, MIXW, D], "w_out": [DEPTH, D, D], "ln_g": [DEPTH, 2, D], "ln_b": [DEPTH, 2, D],
    "w_router": [DEPTH, D, NEXP], "w_up": [DEPTH, NEXP, D, 2 * FF], "w_down": [DEPTH, NEXP, FF, D],
}
OUT_SHAPES = {
    "y": [NTOK, D], "nC": [4, DEPTH, 8, HD, HD], "nn": [4, DEPTH, 8, HD], "nm": [4, DEPTH, 8],
    "ng": [4, DEPTH, 8, HD, HD], "nr": [4, DEPTH, 8, HD, HD],
    "nak": [4, DEPTH, H, 256, HD], "nav": [4, DEPTH, H, 256, HD],
}


def make_in_maps(inputs):
    f = lambda a: np.ascontiguousarray(np.asarray(a, dtype=np.float32))
    shared = {}
    for name in ("w_ada", "b_ada", "w_in", "w_gla_a2", "b_gla_a", "shift_rwkv", "w0_rwkv", "w_w2", "a0_rwkv",
                 "w_a2", "w_g2", "k_k", "k_a", "r_k", "w_br", "w_out", "ln_g", "ln_b", "w_router", "w_up", "w_down"):
        shared[name] = f(inputs[name])
    shared["b_ig"] = f(inputs["b_ig"]).reshape(DEPTH, 8)
    shared["b_fg"] = f(inputs["b_fg"]).reshape(DEPTH, 8)
    shared["rpb"] = f(inputs["rpb"]).reshape(DEPTH, 60, 31)
    xp = f(inputs["x_prompt"])
    xs = f(inputs["x_sample"])
    maps = []
    for i in range(NCORES):
        m = dict(shared)
        m["xin"] = np.ascontiguousarray(np.concatenate([xp[4 * i:4 * i + 4].reshape(1024, D), xs[i]], axis=0))
        m["cvec"] = np.ascontiguousarray(np.stack([f(inputs["c_ctx"]), f(inputs["c"])[i]], axis=0))
        m["sC"] = f(inputs["state_mlstm_C"][i])
        m["sn"] = f(inputs["state_mlstm_n"][i])
        m["sm"] = f(inputs["state_mlstm_m"][i]).reshape(DEPTH, 8)
        m["sg"] = f(inputs["state_gla"][i])
        m["sr"] = f(inputs["state_rwkv"][i])
        m["cak"] = f(inputs["cache_na_k"][i])
        m["cav"] = f(inputs["cache_na_v"][i])
        maps.append(m)
    return maps


def kernel(**inputs):
    prog = Prog()
    nc = prog.build()
    maps = make_in_maps(inputs)
    res = run_bass_kernel_spmd(nc, maps, core_ids=list(range(NCORES)))
    R = res.results
    y = np.stack([r["y"] for r in R], 0)
    y_prompt = y[:, :1024].reshape(32, 256, D)
    y_sample = y[:, 1024:].reshape(8, 1024, D)
    cat = lambda k: np.concatenate([r[k] for r in R], 0)
    nC = cat("nC").reshape(32, DEPTH, 2, H, HD, HD)
    nn = cat("nn").reshape(32, DEPTH, 2, H, HD)
    nm = cat("nm").reshape(32, DEPTH, 2, H)
    ng = cat("ng").reshape(32, DEPTH, 2, H, HD, HD)
    nr = cat("nr").reshape(32, DEPTH, 2, H, HD, HD)
    nak = cat("nak")
    nav = cat("nav")
    return tuple(np.ascontiguousarray(a, dtype=np.float32) for a in (y_prompt, y_sample, nC, nn, nm, ng, nr, nak, nav))
```
